# Optimizing a Trainium2 kernel written in Bass

```python
import jax, jax.numpy as jnp
from jax import lax
import numpy as np

D_MODEL = 1024
BATCH = 4
SEQ = 4096
DEPTH = 1
DEC_BATCH = 2
DEC_SEQ = 8192
PAST_LEN = 128

GRID_W = 64
HGRN_HEADS = 4
HGRN_DK = 128
HGRN_DV = 128
HGRN_WIDTH = HGRN_HEADS * HGRN_DV
CHUNK = 64
NA_HEADS = 8
NA_DH = 64
NA_WIDTH = NA_HEADS * NA_DH
NA_KH_MAX = 8
NA_KW = 16
NA_QBLK = 16
NA_KBAND = 32
D_FF = 2816
EPS = 1e-6
IN_COLS = 5 * HGRN_WIDTH + 3 * NA_WIDTH
SPLITS = [HGRN_WIDTH, 2 * HGRN_WIDTH, 3 * HGRN_WIDTH, 4 * HGRN_WIDTH, 5 * HGRN_WIDTH,
          5 * HGRN_WIDTH + NA_WIDTH, 5 * HGRN_WIDTH + 2 * NA_WIDTH]

kernel_name = "hymba_hgrn2_natten_encoder"


def rmsnorm(x, w):
    xf = x.astype(jnp.float32)
    y = xf * lax.rsqrt(jnp.mean(xf * xf, axis=-1, keepdims=True) + EPS) * w.astype(jnp.float32)
    return y.astype(x.dtype)


def hgrn2_direction(q, k, log_f, v):
    B, T, H, dk = q.shape
    dv = v.shape[-1]
    n = T // CHUNK

    def chunks(a):
        return a.reshape(B, n, CHUNK, H, a.shape[-1]).transpose(1, 0, 3, 2, 4)

    tril = jnp.tril(jnp.ones((CHUNK, CHUNK), dtype=bool))[..., None]

    def step(S, inp):
        qc, kc, gc, vc = inp
        b = jnp.cumsum(gc, axis=2)
        diff = b[:, :, :, None, :] - b[:, :, None, :, :]
        decay = jnp.exp(jnp.where(tril, diff, -jnp.inf))
        A = jnp.einsum('bhtd,bhsd,bhtsd->bhts', qc, kc, decay)
        b_last = b[:, :, -1:, :]
        o = jnp.einsum('bhts,bhse->bhte', A, vc) + jnp.einsum('bhtd,bhde->bhte', qc * jnp.exp(b), S)
        S = jnp.exp(b_last[:, :, 0, :])[..., None] * S + jnp.einsum(
            'bhsd,bhse->bhde', kc * jnp.exp(b_last - b), vc)
        return S, o

    S0 = jnp.zeros((B, H, dk, dv), jnp.float32)
    _, o = lax.scan(step, S0, (chunks(q), chunks(k), chunks(log_f), chunks(v)))
    return o.transpose(1, 0, 3, 2, 4).reshape(B, T, H, dv)


def neighbourhood_attention(q, k, v, rpb):
    B, T = q.shape[0], q.shape[1]
    rows = T // GRID_W
    kh = min(NA_KH_MAX, rows)
    ncb = GRID_W // NA_QBLK
    r = np.arange(rows)
    row_start = np.clip(r - kh // 2, 0, rows - kh)
    row_idx = row_start[:, None] + np.arange(kh)[None, :]
    j = np.arange(ncb)
    band_start = np.clip(j * NA_QBLK - NA_KW // 2, 0, GRID_W - NA_KBAND)
    col_idx = band_start[:, None] + np.arange(NA_KBAND)[None, :]
    qcol = j[:, None] * NA_QBLK + np.arange(NA_QBLK)[None, :]
    win_start = np.clip(qcol - NA_KW // 2, 0, GRID_W - NA_KW)
    valid = ((col_idx[:, None, :] >= win_start[:, :, None]) &
             (col_idx[:, None, :] < win_start[:, :, None] + NA_KW))
    dr = row_idx - r[:, None]
    dc = np.clip(col_idx[:, None, :] - qcol[:, :, None], -(NA_KW - 1), NA_KW - 1)

    def grid(a):
        return a.reshape(B, rows, GRID_W, NA_HEADS, NA_DH).transpose(0, 3, 1, 2, 4)

    qg, kg, vg = grid(q), grid(k), grid(v)
    qb = qg.reshape(B, NA_HEADS, rows, ncb, NA_QBLK, NA_DH)
    ri = row_idx[:, None, :, None]
    ci = col_idx[None, :, None, :]
    kb = kg[:, :, ri, ci]
    vb = vg[:, :, ri, ci]
    scores = jnp.einsum('bhrjqd,bhrjkwd->bhrjqkw', qb, kb).astype(jnp.float32) * (NA_DH ** -0.5)
    bias = rpb[:, (dr + NA_KH_MAX - 1)[:, None, None, :, None],
               (dc + NA_KW - 1)[None, :, :, None, :]]
    scores = scores + bias[None].astype(jnp.float32)
    scores = jnp.where(valid[None, None, None, :, :, None, :], scores, -1e30)
    shp = scores.shape
    p = jax.nn.softmax(scores.reshape(shp[:-2] + (kh * NA_KBAND,)), axis=-1).reshape(shp)
    out = jnp.einsum('bhrjqkw,bhrjkwd->bhrjqd', p.astype(vb.dtype), vb)
    out = out.reshape(B, NA_HEADS, rows, GRID_W, NA_DH).transpose(0, 2, 3, 1, 4)
    return out.reshape(B, T, NA_WIDTH)


def encoder_layer(x, norm_mix_w, w_in, lb, hgrn_gnorm_w, na_q_norm_w, na_k_norm_w, na_rpb,
                  w_out, norm_ffn_w, w_gate, w_up, w_down):
    B, T, _ = x.shape
    f32 = jnp.float32
    h = rmsnorm(x, norm_mix_w)
    proj = h @ w_in
    hq, hf_fwd, hf_bwd, hi, hg, nq, nk, nv = jnp.split(proj, SPLITS, axis=-1)

    def heads(a, d):
        return a.reshape(B, T, -1, d)

    qh = heads(jax.nn.silu(hq.astype(f32)), HGRN_DK)
    vh = heads(hi.astype(f32), HGRN_DV)

    def gate(logit, lb_d):
        f = lb_d + (1.0 - lb_d) * jax.nn.sigmoid(logit.astype(f32))
        return heads(1.0 - f, HGRN_DK), heads(jnp.log(f), HGRN_DK)

    kf, gf = gate(hf_fwd, lb[0])
    kbk, gbk = gate(hf_bwd, lb[1])
    o_f = hgrn2_direction(qh, kf, gf, vh)
    o_b = jnp.flip(hgrn2_direction(jnp.flip(qh, 1), jnp.flip(kbk, 1), jnp.flip(gbk, 1),
                                   jnp.flip(vh, 1)), axis=1)
    o = o_f + o_b
    o = (o * lax.rsqrt(jnp.mean(o * o, axis=-1, keepdims=True) + EPS) * hgrn_gnorm_w.astype(f32)
         * jax.nn.silu(heads(hg.astype(f32), HGRN_DV)))
    o_hgrn = o.reshape(B, T, HGRN_WIDTH).astype(x.dtype)

    qa = rmsnorm(heads(nq, NA_DH), na_q_norm_w)
    ka = rmsnorm(heads(nk, NA_DH), na_k_norm_w)
    va = heads(nv, NA_DH)
    o_na = neighbourhood_attention(qa, ka, va, na_rpb).astype(x.dtype)

    x = x + jnp.concatenate([o_hgrn, o_na], axis=-1) @ w_out

    h2 = rmsnorm(x, norm_ffn_w)
    x = x + (jax.nn.silu(h2 @ w_gate) * (h2 @ w_up)) @ w_down
    return x


def setup_inputs(seed: int = 0) -> dict:
    key = jax.random.key(seed)
    ks = jax.random.split(key, 14)
    n = jax.random.normal
    return {
        "x_prompt": n(ks[0], (BATCH, SEQ, D_MODEL), jnp.float32),
        "x_sample": n(ks[1], (DEC_BATCH, DEC_SEQ, D_MODEL), jnp.float32),
        "norm_mix_w": 1.0 + 0.1 * n(ks[2], (DEPTH, D_MODEL), jnp.float32),
        "w_in": n(ks[3], (DEPTH, D_MODEL, IN_COLS), jnp.float32) * D_MODEL ** -0.5,
        "hgrn_lb": 0.5 * n(ks[4], (DEPTH + 1, 2, HGRN_WIDTH), jnp.float32),
        "hgrn_gnorm_w": 1.0 + 0.1 * n(ks[5], (DEPTH, HGRN_DV), jnp.float32),
        "na_q_norm_w": 1.0 + 0.1 * n(ks[6], (DEPTH, NA_DH), jnp.float32),
        "na_k_norm_w": 1.0 + 0.1 * n(ks[7], (DEPTH, NA_DH), jnp.float32),
        "na_rpb": 0.1 * n(ks[8], (DEPTH, NA_HEADS, 2 * NA_KH_MAX - 1, 2 * NA_KW - 1), jnp.float32),
        "w_out": n(ks[9], (DEPTH, D_MODEL, D_MODEL), jnp.float32) * D_MODEL ** -0.5,
        "norm_ffn_w": 1.0 + 0.1 * n(ks[10], (DEPTH, D_MODEL), jnp.float32),
        "w_gate": n(ks[11], (DEPTH, D_MODEL, D_FF), jnp.float32) * D_MODEL ** -0.5,
        "w_up": n(ks[12], (DEPTH, D_MODEL, D_FF), jnp.float32) * D_MODEL ** -0.5,
        "w_down": n(ks[13], (DEPTH, D_FF, D_MODEL), jnp.float32) * D_FF ** -0.5,
    }


def reference(x_prompt, x_sample, norm_mix_w, w_in, hgrn_lb, hgrn_gnorm_w, na_q_norm_w,
              na_k_norm_w, na_rpb, w_out, norm_ffn_w, w_gate, w_up, w_down):
    lb_all = jnp.cumsum(jax.nn.softmax(hgrn_lb.astype(jnp.float32), axis=0), axis=0)

    def trunk(x):
        for l in range(DEPTH):
            x = encoder_layer(x, norm_mix_w[l], w_in[l], lb_all[l], hgrn_gnorm_w[l], na_q_norm_w[l],
                              na_k_norm_w[l], na_rpb[l], w_out[l], norm_ffn_w[l], w_gate[l],
                              w_up[l], w_down[l])
        return x

    y_prompt = trunk(x_prompt)
    y_sample = trunk(x_sample)
    return (y_prompt, y_sample)
```

```python
import numpy as np
from contextlib import ExitStack
import concourse.bass as bass
import concourse.mybir as mybir
from concourse.bass_utils import run_bass_kernel_spmd

F32 = mybir.dt.float32
BF16 = mybir.dt.bfloat16
AF = mybir.ActivationFunctionType
ALU = mybir.AluOpType

ENGS = ("pe", "act", "dve", "pool", "sp")
EPS = 1e-6
NT = 32
NB = 8
D = 1024
DFF = 2816
NU = 22


class _Op:
    __slots__ = ("eng", "emit", "deps", "odeps", "signal", "count", "is_dma", "sem", "semval", "name",
                 "dur", "idx", "fin", "sched")


class _Rec:
    def __init__(self):
        self.call = None

    def __getattr__(self, name):
        def f(*a, **kw):
            self.call = (name, a, kw)
            return None
        return f


def _fsize(ap):
    try:
        return int(ap.free_size())
    except Exception:
        return 512


def _est_ns(eng, call, is_dma):
    name, a, kw = call
    if is_dma:
        try:
            nb = int(kw["out"].nbytes())
        except Exception:
            nb = 65536
        return 2000.0 + nb / 150.0
    if eng == "pe":
        if name == "transpose":
            return 64.0
        rhs = kw.get("rhs")
        n = _fsize(rhs) if rhs is not None else 512
        f = 4.0 if (rhs is not None and rhs.dtype == F32) else 1.0
        return f * max(n, 64) * 0.42 + 12.0
    out = kw.get("out")
    n = _fsize(out) if out is not None else (_fsize(a[0]) if a else 512)
    if eng == "act":
        return 220.0 + n * 0.72
    if eng == "dve":
        if name in ("tensor_tensor_scan",):
            return 120.0 + 4.9 * n
        if name in ("tensor_tensor", "scalar_tensor_tensor"):
            return 120.0 + 2.5 * n
        return 120.0 + 1.1 * n
    if eng == "pool":
        return 250.0 + 3.3 * n
    return 100.0


class Sched:
    def __init__(self, nc, reorder=True, window=400):
        self.nc = nc
        self.ops = []
        self.last_w = {}
        self.readers = {}
        self.dma_cnt = {}
        self.fence = {}
        self.reorder = reorder
        self.window = window

    def _mk(self, eng, emit, name=None):
        op = _Op()
        op.eng = eng
        op.emit = emit
        op.signal = False
        op.count = 0
        op.is_dma = False
        op.sem = None
        op.semval = 0
        op.name = name
        op.deps = []
        op.odeps = []
        op.dur = 0.0
        op.idx = len(self.ops)
        op.fin = None
        op.sched = False
        return op

    def add(self, eng, emit, reads=(), writes=(), dma_key=None, name=None):
        rec = _Rec()
        emit(rec)
        call = rec.call
        assert call is not None
        op = self._mk(eng, (lambda h, call=call: getattr(h, call[0])(*call[1], **call[2])), name)
        deps = []
        for r in reads:
            w = self.last_w.get(r)
            if w is not None:
                deps.append(w)
        for w_ in writes:
            w = self.last_w.get(w_)
            if w is not None:
                deps.append(w)
            deps.extend(self.readers.get(w_, ()))
        if dma_key is not None:
            op.is_dma = True
            k = self.dma_cnt.get(dma_key, 0) + 1
            self.dma_cnt[dma_key] = k
            op.sem = dma_key
            op.semval = 16 * k
        op.dur = _est_ns(eng, call, op.is_dma)
        seen = set()
        for d in deps:
            if id(d) in seen or d is op:
                continue
            seen.add(id(d))
            if (not d.is_dma) and (not op.is_dma) and d.eng == eng and eng == "pe":
                op.odeps.append(d)
                continue
            op.deps.append(d)
        f = self.fence.get(eng)
        if f is not None:
            op.odeps.append(f)
        for d in op.deps:
            if not d.is_dma:
                d.signal = True
        for r in reads:
            self.readers.setdefault(r, []).append(op)
        for w_ in writes:
            self.last_w[w_] = op
            self.readers[w_] = []
        self.ops.append(op)
        return op

    def barrier(self):
        last = {}
        lastd = {}
        for op in self.ops:
            if op.is_dma:
                lastd[op.sem] = op
            elif op.emit is not None:
                last[op.eng] = op
        deps = list(last.values()) + list(lastd.values())
        allprev = list(self.ops)
        for e in ENGS:
            op = self._mk(e, None, "barrier")
            op.deps = list(deps)
            op.odeps = [o for o in allprev if o.eng == e and o.emit is not None][-1:]
            op.name = "barrier"
            for d in op.deps:
                if not d.is_dma:
                    d.signal = True
            self.ops.append(op)
            self.fence[e] = op
        self.last_w = {}
        self.readers = {}

    def _schedule(self):
        per = {e: [o for o in self.ops if o.eng == e] for e in ENGS}
        if not self.reorder:
            return per
        tail = {}
        for op in self.ops:
            tail[id(op)] = op.dur
        for op in reversed(self.ops):
            tl = tail[id(op)]
            for d in op.deps:
                v = d.dur + tl
                if v > tail[id(d)]:
                    tail[id(d)] = v
            for d in op.odeps:
                v = d.dur + tl
                if v > tail[id(d)]:
                    tail[id(d)] = v
        pending = {e: list(per[e]) for e in ENGS}
        out = {e: [] for e in ENGS}
        t = {e: 0.0 for e in ENGS}
        dma_free = [0.0]
        remaining = sum(len(v) for v in pending.values())
        W = self.window
        while remaining:
            best = None
            for e in ENGS:
                lst = pending[e]
                if not lst:
                    continue
                cand = None
                nready = 0
                lim = 0
                for op in lst:
                    lim += 1
                    if lim > W:
                        break
                    if op.name == "barrier" and op is not lst[0]:
                        break
                    ok = True
                    rdy = t[e]
                    for d in op.deps:
                        if not d.sched:
                            ok = False
                            break
                        if d.fin > rdy:
                            rdy = d.fin
                    if ok:
                        for d in op.odeps:
                            if not d.sched:
                                ok = False
                                break
                    if not ok:
                        if op.name == "barrier":
                            break
                        continue
                    if rdy <= t[e]:
                        tl = tail[id(op)]
                        if nready == 0 or tl > cand[2]:
                            cand = (rdy, op, tl)
                        nready += 1
                        if nready >= 96:
                            break
                    elif nready == 0 and (cand is None or rdy < cand[0]):
                        cand = (rdy, op, 0.0)
                if cand is not None and (best is None or cand[0] < best[0]):
                    best = (cand[0], e, cand[1])
            assert best is not None, "scheduler stuck"
            start, e, op = best
            pending[e].remove(op)
            op.sched = True
            if op.is_dma:
                issue = start + 60.0
                xfer = op.dur - 2000.0
                st_ = max(issue, dma_free[0])
                dma_free[0] = st_ + xfer
                op.fin = st_ + xfer + 2000.0
                t[e] = issue
            else:
                op.fin = start + op.dur
                t[e] = op.fin
            out[e].append(op)
            remaining -= 1
        self.sim_ns = max(t.values())
        return out

    def emit_all(self, stack):
        nc = self.nc
        lastd = {}
        for op in self.ops:
            if op.is_dma:
                lastd[op.sem] = op
        fin = self._mk("sp", None, "final")
        fin.name = "barrier"
        fin.deps = list(lastd.values())
        fin.odeps = [o for o in self.ops if o.eng == "sp"][-1:]
        self.ops.append(fin)

        per = self._schedule()
        eng_sem = {e: stack.enter_context(nc.semaphore("sem_" + e)) for e in ENGS}
        dma_sem = {}
        for i, k in enumerate(self.dma_cnt):
            dma_sem[k] = stack.enter_context(nc.semaphore("dsem%d" % i))
        for e in ENGS:
            c = 0
            for op in per[e]:
                if op.signal:
                    c += 1
                    op.count = c
        self.n_per = {e: len(per[e]) for e in ENGS}

        def run(e, handle):
            known = {}
            for op in per[e]:
                need = {}
                for d in op.deps:
                    if d.is_dma:
                        s, v = dma_sem[d.sem], d.semval
                    else:
                        s, v = eng_sem[d.eng], d.count
                    key = id(s)
                    if key not in need or need[key][1] < v:
                        need[key] = (s, v)
                for key, (s, v) in need.items():
                    if known.get(key, 0) >= v:
                        continue
                    handle.wait_ge(s, v)
                    known[key] = v
                if op.emit is None:
                    continue
                ins = op.emit(handle)
                if op.is_dma:
                    ins.then_inc(dma_sem[op.sem], 16)
                elif op.signal:
                    ins.then_inc(eng_sem[e], 1)

        block = stack.enter_context(nc.Block())

        @block.sync
        def _(h):
            run("sp", h)

        @block.scalar
        def _(h):
            run("act", h)

        @block.vector
        def _(h):
            run("dve", h)

        @block.gpsimd
        def _(h):
            run("pool", h)

        @block.tensor
        def _(h):
            run("pe", h)


V_NMW = 0
V_NFW = 8
V_LB = 16
V_LBO = 32
V_GW = 40
V_WQ = 41
V_WK = 42
V_AL = 43
V_BE = 44
NV = 48

NSLOT = 27
SLOT0 = {"gen": 0, 0: 5, 1: 11, 30: 16, 31: 21}


def na_chunks(p):
    keys = [p + c for c in range(5)]
    if p == 0:
        return SLOT0[0], keys + [3 + 2]
    if p == 31:
        return SLOT0[31], keys + [28 + 2]
    if p in (1, 30):
        return SLOT0[p], keys
    return SLOT0["gen"], keys


def build(do_hgrn=True, do_na=True, do_pre=True, dbg=()):
    nc = bass.Bass("TRN2", target_bir_lowering=False)
    dt_in = lambda n, s: nc.dram_tensor(n, s, F32, kind="ExternalInput").ap()
    xloc = dt_in("xloc", [4096, D])
    xhalo = dt_in("xhalo", [512, D])
    xoth = dt_in("xoth", [4096, D])
    w_in = dt_in("w_in", [D, 4096])
    w_oth = dt_in("w_oth", [D, 1024])
    w_out = dt_in("w_out", [D, D])
    w_gate = dt_in("w_gate", [D, DFF])
    w_up = dt_in("w_up", [D, DFF])
    w_down = dt_in("w_down", [DFF, D])
    vecs_d = dt_in("vecs", [128, NV])
    nab = dt_in("nab", [8, 128, NSLOT * 128])
    consts = dt_in("consts", [128, 5 * 128 + 512])
    y = nc.dram_tensor("y", [4096, D], F32, kind="ExternalOutput").ap()
    catT = nc.dram_tensor("catT", [8, 128, 4096], BF16, kind="Internal").ap()
    wsc_o = nc.dram_tensor("wsc_o", [128, 8, D], BF16, kind="Internal").ap()
    wsc_g = nc.dram_tensor("wsc_g", [128, 8, DFF], BF16, kind="Internal").ap()
    wsc_u = nc.dram_tensor("wsc_u", [128, 8, DFF], BF16, kind="Internal").ap()
    wsc_d = nc.dram_tensor("wsc_d", [128, NU, D], BF16, kind="Internal").ap()
    dbg_out = {}
    for name, shape in dbg:
        dbg_out[name] = nc.dram_tensor("dbg_" + name, list(shape), BF16 if name == "catT" else F32, kind="ExternalOutput").ap()

    w_in_v = w_in.rearrange("(k p) c -> p k c", p=128)
    w_oth_v = w_oth.rearrange("(k p) c -> p k c", p=128)
    w_out_v = w_out.rearrange("(k p) c -> p k c", p=128)
    w_gate_v = w_gate.rearrange("(k p) c -> p k c", p=128)
    w_up_v = w_up.rearrange("(k p) c -> p k c", p=128)
    w_down_v = w_down.rearrange("(u p) c -> p u c", p=128)

    with ExitStack() as st:
        S = Sched(nc)
        A = S.add

        _nm = [0]

        def sbuf(stack, name, shape, dt):
            _nm[0] += 1
            return stack.enter_context(nc.sbuf_tensor("sb%d_%s" % (_nm[0], name), shape, dt))

        ps = [st.enter_context(nc.psum_tensor("ps%d" % i, [128, 512], F32)) for i in range(8)]
        psb = [p[:].bitcast(BF16) for p in ps]
        PS = lambda i: ("ps", i)

        cf = sbuf(st, "cf", [128, 5 * 128 + 512], F32)
        vecs = sbuf(st, "vecs", [128, NV], F32)
        identb = sbuf(st, "identb", [128, 128], BF16)
        maskf = sbuf(st, "maskf", [128, 128], BF16)
        maskb = sbuf(st, "maskb", [128, 128], BF16)
        lbv = sbuf(st, "lbv", [128, 12], F32)
        oml = sbuf(st, "oml", [128, 12], F32)
        noml = sbuf(st, "noml", [128, 12], F32)
        lbd = sbuf(st, "lbd", [128, 12], F32)
        wqk = sbuf(st, "wqk", [128, 2], F32)
        Spre = sbuf(st, "Spre", [128, 4, 128], F32)
        stats = sbuf(st, "stats", [128, 104, 4], F32)
        onesd = cf[:, 384:512]
        blk64 = cf[:, 512:640]
        resetm = cf[:, 640:1152]

        A("sp", lambda e: e.dma_start(out=cf[:], in_=consts), writes=["cf"], dma_key="cf")
        A("sp", lambda e: e.dma_start(out=vecs[:], in_=vecs_d), writes=["vecs"], dma_key="vecs")
        A("dve", lambda e: e.tensor_copy(out=identb[:], in_=cf[:, 0:128]), reads=["cf"], writes=["identb"])
        A("dve", lambda e: e.tensor_copy(out=maskf[:], in_=cf[:, 128:256]), reads=["cf"], writes=["maskf"])
        A("dve", lambda e: e.tensor_copy(out=maskb[:], in_=cf[:, 256:384]), reads=["cf"], writes=["maskb"])
        A("dve", lambda e: e.tensor_tensor(out=lbd[:, 0:8], in0=vecs[:, V_LB:V_LB + 8], in1=vecs[:, V_LB + 8:V_LB + 16],
                                           op=ALU.subtract), reads=["vecs"], writes=["lbd"])
        A("dve", lambda e: e.tensor_tensor(out=lbd[:, 8:12], in0=vecs[:, V_LBO:V_LBO + 4], in1=vecs[:, V_LBO + 4:V_LBO + 8],
                                           op=ALU.subtract), reads=["vecs", "lbd"], writes=["lbd"])
        A("act", lambda e: e.activation(out=lbv[:], in_=lbd[:], func=AF.Sigmoid), reads=["lbd"], writes=["lbv"])
        A("act", lambda e: e.activation(out=oml[:], in_=lbd[:], func=AF.Sigmoid, scale=-1.0), reads=["lbd"], writes=["oml"])
        A("dve", lambda e: e.tensor_scalar(out=noml[:], in0=oml[:], scalar1=-1.0, scalar2=None, op0=ALU.mult),
          reads=["oml"], writes=["noml"])
        A("dve", lambda e: e.tensor_scalar(out=wqk[:, 0:1], in0=vecs[:, V_WQ:V_WQ + 1], scalar1=0.125, scalar2=None,
                                           op0=ALU.mult), reads=["vecs"], writes=["wqk"])
        A("dve", lambda e: e.tensor_copy(out=wqk[:, 1:2], in_=vecs[:, V_WK:V_WK + 1]), reads=["vecs", "wqk"], writes=["wqk"])
        A("pool", lambda e: e.memset(Spre[:], 0.0), writes=["Spre"])
        A("pool", lambda e: e.memset(stats[:, 103, 0:1], EPS), writes=["epsc"])
        A("pool", lambda e: e.memset(stats[:, 103, 1:2], 1.0), reads=["epsc"], writes=["epsc"])

        def dump(name, ap, key):
            if name in dbg_out:
                A("sp", lambda e: e.dma_start(out=dbg_out[name], in_=ap), reads=[key], dma_key="dbg_" + name)

        nt_ctr = [0]

        def norm_tile(src, dst3, dst_key, wcol, xt, xb, pbank, src_key=None, x_keep=None, scale_eng="act"):
            i = nt_ctr[0]
            nt_ctr[0] += 1
            sc = i % 103
            if src is not None:
                sl = i % len(xt)
                xs, xk = xt[sl], "xt%d" % sl
                A("sp", lambda e: e.dma_start(out=xs[:], in_=src), writes=[xk], dma_key=xk)
            else:
                xs, xk = x_keep, src_key
            bs, bk = xb[i % len(xb)], "xb%d" % (i % len(xb))
            stc = ("st", sc)
            A("act", lambda e: e.activation(out=bs[:], in_=xs[:], func=AF.Square, accum_out=stats[:, sc, 0:1]),
              reads=[xk], writes=[bk, stc])
            A("act", lambda e: e.activation(out=stats[:, sc, 2:3], in_=stats[:, sc, 0:1], func=AF.Ln, scale=1.0 / D, bias=stats[:, 103, 0:1]),
              reads=[stc, "epsc"], writes=[stc])
            A("act", lambda e: e.activation(out=stats[:, sc, 3:4], in_=stats[:, sc, 2:3], func=AF.Exp, scale=-0.5),
              reads=[stc], writes=[stc])
            if scale_eng == "dve":
                A("dve", lambda e: e.tensor_scalar(out=bs[:], in0=xs[:], scalar1=stats[:, sc, 3:4], scalar2=None, op0=ALU.mult),
                  reads=[xk, stc], writes=[bk])
            else:
                A("act", lambda e: e.activation(out=bs[:], in_=xs[:], func=AF.Copy, scale=stats[:, sc, 3:4]),
                  reads=[xk, stc], writes=[bk])
            for k in range(8):
                A("pe", lambda e, k=k: e.transpose(out=psb[pbank][:, k * 128:(k + 1) * 128], in_=bs[:, k * 128:(k + 1) * 128],
                                                    identity=identb[:]),
                  reads=[bk, "identb"], writes=[PS(pbank)])
            A("dve", lambda e: e.tensor_tensor(out=dst3, in0=psb[pbank][:, 0:1024].rearrange("p (k t) -> p k t", k=8),
                                               in1=vecs[:, wcol:wcol + 8].unsqueeze(2).to_broadcast([128, 8, 128]),
                                               op=ALU.mult),
              reads=[PS(pbank), "vecs"], writes=[dst_key])

        def load_cast(dst, dst_key, src, stage, nstage, ctr, eng="pool", shape=None):
            i = ctr[0]
            ctr[0] += 1
            sg, sk = stage[i % nstage], "wst%d" % (i % nstage)
            sv = sg[:] if shape is None else shape(sg)
            A("sp", lambda e: e.dma_start(out=sv, in_=src), writes=[sk], dma_key=sk)
            if eng == "act":
                A("act", lambda e: e.activation(out=dst, in_=sv, func=AF.Copy), reads=[sk], writes=[dst_key])
            else:
                A(eng, lambda e: e.tensor_copy(out=dst, in_=sv), reads=[sk], writes=[dst_key])

        with ExitStack() as p12:
            hT = sbuf(p12, "hT", [128, 8, 36 * 128], BF16)
            wctr = [0]
            wst = [sbuf(p12, "wst%d" % i, [128, 8, 128], F32) for i in range(1)]
            p1x = p12.enter_context(ExitStack())
            xt = [sbuf(p1x, "xt%d" % i, [128, D], F32) for i in range(3)]
            xb = [sbuf(p1x, "xb%d" % i, [128, D], BF16) for i in range(3)]
            for e_ in range(36):
                if e_ < 2:
                    src = xhalo[e_ * 128:(e_ + 1) * 128, :]
                elif e_ >= 34:
                    src = xhalo[(e_ - 32) * 128:(e_ - 31) * 128, :]
                else:
                    src = xloc[(e_ - 2) * 128:(e_ - 1) * 128, :]
                norm_tile(src, hT[:, :, e_ * 128:(e_ + 1) * 128], ("hT", e_), V_NMW, xt, xb, e_ % 4, scale_eng="dve")
            if "hT" in dbg_out:
                hTf = sbuf(p1x, "hTf", [128, 8, 512], F32)
                A("dve", lambda e: e.tensor_copy(out=hTf[:], in_=hT[:, :, 256:768]), reads=[("hT", 2), ("hT", 3), ("hT", 4), ("hT", 5)],
                  writes=["hTf"])
                dump("hT", hTf[:], "hTf")


            def gate_batch(G, zbank, lcol, bwd, qsrc, qkeys, tag, ttag=None):
                s1, s2, g, b, bp = G["s1"], G["s2"], G["g"], G["b"], G["bp"]
                k32 = b
                eb = s1
                ttag = ttag or tag
                _al = {"eb": "s1", "k32": "b"}
                _tmp = ("s1", "s2", "g", "b", "bp", "bb")
                K = lambda n: ((ttag if _al.get(n, n) in _tmp else tag), _al.get(n, n))
                A("act", lambda e: e.activation(out=s1[:], in_=ps[zbank][:], func=AF.Exp, scale=-1.0), reads=[PS(zbank)], writes=[K("s1")])
                A("act", lambda e: e.activation(out=s2[:], in_=s1[:], func=AF.Ln, bias=stats[:, 103, 1:2]), reads=[K("s1"), "epsc"], writes=[K("s2")])
                A("act", lambda e: e.activation(out=s1[:], in_=s2[:], func=AF.Exp, scale=-1.0), reads=[K("s2")], writes=[K("s1")])
                A("act", lambda e: e.activation(out=g[:], in_=s1[:], func=AF.Ln, scale=oml[:, lcol:lcol + 1],
                                                bias=lbv[:, lcol:lcol + 1]),
                  reads=[K("s1"), "oml", "lbv"], writes=[K("g")])
                A("pool", lambda e: e.tensor_scalar(out=s2[:], in0=s1[:], scalar1=noml[:, lcol:lcol + 1], scalar2=oml[:, lcol:lcol + 1],
                                                    op0=ALU.mult, op1=ALU.add),
                  reads=[K("s1"), "oml", "noml"], writes=[K("s2")])
                A("dve", lambda e: e.tensor_tensor_scan(out=b[:], data0=resetm, data1=g[:], initial=0.0,
                                                        op0=ALU.mult, op1=ALU.add),
                  reads=[K("g"), "cf"], writes=[K("b")])
                b3 = b[:].rearrange("p (t s) -> p t s", t=4)
                g3 = g[:].rearrange("p (t s) -> p t s", t=4)
                bp3 = bp[:].rearrange("p (t s) -> p t s", t=4)
                dec = G["dec"]
                if not bwd:
                    A("dve", lambda e: e.tensor_tensor(out=bp3, in0=b3, in1=b3[:, :, 63:64].to_broadcast([128, 4, 128]),
                                                       op=ALU.subtract), reads=[K("b")], writes=[K("bp")])
                    A("act", lambda e: e.activation(out=dec[:, :, 0:2], in_=b3[:, :, 63:128:64], func=AF.Exp),
                      reads=[K("b")], writes=[K("dec0"), K("dec1")])
                    G["ci"] = (1, 0)
                    A("act", lambda e: e.activation(out=dec[:, :, 2:3], in_=bp3[:, :, 127:128], func=AF.Exp),
                      reads=[K("bp")], writes=[K("dec2")])
                else:
                    bb3 = G["bb"][:].rearrange("p (t s) -> p t s", t=4)
                    A("dve", lambda e: e.tensor_tensor(out=bb3, in0=g3, in1=b3, op=ALU.subtract),
                      reads=[K("g"), K("b")], writes=[K("bb")])
                    A("dve", lambda e: e.tensor_tensor(out=bp3, in0=bb3, in1=bb3[:, :, 64:65].to_broadcast([128, 4, 128]),
                                                       op=ALU.subtract), reads=[K("bb")], writes=[K("bp")])
                    A("dve", lambda e: e.tensor_tensor(out=dec[:, :, 0:2], in0=bb3[:, :, 0:128:64],
                                                       in1=b3[:, :, 127:128].to_broadcast([128, 4, 2]), op=ALU.add),
                      reads=[K("bb"), K("b")], writes=[K("dec0"), K("dec1")])
                    A("act", lambda e: e.activation(out=dec[:, :, 0:2], in_=dec[:, :, 0:2], func=AF.Exp),
                      reads=[K("dec0"), K("dec1")], writes=[K("dec0"), K("dec1")])
                    G["ci"] = (0, 1)
                    A("act", lambda e: e.activation(out=dec[:, :, 2:3], in_=bp3[:, :, 0:1], func=AF.Exp),
                      reads=[K("bp")], writes=[K("dec2")])
                A("act", lambda e: e.activation(out=eb[:], in_=bp[:], func=AF.Exp, scale=-1.0), reads=[K("bp")], writes=[K("eb")])
                A("dve", lambda e: e.tensor_tensor(out=k32[:], in0=s2[:], in1=eb[:], op=ALU.mult),
                  reads=[K("s2"), K("eb")], writes=[K("k32")])
                if G.get("kt") is not None:
                    A("pool", lambda e: e.tensor_copy(out=G["kt"][:], in_=k32[:]), reads=[K("k32")], writes=[K("kt")])
                A("pool", lambda e: e.tensor_tensor(out=G["khT"][:].rearrange("p (t s) -> p t s", t=4),
                                                   in0=k32[:].rearrange("p (t s) -> p t s", t=4),
                                                   in1=dec[:, :, 2:3].to_broadcast([128, 4, 128]), op=ALU.mult),
                  reads=[K("k32"), K("dec2")], writes=[K("khT")])
                if qsrc is not None:
                    A("act", lambda e: e.activation(out=eb[:], in_=bp[:], func=AF.Exp), reads=[K("bp"), K("k32")], writes=[K("eb")])
                    A("pool", lambda e: e.tensor_tensor(out=G["qt"][:], in0=qsrc, in1=eb[:], op=ALU.mult),
                      reads=[K("eb")] + list(qkeys), writes=[K("qt")])

            def alloc_gate_tmp(stack, tag, bwd):
                T = {}
                for n in ["s1", "s2", "g", "b", "bp"] + (["bb"] if bwd else []):
                    T[n] = sbuf(stack, "%s_%s" % (tag, n), [128, 512], F32)
                return T

            def alloc_gate(stack, tag, with_q, bwd=False, tmp=True):
                G = {}
                if tmp:
                    G.update(alloc_gate_tmp(stack, tag, bwd))
                G["khT"] = sbuf(stack, tag + "_khT", [128, 512], BF16)
                G["kh"] = sbuf(stack, tag + "_kh", [128, 4, 128], BF16)
                G["dec"] = sbuf(stack, tag + "_dec", [128, 4, 3], F32)
                if with_q:
                    G["kt"] = sbuf(stack, tag + "_kt", [128, 512], BF16)
                    G["kt2"] = sbuf(stack, tag + "_kt2", [128, 512], BF16)
                    G["qt"] = sbuf(stack, tag + "_qt", [128, 512], BF16)
                    G["am"] = sbuf(stack, tag + "_am", [128, 4, 128], BF16)
                    G["sin"] = sbuf(stack, tag + "_sin", [128, 4, 128], BF16)
                return G

            def khat_transpose(G, tag, pbank):
                for t in range(4):
                    A("pe", lambda e, t=t: e.transpose(out=psb[pbank][:, t * 128:(t + 1) * 128],
                                                       in_=G["khT"][:, t * 128:(t + 1) * 128], identity=identb[:]),
                      reads=[(tag, "khT"), "identb"], writes=[PS(pbank)])
                A("act", lambda e: e.activation(out=G["kh"][:].rearrange("p t d -> p (t d)"), in_=psb[pbank][:, 0:512], func=AF.Copy),
                  reads=[PS(pbank)], writes=[(tag, "kh")])

            if do_hgrn and do_pre:
                with ExitStack() as pp:
                    wo = sbuf(pp, "wo", [128, 8, 1024], BF16)
                    for c in range(8):
                        load_cast(wo[:, :, c * 128:(c + 1) * 128], ("wo", c), w_oth_v[:, :, c * 128:(c + 1) * 128], wst, 1, wctr)
                    hTo = [sbuf(pp, "hTo%d" % i, [128, 8, 512], BF16) for i in range(2)]
                    Gps = [alloc_gate(pp, "gp%d" % i, False, bwd=False, tmp=True) for i in range(2)]
                    vtos = [sbuf(pp, "vto%d" % i, [128, 4, 128], BF16) for i in range(2)]
                    for jb in range(NB):
                        ho = hTo[jb % 2]
                        hk = "hTo%d" % (jb % 2)
                        for t in range(4):
                            r0 = (jb * 4 + t) * 128
                            norm_tile(xoth[r0:r0 + 128, :], ho[:, :, t * 128:(t + 1) * 128], hk, V_NMW, xt, xb, t % 2)
                        for h in range(4):
                            par = h % 2
                            Gp, gtag = Gps[par], "gp%d" % par
                            vto, vk = vtos[par], "vto%d" % par
                            zbk, vbk = 2 + 4 * par, 3 + 4 * par
                            for k in range(8):
                                A("pe", lambda e, k=k, h=h: e.matmul(ps[zbk][:], lhsT=wo[:, k, h * 128:(h + 1) * 128], rhs=ho[:, k, :],
                                                                     start=(k == 0), stop=(k == 7)),
                                  reads=[hk, ("wo", h)], writes=[PS(zbk)])
                            for t in range(4):
                                for k in range(8):
                                    A("pe", lambda e, k=k, h=h, t=t: e.matmul(ps[vbk][:, t * 128:(t + 1) * 128],
                                                                              lhsT=ho[:, k, t * 128:(t + 1) * 128],
                                                                              rhs=wo[:, k, 512 + h * 128:512 + (h + 1) * 128],
                                                                              start=(k == 0), stop=(k == 7)),
                                      reads=[hk, ("wo", 4 + h)], writes=[PS(vbk)])
                            A("act", lambda e: e.activation(out=vto[:].rearrange("p t d -> p (t d)"), in_=ps[vbk][:], func=AF.Copy),
                              reads=[PS(vbk)], writes=[vk])
                            gate_batch(Gp, zbk, 8 + h, False, None, None, gtag)
                            khat_transpose(Gp, gtag, 4)
                            for t in range(4):
                                A("pe", lambda e, t=t: e.matmul(ps[5][:, t * 128:(t + 1) * 128], lhsT=Gp["kh"][:, t, :], rhs=vto[:, t, :],
                                                                start=True, stop=True),
                                  reads=[(gtag, "kh"), vk], writes=[PS(5)])
                            for t in range(4):
                                A("dve", lambda e, t=t, h=h: e.scalar_tensor_tensor(out=Spre[:, h, :], in0=Spre[:, h, :],
                                                                                    scalar=Gp["dec"][:, t, Gp["ci"][0]:Gp["ci"][0] + 1],
                                                                                    in1=ps[5][:, t * 128:(t + 1) * 128],
                                                                                    op0=ALU.mult, op1=ALU.add),
                                  reads=[PS(5), (gtag, "dec0"), ("Spre", h), "Spre"], writes=[("Spre", h)])
                S.barrier()
            dump("Spre", Spre[:].rearrange("p h d -> p (h d)"), "Spre")
            S.barrier()
            p1x.close()

            if do_hgrn:
                with ExitStack() as ph:
                    whd = [sbuf(ph, "whd%d" % i, [128, 8, 5, 128], BF16) for i in range(2)]
                    qsT = sbuf(ph, "qsT", [128, 4096], BF16)
                    vtm = sbuf(ph, "vtm", [128, NT, 128], BF16)
                    oacc = sbuf(ph, "oacc", [128, 4096], F32)
                    Sst = [sbuf(ph, "Sst%d" % i, [128, 128], F32) for i in range(2)]
                    Gsh = [alloc_gate(ph, "gf", True, tmp=False), alloc_gate(ph, "gb", True, tmp=False)]
                    Gtm = [[alloc_gate_tmp(ph, "gf%d" % i, False) for i in range(2)],
                           [alloc_gate_tmp(ph, "gb%d" % i, True) for i in range(2)]]
                    sgate = sbuf(ph, "sgate", [128, 512], F32)
                    sq = sbuf(ph, "sq", [128, 512], F32)
                    rs = sbuf(ph, "rs", [128, 512], F32)
                    cblk = [sbuf(ph, "cblk%d" % i, [128, 512], BF16) for i in range(2)]
                    colbase = [0, 512, 1024, 1536, 2048]

                    def load_head_w(h):
                        wb = whd[h % 2]
                        for ci, cb in enumerate(colbase):
                            load_cast(wb[:, :, ci, :], ("whd", h % 2, ci), w_in_v[:, :, cb + h * 128:cb + (h + 1) * 128], wst, 1, wctr)

                    def hcols(j):
                        return slice((2 + 4 * j) * 128, (2 + 4 * j + 4) * 128)

                    def hkeys(j):
                        return [("hT", 2 + 4 * j + t) for t in range(4)]

                    def proj_fm(bank, wb, wkey, ci, j):
                        for k in range(8):
                            A("pe", lambda e, k=k: e.matmul(ps[bank][:], lhsT=wb[:, k, ci, :], rhs=hT[:, k, hcols(j)],
                                                            start=(k == 0), stop=(k == 7)),
                              reads=hkeys(j) + [wkey], writes=[PS(bank)])

                    A("dve", lambda e: e.memset(ps[6][:], 0.0), writes=[PS(6)])
                    A("pool", lambda e: e.memset(Gsh[0]["kt2"][:], 0.0), writes=["kt2z", ("gf", "kt2")])
                    A("pool", lambda e: e.memset(Gsh[1]["kt2"][:], 0.0), reads=["kt2z"], writes=["kt2z", ("gb", "kt2")])
                    load_head_w(0)
                    for h in range(4):
                        wb = whd[h % 2]
                        WK = lambda ci: ("whd", h % 2, ci)
                        if h + 1 < 4:
                            load_head_w(h + 1)
                        for j in range(NB):
                            proj_fm(0 + j % 2, wb, WK(0), 0, j)
                            ta_, tk = ((sq, "sq"), (rs, "rs"))[j % 2]
                            A("act", lambda e, j=j, ta_=ta_: e.activation(out=ta_[:], in_=ps[j % 2][:], func=AF.Exp, scale=-1.0),
                              reads=[PS(j % 2)], writes=[tk])
                            A("act", lambda e, ta_=ta_: e.activation(out=ta_[:], in_=ta_[:], func=AF.Ln, bias=stats[:, 103, 1:2]),
                              reads=[tk, "epsc"], writes=[tk])
                            A("act", lambda e, ta_=ta_: e.activation(out=ta_[:], in_=ta_[:], func=AF.Exp, scale=-1.0), reads=[tk], writes=[tk])
                            A("dve", lambda e, j=j, ta_=ta_: e.tensor_tensor(out=qsT[:, j * 512:(j + 1) * 512], in0=ps[j % 2][:], in1=ta_[:],
                                                                            op=ALU.mult),
                              reads=[PS(j % 2), tk], writes=[("qsT", j)])
                            bank = 2 + j % 2
                            for t in range(4):
                                et = 2 + 4 * j + t
                                for k in range(8):
                                    A("pe", lambda e, k=k, t=t, et=et, bank=bank: e.matmul(
                                        ps[bank][:, t * 128:(t + 1) * 128], lhsT=hT[:, k, et * 128:(et + 1) * 128],
                                        rhs=wb[:, k, 3, :], start=(k == 0), stop=(k == 7)),
                                      reads=[("hT", et), WK(3)], writes=[PS(bank)])
                            A("dve", lambda e, j=j, bank=bank: e.tensor_copy(out=vtm[:, 4 * j:4 * j + 4, :].rearrange("p t d -> p (t d)"),
                                                                            in_=ps[bank][:]),
                              reads=[PS(bank)], writes=[("vtm", j)])
                        A("dve", lambda e, h=h: e.tensor_scalar(out=Sst[0][:], in0=Spre[:, h, :], scalar1=vecs[:, V_AL:V_AL + 1],
                                                                scalar2=None, op0=ALU.mult),
                          reads=["Spre", "vecs"], writes=["Sst0"])
                        A("dve", lambda e, h=h: e.tensor_scalar(out=Sst[1][:], in0=Spre[:, h, :], scalar1=vecs[:, V_BE:V_BE + 1],
                                                                scalar2=None, op0=ALU.mult),
                          reads=["Spre", "vecs"], writes=["Sst1"])
                        for step in range(NB):
                            for d in range(2):
                                j = step if d == 0 else NB - 1 - step
                                G = dict(Gsh[d])
                                G.update(Gtm[d][step % 2])
                                tag = "gf" if d == 0 else "gb"
                                ttag = tag + str(step % 2)
                                zb = 4 + d
                                proj_fm(zb, wb, WK(1 + d), 1 + d, j)
                                gate_batch(G, zb, d * 4 + h, d == 1, qsT[:, j * 512:(j + 1) * 512], [("qsT", j)], tag, ttag)
                                zb = 6
                                khat_transpose(G, tag, 3)
                                for t in range(4):
                                    A("pe", lambda e, t=t, j=j, G=G: e.matmul(ps[7][:, t * 128:(t + 1) * 128], lhsT=G["kh"][:, t, :],
                                                                              rhs=vtm[:, 4 * j + t, :], start=True, stop=True),
                                      reads=[(tag, "kh"), ("vtm", j)], writes=[PS(7)])
                                hsl = slice(0, 64) if d == 0 else slice(64, 128)
                                A("pool", lambda e, G=G, hsl=hsl: e.tensor_copy(
                                    out=G["kt2"][:].rearrange("p (t s) -> p t s", t=4)[:, :, hsl],
                                    in_=G["b"][:].rearrange("p (t s) -> p t s", t=4)[:, :, hsl]),
                                  reads=[(ttag, "b"), "kt2z"], writes=[(tag, "kt2")])
                                for t in range(4):
                                    c0 = t * 128
                                    l1, l2 = (G["kt2"], G["kt"]) if d == 0 else (G["kt"], G["kt2"])
                                    k1, k2 = ((tag, "kt2"), (tag, "kt")) if d == 0 else ((tag, "kt"), (tag, "kt2"))
                                    A("pe", lambda e, c0=c0, l1=l1, G=G: e.matmul(ps[zb][:, c0:c0 + 64], lhsT=l1[:, c0:c0 + 128],
                                                                                  rhs=G["qt"][:, c0:c0 + 64], start=True, stop=True),
                                      reads=[k1, (tag, "qt")], writes=[PS(zb)])
                                    A("pe", lambda e, c0=c0, l2=l2, G=G: e.matmul(ps[zb][:, c0 + 64:c0 + 128], lhsT=l2[:, c0:c0 + 128],
                                                                                  rhs=G["qt"][:, c0 + 64:c0 + 128], start=True, stop=True),
                                      reads=[k2, (tag, "qt")], writes=[PS(zb)])
                                mk = maskf if d == 0 else maskb
                                A("dve", lambda e, G=G, mk=mk: e.tensor_tensor(out=G["am"][:], in0=ps[zb][:].rearrange("p (t s) -> p t s", t=4),
                                                                               in1=mk[:].unsqueeze(1).to_broadcast([128, 4, 128]),
                                                                               op=ALU.mult),
                                  reads=[PS(zb), "maskf", "maskb"], writes=[(tag, "am")])
                                order = range(4) if d == 0 else range(3, -1, -1)
                                Sd, sk = Sst[d], "Sst%d" % d
                                for t in order:
                                    A("act", lambda e, t=t, G=G, Sd=Sd: e.activation(out=G["sin"][:, t, :], in_=Sd[:], func=AF.Copy,
                                                                                     scale=G["dec"][:, t, G["ci"][1]:G["ci"][1] + 1]),
                                      reads=[sk, (tag, "dec1")], writes=[(tag, "sin", t)])
                                    A("dve", lambda e, t=t, G=G, Sd=Sd: e.scalar_tensor_tensor(out=Sd[:], in0=Sd[:], scalar=G["dec"][:, t, G["ci"][0]:G["ci"][0] + 1],
                                                                                               in1=ps[7][:, t * 128:(t + 1) * 128],
                                                                                               op0=ALU.mult, op1=ALU.add),
                                      reads=[PS(7), (tag, "dec0"), sk, (tag, "sin", t)], writes=[sk])
                                ob = d
                                for t in range(4):
                                    A("pe", lambda e, t=t, j=j, G=G: e.matmul(ps[ob][:, t * 128:(t + 1) * 128], lhsT=vtm[:, 4 * j + t, :],
                                                                              rhs=G["am"][:, t, :], start=True, stop=False),
                                      reads=[(tag, "am"), ("vtm", j)], writes=[PS(ob)])
                                    A("pe", lambda e, t=t, G=G: e.matmul(ps[ob][:, t * 128:(t + 1) * 128], lhsT=G["sin"][:, t, :],
                                                                         rhs=G["qt"][:, t * 128:(t + 1) * 128], start=False, stop=True),
                                      reads=[(tag, "sin", t), (tag, "qt")], writes=[PS(ob)])
                                first = (step < NB // 2)
                                if first:
                                    A("act", lambda e, j=j: e.activation(out=oacc[:, j * 512:(j + 1) * 512], in_=ps[ob][:], func=AF.Copy),
                                      reads=[PS(ob)], writes=[("oacc", j)])
                                else:
                                    A("dve", lambda e, j=j: e.tensor_tensor(out=oacc[:, j * 512:(j + 1) * 512], in0=oacc[:, j * 512:(j + 1) * 512],
                                                                            in1=ps[ob][:], op=ALU.add),
                                      reads=[PS(ob), ("oacc", j)], writes=[("oacc", j)])
                        for j in range(NB):
                            proj_fm(2, wb, WK(4), 4, j)
                            A("act", lambda e: e.activation(out=sgate[:], in_=ps[2][:], func=AF.Exp, scale=-1.0), reads=[PS(2)], writes=["sgate"])
                            A("act", lambda e: e.activation(out=sgate[:], in_=sgate[:], func=AF.Ln, bias=stats[:, 103, 1:2]),
                              reads=["sgate", "epsc"], writes=["sgate"])
                            A("act", lambda e: e.activation(out=sgate[:], in_=sgate[:], func=AF.Exp, scale=-1.0), reads=["sgate"], writes=["sgate"])
                            A("dve", lambda e: e.tensor_tensor(out=sgate[:], in0=ps[2][:], in1=sgate[:], op=ALU.mult),
                              reads=[PS(2), "sgate"], writes=["sgate"])
                            oj = oacc[:, j * 512:(j + 1) * 512]
                            A("pool", lambda e, oj=oj: e.tensor_tensor(out=sq[:], in0=oj, in1=oj, op=ALU.mult),
                              reads=[("oacc", j)], writes=["sq"])
                            A("pe", lambda e: e.matmul(ps[3][:], lhsT=onesd, rhs=sq[:], start=True, stop=True),
                              reads=["sq", "cf"], writes=[PS(3)])
                            A("act", lambda e: e.activation(out=rs[:], in_=ps[3][:], func=AF.Ln, bias=stats[:, 103, 0:1]),
                              reads=[PS(3), "epsc"], writes=["rs"])
                            A("act", lambda e: e.activation(out=rs[:], in_=rs[:], func=AF.Exp, scale=-0.5), reads=["rs"], writes=["rs"])
                            A("dve", lambda e, oj=oj: e.scalar_tensor_tensor(out=sq[:], in0=oj, scalar=vecs[:, V_GW:V_GW + 1], in1=rs[:],
                                                                             op0=ALU.mult, op1=ALU.mult),
                              reads=[("oacc", j), "rs", "vecs", "sq"], writes=["sq"])
                            cb_, ck = cblk[j % 2], "cblk%d" % (j % 2)
                            A("dve", lambda e, cb_=cb_: e.tensor_tensor(out=cb_[:], in0=sq[:], in1=sgate[:], op=ALU.mult),
                              reads=["sq", "sgate"], writes=[ck])
                            A("sp", lambda e, cb_=cb_, j=j, h=h: e.dma_start(out=catT[h, :, j * 512:(j + 1) * 512], in_=cb_[:]),
                              reads=[ck], writes=[("catT", h, j)], dma_key=ck)
                            if h == 0 and j == 0:
                                dump("oacc", oacc[:, 0:512], ("oacc", 0))
                S.barrier()

            if do_na:
                with ExitStack() as pa:
                    wna = [sbuf(pa, "wna%d" % i, [128, 8, 3, 128], BF16) for i in range(2)]
                    knT2 = [sbuf(pa, "knT%d" % i, [128, 36 * 128], BF16) for i in range(2)]
                    qnT2 = [sbuf(pa, "qnT%d" % i, [128, 4096], BF16) for i in range(2)]
                    vaug2 = [sbuf(pa, "vaug%d" % i, [128, 36, 2, 65], BF16) for i in range(2)]
                    EB = sbuf(pa, "EB", [128, 2, NSLOT * 128], BF16)
                    ebst = [sbuf(pa, "ebst%d" % i, [128, 6 * 128], F32) for i in range(2)]
                    qf = sbuf(pa, "qf", [128, 512], F32)
                    sq = sbuf(pa, "nsq", [128, 512], F32)
                    rs = sbuf(pa, "nrs", [128, 512], F32)
                    Eb = [sbuf(pa, "Eb%d" % i, [128, 6 * 128], BF16) for i in range(3)]
                    onat = [sbuf(pa, "onat%d" % i, [128, 128], BF16) for i in range(2)]
                    rcp = sbuf(pa, "rcp", [128, 8], F32)
                    cblk = [sbuf(pa, "ncblk%d" % i, [128, 512], BF16) for i in range(2)]
                    ebctr = [0]
                    A("pool", lambda e: e.memset(vaug2[0][:], 1.0), writes=[("vaug", 0)])
                    A("pool", lambda e: e.memset(vaug2[1][:], 1.0), writes=[("vaug", 1)])
                    wcf = [sbuf(pa, "wcf%d" % i, [128, D], F32) for i in range(3)]
                    wcb = [sbuf(pa, "wcb%d" % i, [128, D], BF16) for i in range(3)]
                    pieces = []
                    for k in range(8):
                        pieces.append((w_out[k * 128:(k + 1) * 128, :], wsc_o[:, k, :], D))
                    for k in range(8):
                        for (c0, cw) in ((0, 1024), (1024, 1024), (2048, 768)):
                            pieces.append((w_gate[k * 128:(k + 1) * 128, c0:c0 + cw], wsc_g[:, k, c0:c0 + cw], cw))
                            pieces.append((w_up[k * 128:(k + 1) * 128, c0:c0 + cw], wsc_u[:, k, c0:c0 + cw], cw))
                    for u in range(NU):
                        pieces.append((w_down[u * 128:(u + 1) * 128, :], wsc_d[:, u, :], D))
                    for i, (src_, dst_, cw) in enumerate(pieces):
                        f_, fk = wcf[i % 3], "wcf%d" % (i % 3)
                        b_, bk = wcb[i % 3], "wcb%d" % (i % 3)
                        A("pool", lambda e, f_=f_, src_=src_, cw=cw: e.dma_start(out=f_[:, 0:cw], in_=src_), writes=[fk], dma_key=fk)
                        A("pool", lambda e, f_=f_, b_=b_, cw=cw: e.tensor_copy(out=b_[:, 0:cw], in_=f_[:, 0:cw]), reads=[fk], writes=[bk])
                        A("pool", lambda e, b_=b_, dst_=dst_, cw=cw: e.dma_start(out=dst_, in_=b_[:, 0:cw]), reads=[bk], dma_key=bk)

                    nacol = [2560, 3072, 3584]

                    def load_na_w(hp):
                        wb = wna[hp % 2]
                        for ci, cb in enumerate(nacol):
                            load_cast(wb[:, :, ci, :], ("wna", hp % 2, ci), w_in_v[:, :, cb + hp * 128:cb + (hp + 1) * 128], wst, 1, wctr, eng="act")

                    def qknorm(bank, ncols, wcol, dst, dkey):
                        A("act", lambda e: e.activation(out=qf[:, 0:ncols], in_=ps[bank][:, 0:ncols], func=AF.Copy),
                          reads=[PS(bank)], writes=["qf"])
                        A("act", lambda e: e.activation(out=sq[:, 0:ncols], in_=ps[bank][:, 0:ncols], func=AF.Square),
                          reads=[PS(bank)], writes=["nsq"])
                        A("pe", lambda e: e.matmul(ps[7][:, 0:ncols], lhsT=blk64, rhs=sq[:, 0:ncols], start=True, stop=True),
                          reads=["nsq", "cf"], writes=[PS(7)])
                        A("act", lambda e: e.activation(out=rs[:, 0:ncols], in_=ps[7][:, 0:ncols], func=AF.Ln, bias=stats[:, 103, 0:1]),
                          reads=[PS(7), "epsc"], writes=["nrs"])
                        A("act", lambda e: e.activation(out=rs[:, 0:ncols], in_=rs[:, 0:ncols], func=AF.Exp, scale=-0.5),
                          reads=["nrs"], writes=["nrs"])
                        A("dve", lambda e: e.scalar_tensor_tensor(out=dst, in0=qf[:, 0:ncols], scalar=wqk[:, wcol:wcol + 1],
                                                                  in1=rs[:, 0:ncols], op0=ALU.mult, op1=ALU.mult),
                          reads=["qf", "nrs", "wqk"], writes=[dkey])

                    load_na_w(0)
                    for hp in range(4):
                        wb = wna[hp % 2]
                        WK = lambda ci: ("wna", hp % 2, ci)
                        knT, qnT, vaug = knT2[hp % 2], qnT2[hp % 2], vaug2[hp % 2]
                        hb = hp % 2
                        if hp + 1 < 4:
                            load_na_w(hp + 1)
                        for hh in range(2):
                            for (s0, ns) in ((0, 5), (5, 6), (11, 5), (16, 5), (21, 6)):
                                i = ebctr[0]
                                ebctr[0] += 1
                                sg, sk = ebst[i % 2], "ebst%d" % (i % 2)
                                A("sp", lambda e, sg=sg, s0=s0, ns=ns, hh=hh: e.dma_start(
                                    out=sg[:, 0:ns * 128], in_=nab[hp * 2 + hh, :, s0 * 128:(s0 + ns) * 128]),
                                  writes=[sk], dma_key=sk)
                                A("act", lambda e, sg=sg, s0=s0, ns=ns, hh=hh: e.activation(
                                    out=EB[:, hh, s0 * 128:(s0 + ns) * 128], in_=sg[:, 0:ns * 128], func=AF.Exp),
                                  reads=[sk], writes=[("EB", hh, s0)])
                        pieces = [(0, 2), (34, 2)] + [(2 + 4 * j, 4) for j in range(NB)]
                        for pi, (e0, ntl) in enumerate(pieces):
                            ncols = ntl * 128
                            cs = slice(e0 * 128, (e0 + ntl) * 128)
                            hk = [("hT", e0 + t) for t in range(ntl)]
                            kb = pi % 2
                            for k in range(8):
                                A("pe", lambda e, k=k, kb=kb, cs=cs, ncols=ncols: e.matmul(ps[kb][:, 0:ncols], lhsT=wb[:, k, 1, :],
                                                                                           rhs=hT[:, k, cs], start=(k == 0), stop=(k == 7)),
                                  reads=hk + [WK(1)], writes=[PS(kb)])
                            qknorm(kb, ncols, 1, knT[:, cs], ("knT", hb, pi))
                            vb = 2 + pi % 2
                            for t in range(ntl):
                                et = e0 + t
                                for k in range(8):
                                    A("pe", lambda e, k=k, t=t, et=et, vb=vb: e.matmul(
                                        ps[vb][:, t * 128:(t + 1) * 128], lhsT=hT[:, k, et * 128:(et + 1) * 128],
                                        rhs=wb[:, k, 2, :], start=(k == 0), stop=(k == 7)),
                                      reads=[("hT", et), WK(2)], writes=[PS(vb)])
                            A("dve", lambda e, vb=vb, e0=e0, ntl=ntl: e.tensor_copy(
                                out=vaug[:, e0:e0 + ntl, :, 0:64],
                                in_=ps[vb][:, 0:ntl * 128].rearrange("p (t h d) -> p t h d", t=ntl, h=2)),
                              reads=[PS(vb), ("vaug", hb)], writes=[("vaug", hb, pi)])
                            if e0 >= 2 and e0 < 34:
                                j = (e0 - 2) // 4
                                qb = 4 + pi % 2
                                for k in range(8):
                                    A("pe", lambda e, k=k, qb=qb, cs=cs: e.matmul(ps[qb][:], lhsT=wb[:, k, 0, :], rhs=hT[:, k, cs],
                                                                                  start=(k == 0), stop=(k == 7)),
                                      reads=hk + [WK(0)], writes=[PS(qb)])
                                qknorm(qb, 512, 0, qnT[:, j * 512:(j + 1) * 512], ("qnT", hb, j))
                        allk = [("knT", hb, pi) for pi in range(len(pieces))]
                        allv = [("vaug", hb, pi) for pi in range(len(pieces))]
                        for p in range(NT):
                            s0, keys = na_chunks(p)
                            nch = len(keys)
                            for hh in range(2):
                                it = p * 2 + hh
                                hs = slice(hh * 64, (hh + 1) * 64)
                                r3 = it % 3
                                ba, bb_ = 2 * r3, 2 * r3 + 1
                                for c, ke in enumerate(keys):
                                    if c < 4:
                                        dst, bank = ps[ba][:, c * 128:(c + 1) * 128], ba
                                    else:
                                        dst, bank = ps[bb_][:, 128 + (c - 4) * 128:128 + (c - 3) * 128], bb_
                                    A("pe", lambda e, dst=dst, ke=ke, hs=hs: e.matmul(
                                        dst, lhsT=knT[hs, ke * 128:(ke + 1) * 128],
                                        rhs=qnT[hs, p * 128:(p + 1) * 128], start=True, stop=True),
                                      reads=allk + [("qnT", hb, p // 4)], writes=[PS(bank)])
                                Eb_, ek = Eb[r3], "Eb%d" % r3
                                A("act", lambda e, Eb_=Eb_, ba=ba: e.activation(out=Eb_[:, 0:512], in_=ps[ba][:], func=AF.Exp),
                                  reads=[PS(ba)], writes=[ek])
                                A("act", lambda e, Eb_=Eb_, nch=nch, bb_=bb_: e.activation(out=Eb_[:, 512:nch * 128],
                                                                                           in_=ps[bb_][:, 128:128 + (nch - 4) * 128], func=AF.Exp),
                                  reads=[PS(bb_), ek], writes=[ek])
                                A("dve", lambda e, Eb_=Eb_, nch=nch, s0=s0, hh=hh: e.tensor_tensor(
                                    out=Eb_[:, 0:nch * 128], in0=Eb_[:, 0:nch * 128], in1=EB[:, hh, s0 * 128:(s0 + nch) * 128], op=ALU.mult),
                                  reads=[ek, ("EB", hh, s0)], writes=[ek])
                                ob = bb_
                                for c, ke in enumerate(keys):
                                    A("pe", lambda e, c=c, ke=ke, Eb_=Eb_, hh=hh, ob=ob: e.matmul(
                                        ps[ob][:, 0:65], lhsT=Eb_[:, c * 128:(c + 1) * 128], rhs=vaug[:, ke, hh, :],
                                        start=(c == 0), stop=(c == nch - 1)),
                                      reads=[ek] + allv, writes=[PS(ob)])
                                rc = it % 6
                                A("dve", lambda e, ob=ob, rc=rc: e.reciprocal(out=rcp[:, rc:rc + 1], in_=ps[ob][:, 64:65]),
                                  reads=[PS(ob)], writes=[("rcp", rc)])
                                on_, ok_ = onat[p % 2], "onat%d" % (p % 2)
                                A("dve", lambda e, ob=ob, hh=hh, on_=on_, rc=rc: e.tensor_scalar(
                                    out=on_[:, hh * 64:(hh + 1) * 64], in0=ps[ob][:, 0:64], scalar1=rcp[:, rc:rc + 1], scalar2=None, op0=ALU.mult),
                                  reads=[PS(ob), ("rcp", rc)], writes=[(ok_, hh)])
                            on_, ok_ = onat[p % 2], "onat%d" % (p % 2)
                            A("pe", lambda e, on_=on_, p=p: e.transpose(out=psb[6][:, (p % 4) * 128:(p % 4 + 1) * 128], in_=on_[:], identity=identb[:]),
                              reads=[(ok_, 0), (ok_, 1), "identb"], writes=[PS(6)])
                            if p % 4 == 3:
                                j = p // 4
                                cb_, ck = cblk[j % 2], "ncblk%d" % (j % 2)
                                A("act", lambda e, cb_=cb_: e.activation(out=cb_[:], in_=psb[6][:, 0:512], func=AF.Copy),
                                  reads=[PS(6)], writes=[ck])
                                A("sp", lambda e, cb_=cb_, j=j, hp=hp: e.dma_start(out=catT[4 + hp, :, j * 512:(j + 1) * 512], in_=cb_[:]),
                                  reads=[ck], writes=[("catT", 4 + hp, j)], dma_key=ck)
                S.barrier()
        S.barrier()
        if "catT" in dbg_out:
            A("sp", lambda e: e.dma_start(out=dbg_out["catT"], in_=catT), dma_key="dbg_catT")

        with ExitStack() as p3:
            wo_b = sbuf(p3, "wo_b", [128, 8, D], BF16)
            wg_b = sbuf(p3, "wg_b", [128, 8, DFF], BF16)
            wu_b = sbuf(p3, "wu_b", [128, 8, DFF], BF16)
            wdr = [sbuf(p3, "wdr%d" % i, [128, D], BF16) for i in range(6)]
            cat = [sbuf(p3, "cat%d" % i, [128, 8, 512], BF16) for i in range(2)]
            x1 = [sbuf(p3, "x1_%d" % i, [128, D], F32) for i in range(6)]
            xb3 = [sbuf(p3, "xb%d" % i, [128, D], BF16) for i in range(2)]
            h2T = sbuf(p3, "h2T", [128, 8, 512], BF16)
            actT = sbuf(p3, "actT", [128, NU, 512], BF16)
            sg3 = [sbuf(p3, "sg3_%d" % i, [128, 512], BF16) for i in range(2)]
            for kq in range(4):
                A("sp", lambda e, kq=kq: e.dma_start(out=wo_b[:, 2 * kq:2 * kq + 2, :], in_=wsc_o[:, 2 * kq:2 * kq + 2, :]),
                  writes=[("wo_b", kq)], dma_key=("wo_b", kq))
            for kq in range(4):
                A("sp", lambda e, kq=kq: e.dma_start(out=wg_b[:, 2 * kq:2 * kq + 2, :], in_=wsc_g[:, 2 * kq:2 * kq + 2, :]),
                  writes=[("wg_b", kq)], dma_key=("wg_b", kq))
                A("sp", lambda e, kq=kq: e.dma_start(out=wu_b[:, 2 * kq:2 * kq + 2, :], in_=wsc_u[:, 2 * kq:2 * kq + 2, :]),
                  writes=[("wu_b", kq)], dma_key=("wu_b", kq))
            wdc = [0]
            x1c = [0]
            for j in range(NB):
                cj, cjk = cat[j % 2], "cat%d" % (j % 2)
                if do_hgrn or do_na:
                    for k in range(8):
                        if (k < 4 and do_hgrn) or (k >= 4 and do_na):
                            A("sp", lambda e, k=k, cj=cj, j=j: e.dma_start(out=cj[:, k, :], in_=catT[k, :, j * 512:(j + 1) * 512]),
                              reads=[("catT", k, j)], writes=[(cjk, k)], dma_key=(cjk, k))
                xts = []
                for t in range(4):
                    tt = j * 4 + t
                    i = x1c[0]
                    x1c[0] += 1
                    xs, xk = x1[i % 6], "x1_%d" % (i % 6)
                    xts.append((xs, xk))
                    A("sp", lambda e, xs=xs, tt=tt: e.dma_start(out=xs[:], in_=xloc[tt * 128:(tt + 1) * 128, :]), writes=[xk], dma_key=xk)
                    if do_hgrn or do_na:
                        ks = [k for k in range(8) if (k < 4 and do_hgrn) or (k >= 4 and do_na)]
                        for half in range(2):
                            for ki, k in enumerate(ks):
                                A("pe", lambda e, k=k, half=half, t=t, cj=cj, ki=ki, nk=len(ks): e.matmul(
                                    ps[half][:], lhsT=cj[:, k, t * 128:(t + 1) * 128], rhs=wo_b[:, k, half * 512:(half + 1) * 512],
                                    start=(ki == 0), stop=(ki == nk - 1)),
                                  reads=[(cjk, k), ("wo_b", k // 2)],
                                  writes=[PS(half)])
                            A("dve", lambda e, xs=xs, half=half: e.tensor_tensor(out=xs[:, half * 512:(half + 1) * 512],
                                                                                 in0=xs[:, half * 512:(half + 1) * 512], in1=ps[half][:], op=ALU.add),
                              reads=[PS(half), xk], writes=[xk])
                    norm_tile(None, h2T[:, :, t * 128:(t + 1) * 128], ("h2T", t), V_NFW, None, xb3, 2, src_key=xk, x_keep=xs)
                hkeys3 = [("h2T", t) for t in range(4)]
                for u in range(NU):
                    gb, ub = 3 + 2 * (u % 2), 4 + 2 * (u % 2)
                    for k in range(8):
                        A("pe", lambda e, k=k, u=u, gb=gb: e.matmul(ps[gb][:], lhsT=wg_b[:, k, u * 128:(u + 1) * 128], rhs=h2T[:, k, :],
                                                                    start=(k == 0), stop=(k == 7)),
                          reads=hkeys3 + [("wg_b", k // 2)], writes=[PS(gb)])
                    for k in range(8):
                        A("pe", lambda e, k=k, u=u, ub=ub: e.matmul(ps[ub][:], lhsT=wu_b[:, k, u * 128:(u + 1) * 128], rhs=h2T[:, k, :],
                                                                    start=(k == 0), stop=(k == 7)),
                          reads=hkeys3 + [("wu_b", k // 2)], writes=[PS(ub)])
                    sg_, sgk = sg3[u % 2], "sg3_%d" % (u % 2)
                    A("act", lambda e, sg_=sg_, gb=gb: e.activation(out=sg_[:], in_=ps[gb][:], func=AF.Silu), reads=[PS(gb)], writes=[sgk])
                    A("dve", lambda e, sg_=sg_, ub=ub, u=u: e.tensor_tensor(out=actT[:, u, :], in0=sg_[:], in1=ps[ub][:], op=ALU.mult),
                      reads=[PS(ub), sgk], writes=[("actT", u)])
                wds = []
                for u in range(NU):
                    i = wdc[0]
                    wdc[0] += 1
                    wd_, wdk = wdr[i % 6], "wdr%d" % (i % 6)
                    A("sp", lambda e, wd_=wd_, u=u: e.dma_start(out=wd_[:], in_=wsc_d[:, u, :]), writes=[wdk], dma_key=wdk)
                    for t in range(4):
                        for half in range(2):
                            A("pe", lambda e, t=t, half=half, u=u, wd_=wd_: e.matmul(
                                dwn_view(ps, t, half), lhsT=actT[:, u, t * 128:(t + 1) * 128], rhs=wd_[:, half * 512:(half + 1) * 512],
                                start=(u == 0), stop=(u == NU - 1)),
                              reads=[("actT", u), wdk], writes=[("ps", t * 2 + half)])
                for t in range(4):
                    xs, xk = xts[t]
                    tt = j * 4 + t
                    for half in range(2):
                        A("dve", lambda e, xs=xs, half=half, t=t: e.tensor_tensor(out=xs[:, half * 512:(half + 1) * 512],
                                                                                  in0=xs[:, half * 512:(half + 1) * 512],
                                                                                  in1=dwn_view(ps, t, half), op=ALU.add),
                          reads=[("ps", t * 2 + half), xk], writes=[xk])
                    A("pool", lambda e, xs=xs, tt=tt: e.dma_start(out=y[tt * 128:(tt + 1) * 128, :], in_=xs[:]), reads=[xk], dma_key=("yst", xk))
        S.emit_all(st)
    return nc


def dwn_view(ps, t, half):
    return ps[t * 2 + half][:]


def _consts():
    c = np.zeros((128, 5 * 128 + 512), np.float32)
    c[:, 0:128] = np.eye(128, dtype=np.float32)
    s_ = np.arange(128)[:, None]
    t_ = np.arange(128)[None, :]
    c[:, 128:256] = (s_ <= t_).astype(np.float32)
    c[:, 256:384] = (s_ >= t_).astype(np.float32)
    c[:, 384:512] = 1.0 / 128.0
    blk = np.zeros((128, 128), np.float32)
    blk[:64, :64] = 1.0 / 64.0
    blk[64:, 64:] = 1.0 / 64.0
    c[:, 512:640] = blk
    rm = np.ones(512, np.float32)
    rm[0::128] = 0.0
    c[:, 640:1152] = rm[None, :]
    return c


MASKV = -30000.0


def _na_table(rpb, row_base, rows):
    out = np.full((8, 128, NSLOT, 128), MASKV, np.float32)
    kr = np.arange(128) // 64
    kc = np.arange(128) % 64
    qr = np.arange(128) // 64
    qc = np.arange(128) % 64
    cs = np.clip(qc - 8, 0, 64 - 16)

    def fill(p, slot_base, keys):
        R = row_base + 2 * p + qr
        rs = np.clip(R - 4, 0, rows - 8)
        for c, ke in enumerate(keys):
            kt = ke - 2
            KR = row_base + 2 * kt + kr
            valid = ((KR[:, None] >= rs[None, :]) & (KR[:, None] < rs[None, :] + 8) &
                     (kc[:, None] >= cs[None, :]) & (kc[:, None] < cs[None, :] + 16))
            dr = np.clip(KR[:, None] - R[None, :] + 7, 0, 14)
            dc = np.clip(kc[:, None] - qc[None, :] + 15, 0, 30)
            vals = rpb[:, dr, dc]
            out[:, :, slot_base + c, :] = np.where(valid[None], vals, np.float32(MASKV))

    s0, keys = na_chunks(5)
    fill(5, s0, keys)
    for p in (0, 1, 30, 31):
        s0, keys = na_chunks(p)
        fill(p, s0, keys)
    return out.reshape(8, 128, NSLOT * 128)


def make_core_inputs(c, inp):
    f32 = np.float32
    w_in = np.asarray(inp["w_in"][0], f32)
    lbraw = np.asarray(inp["hgrn_lb"], f32)
    if c < 4:
        seq = np.asarray(inp["x_prompt"][c], f32)
        xloc = seq
        xhalo = np.zeros((512, D), f32)
        xoth = np.zeros((4096, D), f32)
        osel, alpha, beta = 0, 0.0, 0.0
        row_base, rows = 0, 64
    else:
        sidx, half = (c - 4) // 2, (c - 4) % 2
        seq = np.asarray(inp["x_sample"][sidx], f32)
        xhalo = np.zeros((512, D), f32)
        rows = 128
        if half == 0:
            xloc = seq[:4096]
            xhalo[256:512] = seq[4096:4352]
            xoth = seq[4096:][::-1]
            osel, alpha, beta = 1, 0.0, 1.0
            row_base = 0
        else:
            xloc = seq[4096:]
            xhalo[0:256] = seq[3840:4096]
            xoth = seq[:4096]
            osel, alpha, beta = 0, 1.0, 0.0
            row_base = 64
    fcol = 512 + osel * 512
    w_oth = np.concatenate([w_in[:, fcol:fcol + 512], w_in[:, 1536:2048]], axis=1)
    vecs = np.zeros((128, NV), f32)
    vecs[:, V_NMW:V_NMW + 8] = np.asarray(inp["norm_mix_w"][0], f32).reshape(8, 128).T
    vecs[:, V_NFW:V_NFW + 8] = np.asarray(inp["norm_ffn_w"][0], f32).reshape(8, 128).T
    lb4 = lbraw.reshape(2, 2, 4, 128)
    for sl in range(2):
        for d in range(2):
            for h in range(4):
                vecs[:, V_LB + sl * 8 + d * 4 + h] = lb4[sl, d, h]
        for h in range(4):
            vecs[:, V_LBO + sl * 4 + h] = lb4[sl, osel, h]
    vecs[:, V_GW] = np.asarray(inp["hgrn_gnorm_w"][0], f32)
    vecs[:, V_WQ] = np.tile(np.asarray(inp["na_q_norm_w"][0], f32), 2)
    vecs[:, V_WK] = np.tile(np.asarray(inp["na_k_norm_w"][0], f32), 2)
    vecs[:, V_AL] = alpha
    vecs[:, V_BE] = beta
    return {
        "xloc": np.ascontiguousarray(xloc), "xhalo": xhalo, "xoth": np.ascontiguousarray(xoth),
        "w_in": w_in, "w_oth": np.ascontiguousarray(w_oth),
        "w_out": np.asarray(inp["w_out"][0], f32), "w_gate": np.asarray(inp["w_gate"][0], f32),
        "w_up": np.asarray(inp["w_up"][0], f32), "w_down": np.asarray(inp["w_down"][0], f32),
        "vecs": vecs, "nab": _na_table(np.asarray(inp["na_rpb"][0], f32), row_base, rows),
        "consts": _consts(),
    }


_NC_CACHE = {}


def kernel(**inputs):
    key = "full"
    if key not in _NC_CACHE:
        _NC_CACHE[key] = build()
    nc = _NC_CACHE[key]
    in_maps = [make_core_inputs(c, inputs) for c in range(8)]
    res = run_bass_kernel_spmd(nc, in_maps, core_ids=list(range(8)))
    ys = [np.asarray(r["y"], np.float32) for r in res.results]
    y_prompt = np.stack(ys[0:4], axis=0)
    y_sample = np.stack([np.concatenate([ys[4], ys[5]], axis=0), np.concatenate([ys[6], ys[7]], axis=0)], axis=0)
    return (y_prompt, y_sample)
```

```python
import numpy as np
from contextlib import ExitStack
import concourse.bass as bass
import concourse.mybir as mybir
from concourse.bass_utils import run_bass_kernel_spmd

F32 = mybir.dt.float32
BF16 = mybir.dt.bfloat16
AF = mybir.ActivationFunctionType
ALU = mybir.AluOpType

ENGS = ("pe", "act", "dve", "pool", "sp")
EPS = 1e-6
NT = 32
NB = 8
D = 1024
DFF = 2816
NU = 22


class _Op:
    __slots__ = ("eng", "emit", "deps", "odeps", "signal", "count", "is_dma", "sem", "semval", "name",
                 "dur", "idx", "fin", "sched")


class _Rec:
    def __init__(self):
        self.call = None

    def __getattr__(self, name):
        def f(*a, **kw):
            self.call = (name, a, kw)
            return None
        return f


def _fsize(ap):
    try:
        return int(ap.free_size())
    except Exception:
        return 512


def _est_ns(eng, call, is_dma):
    name, a, kw = call
    if is_dma:
        try:
            nb = int(kw["out"].nbytes())
        except Exception:
            nb = 65536
        return 2000.0 + nb / 150.0
    if eng == "pe":
        if name == "transpose":
            return 64.0
        rhs = kw.get("rhs")
        n = _fsize(rhs) if rhs is not None else 512
        f = 4.0 if (rhs is not None and rhs.dtype == F32) else 1.0
        return f * max(n, 64) * 0.42 + 12.0
    out = kw.get("out")
    n = _fsize(out) if out is not None else (_fsize(a[0]) if a else 512)
    if eng == "act":
        return 220.0 + n * 0.72
    if eng == "dve":
        if name in ("tensor_tensor_scan",):
            return 120.0 + 2.1 * n
        if name in ("tensor_tensor", "scalar_tensor_tensor"):
            return 120.0 + 1.3 * n
        return 120.0 + 1.05 * n
    if eng == "pool":
        return 250.0 + 2.1 * n
    return 100.0


class Sched:
    def __init__(self, nc, reorder=True, window=400):
        self.nc = nc
        self.ops = []
        self.last_w = {}
        self.readers = {}
        self.dma_cnt = {}
        self.fence = {}
        self.reorder = reorder
        self.window = window

    def _mk(self, eng, emit, name=None):
        op = _Op()
        op.eng = eng
        op.emit = emit
        op.signal = False
        op.count = 0
        op.is_dma = False
        op.sem = None
        op.semval = 0
        op.name = name
        op.deps = []
        op.odeps = []
        op.dur = 0.0
        op.idx = len(self.ops)
        op.fin = None
        op.sched = False
        return op

    def add(self, eng, emit, reads=(), writes=(), dma_key=None, name=None):
        rec = _Rec()
        emit(rec)
        call = rec.call
        assert call is not None
        op = self._mk(eng, (lambda h, call=call: getattr(h, call[0])(*call[1], **call[2])), name)
        deps = []
        for r in reads:
            w = self.last_w.get(r)
            if w is not None:
                deps.append(w)
        for w_ in writes:
            w = self.last_w.get(w_)
            if w is not None:
                deps.append(w)
            deps.extend(self.readers.get(w_, ()))
        if dma_key is not None:
            op.is_dma = True
            k = self.dma_cnt.get(dma_key, 0) + 1
            self.dma_cnt[dma_key] = k
            op.sem = dma_key
            op.semval = 16 * k
        op.dur = _est_ns(eng, call, op.is_dma)
        seen = set()
        for d in deps:
            if id(d) in seen or d is op:
                continue
            seen.add(id(d))
            if (not d.is_dma) and (not op.is_dma) and d.eng == eng and eng == "pe":
                op.odeps.append(d)
                continue
            op.deps.append(d)
        f = self.fence.get(eng)
        if f is not None:
            op.odeps.append(f)
        for d in op.deps:
            if not d.is_dma:
                d.signal = True
        for r in reads:
            self.readers.setdefault(r, []).append(op)
        for w_ in writes:
            self.last_w[w_] = op
            self.readers[w_] = []
        self.ops.append(op)
        return op

    def barrier(self):
        last = {}
        lastd = {}
        for op in self.ops:
            if op.is_dma:
                lastd[op.sem] = op
            elif op.emit is not None:
                last[op.eng] = op
        deps = list(last.values()) + list(lastd.values())
        allprev = list(self.ops)
        for e in ENGS:
            op = self._mk(e, None, "barrier")
            op.deps = list(deps)
            op.odeps = [o for o in allprev if o.eng == e and o.emit is not None][-1:]
            op.name = "barrier"
            for d in op.deps:
                if not d.is_dma:
                    d.signal = True
            self.ops.append(op)
            self.fence[e] = op
        self.last_w = {}
        self.readers = {}

    def _schedule(self):
        per = {e: [o for o in self.ops if o.eng == e] for e in ENGS}
        if not self.reorder:
            return per
        tail = {}
        for op in self.ops:
            tail[id(op)] = op.dur
        for op in reversed(self.ops):
            tl = tail[id(op)]
            for d in op.deps:
                v = d.dur + tl
                if v > tail[id(d)]:
                    tail[id(d)] = v
            for d in op.odeps:
                v = d.dur + tl
                if v > tail[id(d)]:
                    tail[id(d)] = v
        pending = {e: list(per[e]) for e in ENGS}
        out = {e: [] for e in ENGS}
        t = {e: 0.0 for e in ENGS}
        dma_free = [0.0]
        remaining = sum(len(v) for v in pending.values())
        W = self.window
        while remaining:
            best = None
            for e in ENGS:
                lst = pending[e]
                if not lst:
                    continue
                cand = None
                nready = 0
                lim = 0
                for op in lst:
                    lim += 1
                    if lim > W:
                        break
                    if op.name == "barrier" and op is not lst[0]:
                        break
                    ok = True
                    rdy = t[e]
                    for d in op.deps:
                        if not d.sched:
                            ok = False
                            break
                        if d.fin > rdy:
                            rdy = d.fin
                    if ok:
                        for d in op.odeps:
                            if not d.sched:
                                ok = False
                                break
                    if not ok:
                        if op.name == "barrier":
                            break
                        continue
                    if rdy <= t[e]:
                        tl = tail[id(op)]
                        if nready == 0 or tl > cand[2]:
                            cand = (rdy, op, tl)
                        nready += 1
                        if nready >= 400:
                            break
                    elif nready == 0 and (cand is None or rdy < cand[0]):
                        cand = (rdy, op, 0.0)
                if cand is not None and (best is None or cand[0] < best[0]):
                    best = (cand[0], e, cand[1])
            assert best is not None, "scheduler stuck"
            start, e, op = best
            pending[e].remove(op)
            op.sched = True
            if op.is_dma:
                issue = start + 60.0
                xfer = op.dur - 2000.0
                st_ = max(issue, dma_free[0])
                dma_free[0] = st_ + xfer
                op.fin = st_ + xfer + 2000.0
                t[e] = issue
            else:
                op.fin = start + op.dur
                t[e] = op.fin
            out[e].append(op)
            remaining -= 1
        self.sim_ns = max(t.values())
        return out

    def emit_all(self, stack):
        nc = self.nc
        lastd = {}
        for op in self.ops:
            if op.is_dma:
                lastd[op.sem] = op
        fin = self._mk("sp", None, "final")
        fin.name = "barrier"
        fin.deps = list(lastd.values())
        fin.odeps = [o for o in self.ops if o.eng == "sp"][-1:]
        self.ops.append(fin)

        per = self._schedule()
        eng_sem = {e: stack.enter_context(nc.semaphore("sem_" + e)) for e in ENGS}
        dma_sem = {}
        for i, k in enumerate(self.dma_cnt):
            dma_sem[k] = stack.enter_context(nc.semaphore("dsem%d" % i))
        for e in ENGS:
            c = 0
            for op in per[e]:
                if op.signal:
                    c += 1
                    op.count = c
        self.n_per = {e: len(per[e]) for e in ENGS}

        def run(e, handle):
            known = {}
            for op in per[e]:
                need = {}
                for d in op.deps:
                    if d.is_dma:
                        s, v = dma_sem[d.sem], d.semval
                    else:
                        s, v = eng_sem[d.eng], d.count
                    key = id(s)
                    if key not in need or need[key][1] < v:
                        need[key] = (s, v)
                for key, (s, v) in need.items():
                    if known.get(key, 0) >= v:
                        continue
                    handle.wait_ge(s, v)
                    known[key] = v
                if op.emit is None:
                    continue
                ins = op.emit(handle)
                if op.is_dma:
                    ins.then_inc(dma_sem[op.sem], 16)
                elif op.signal:
                    ins.then_inc(eng_sem[e], 1)

        block = stack.enter_context(nc.Block())

        @block.sync
        def _(h):
            run("sp", h)

        @block.scalar
        def _(h):
            run("act", h)

        @block.vector
        def _(h):
            run("dve", h)

        @block.gpsimd
        def _(h):
            run("pool", h)

        @block.tensor
        def _(h):
            run("pe", h)


V_NMW = 0
V_NFW = 8
V_LB = 16
V_LBO = 32
V_GW = 40
V_WQ = 41
V_WK = 42
V_AL = 43
V_BE = 44
NV = 48

NSLOT = 27
SLOT0 = {"gen": 0, 0: 5, 1: 11, 30: 16, 31: 21}


def na_chunks(p):
    keys = [p + c for c in range(5)]
    if p == 0:
        return SLOT0[0], keys + [3 + 2]
    if p == 31:
        return SLOT0[31], keys + [28 + 2]
    if p in (1, 30):
        return SLOT0[p], keys
    return SLOT0["gen"], keys


def build(do_hgrn=True, do_na=True, do_pre=True, dbg=()):
    nc = bass.Bass("TRN2", target_bir_lowering=False)
    dt_in = lambda n, s: nc.dram_tensor(n, s, F32, kind="ExternalInput").ap()
    xloc = dt_in("xloc", [4096, D])
    xhalo = dt_in("xhalo", [512, D])
    xoth = dt_in("xoth", [4096, D])
    w_in = dt_in("w_in", [D, 4096])
    w_oth = dt_in("w_oth", [D, 1024])
    w_out = dt_in("w_out", [D, D])
    w_gate = dt_in("w_gate", [D, DFF])
    w_up = dt_in("w_up", [D, DFF])
    w_down = dt_in("w_down", [DFF, D])
    vecs_d = dt_in("vecs", [128, NV])
    nab = dt_in("nab", [8, 128, NSLOT * 128])
    consts = dt_in("consts", [128, 5 * 128 + 512])
    y = nc.dram_tensor("y", [4096, D], F32, kind="ExternalOutput").ap()
    catT = nc.dram_tensor("catT", [8, 128, 4096], BF16, kind="Internal").ap()
    wsc_o = nc.dram_tensor("wsc_o", [128, 8, D], BF16, kind="Internal").ap()
    wsc_g = nc.dram_tensor("wsc_g", [128, 8, DFF], BF16, kind="Internal").ap()
    wsc_u = nc.dram_tensor("wsc_u", [128, 8, DFF], BF16, kind="Internal").ap()
    wsc_d = nc.dram_tensor("wsc_d", [128, NU, D], BF16, kind="Internal").ap()
    dbg_out = {}
    for name, shape in dbg:
        dbg_out[name] = nc.dram_tensor("dbg_" + name, list(shape), BF16 if name == "catT" else F32, kind="ExternalOutput").ap()

    w_in_v = w_in.rearrange("(k p) c -> p k c", p=128)
    w_oth_v = w_oth.rearrange("(k p) c -> p k c", p=128)
    w_out_v = w_out.rearrange("(k p) c -> p k c", p=128)
    w_gate_v = w_gate.rearrange("(k p) c -> p k c", p=128)
    w_up_v = w_up.rearrange("(k p) c -> p k c", p=128)
    w_down_v = w_down.rearrange("(u p) c -> p u c", p=128)

    with ExitStack() as st:
        S = Sched(nc)
        A = S.add

        _nm = [0]

        def sbuf(stack, name, shape, dt):
            _nm[0] += 1
            return stack.enter_context(nc.sbuf_tensor("sb%d_%s" % (_nm[0], name), shape, dt))

        ps = [st.enter_context(nc.psum_tensor("ps%d" % i, [128, 512], F32)) for i in range(8)]
        psb = [p[:].bitcast(BF16) for p in ps]
        PS = lambda i: ("ps", i)

        cf = sbuf(st, "cf", [128, 5 * 128 + 512], F32)
        vecs = sbuf(st, "vecs", [128, NV], F32)
        identb = sbuf(st, "identb", [128, 128], BF16)
        maskf = sbuf(st, "maskf", [128, 128], BF16)
        maskb = sbuf(st, "maskb", [128, 128], BF16)
        lbv = sbuf(st, "lbv", [128, 12], F32)
        oml = sbuf(st, "oml", [128, 12], F32)
        noml = sbuf(st, "noml", [128, 12], F32)
        lbd = sbuf(st, "lbd", [128, 12], F32)
        wqk = sbuf(st, "wqk", [128, 2], F32)
        Spre = sbuf(st, "Spre", [128, 4, 128], F32)
        stats = sbuf(st, "stats", [128, 104, 4], F32)
        onesd = cf[:, 384:512]
        blk64 = cf[:, 512:640]
        resetm = cf[:, 640:1152]

        A("sp", lambda e: e.dma_start(out=cf[:], in_=consts), writes=["cf"], dma_key="cf")
        A("sp", lambda e: e.dma_start(out=vecs[:], in_=vecs_d), writes=["vecs"], dma_key="vecs")
        A("dve", lambda e: e.tensor_copy(out=identb[:], in_=cf[:, 0:128]), reads=["cf"], writes=["identb"])
        A("dve", lambda e: e.tensor_copy(out=maskf[:], in_=cf[:, 128:256]), reads=["cf"], writes=["maskf"])
        A("dve", lambda e: e.tensor_copy(out=maskb[:], in_=cf[:, 256:384]), reads=["cf"], writes=["maskb"])
        A("dve", lambda e: e.tensor_tensor(out=lbd[:, 0:8], in0=vecs[:, V_LB:V_LB + 8], in1=vecs[:, V_LB + 8:V_LB + 16],
                                           op=ALU.subtract), reads=["vecs"], writes=["lbd"])
        A("dve", lambda e: e.tensor_tensor(out=lbd[:, 8:12], in0=vecs[:, V_LBO:V_LBO + 4], in1=vecs[:, V_LBO + 4:V_LBO + 8],
                                           op=ALU.subtract), reads=["vecs", "lbd"], writes=["lbd"])
        A("act", lambda e: e.activation(out=lbv[:], in_=lbd[:], func=AF.Sigmoid), reads=["lbd"], writes=["lbv"])
        A("act", lambda e: e.activation(out=oml[:], in_=lbd[:], func=AF.Sigmoid, scale=-1.0), reads=["lbd"], writes=["oml"])
        A("dve", lambda e: e.tensor_scalar(out=noml[:], in0=oml[:], scalar1=-1.0, scalar2=None, op0=ALU.mult),
          reads=["oml"], writes=["noml"])
        A("dve", lambda e: e.tensor_scalar(out=wqk[:, 0:1], in0=vecs[:, V_WQ:V_WQ + 1], scalar1=0.125, scalar2=None,
                                           op0=ALU.mult), reads=["vecs"], writes=["wqk"])
        A("dve", lambda e: e.tensor_copy(out=wqk[:, 1:2], in_=vecs[:, V_WK:V_WK + 1]), reads=["vecs", "wqk"], writes=["wqk"])
        A("pool", lambda e: e.memset(Spre[:], 0.0), writes=["Spre"])
        A("pool", lambda e: e.memset(stats[:, 103, 0:1], EPS), writes=["epsc"])
        A("pool", lambda e: e.memset(stats[:, 103, 1:2], 1.0), reads=["epsc"], writes=["epsc"])

        def dump(name, ap, key):
            if name in dbg_out:
                A("sp", lambda e: e.dma_start(out=dbg_out[name], in_=ap), reads=[key], dma_key="dbg_" + name)

        nt_ctr = [0]

        def norm_tile(src, dst3, dst_key, wcol, xt, xb, pbank, src_key=None, x_keep=None, scale_eng="act"):
            i = nt_ctr[0]
            nt_ctr[0] += 1
            sc = i % 103
            if src is not None:
                sl = i % len(xt)
                xs, xk = xt[sl], "xt%d" % sl
                A("sp", lambda e: e.dma_start(out=xs[:], in_=src), writes=[xk], dma_key=xk)
            else:
                xs, xk = x_keep, src_key
            bs, bk = xb[i % len(xb)], "xb%d" % (i % len(xb))
            stc = ("st", sc)
            A("act", lambda e: e.activation(out=bs[:], in_=xs[:], func=AF.Square, accum_out=stats[:, sc, 0:1]),
              reads=[xk], writes=[bk, stc])
            A("act", lambda e: e.activation(out=stats[:, sc, 2:3], in_=stats[:, sc, 0:1], func=AF.Ln, scale=1.0 / D, bias=stats[:, 103, 0:1]),
              reads=[stc, "epsc"], writes=[stc])
            A("act", lambda e: e.activation(out=stats[:, sc, 3:4], in_=stats[:, sc, 2:3], func=AF.Exp, scale=-0.5),
              reads=[stc], writes=[stc])
            if scale_eng == "dve":
                A("dve", lambda e: e.tensor_scalar(out=bs[:], in0=xs[:], scalar1=stats[:, sc, 3:4], scalar2=None, op0=ALU.mult),
                  reads=[xk, stc], writes=[bk])
            else:
                A("act", lambda e: e.activation(out=bs[:], in_=xs[:], func=AF.Copy, scale=stats[:, sc, 3:4]),
                  reads=[xk, stc], writes=[bk])
            for k in range(8):
                A("pe", lambda e, k=k: e.transpose(out=psb[pbank][:, k * 128:(k + 1) * 128], in_=bs[:, k * 128:(k + 1) * 128],
                                                    identity=identb[:]),
                  reads=[bk, "identb"], writes=[PS(pbank)])
            A("dve", lambda e: e.tensor_tensor(out=dst3, in0=psb[pbank][:, 0:1024].rearrange("p (k t) -> p k t", k=8),
                                               in1=vecs[:, wcol:wcol + 8].unsqueeze(2).to_broadcast([128, 8, 128]),
                                               op=ALU.mult),
              reads=[PS(pbank), "vecs"], writes=[dst_key])

        def load_cast(dst, dst_key, src, stage, nstage, ctr, eng="pool", shape=None):
            i = ctr[0]
            ctr[0] += 1
            sg, sk = stage[i % nstage], "wst%d" % (i % nstage)
            sv = sg[:] if shape is None else shape(sg)
            A("sp", lambda e: e.dma_start(out=sv, in_=src), writes=[sk], dma_key=sk)
            if eng == "act":
                A("act", lambda e: e.activation(out=dst, in_=sv, func=AF.Copy), reads=[sk], writes=[dst_key])
            else:
                A(eng, lambda e: e.tensor_copy(out=dst, in_=sv), reads=[sk], writes=[dst_key])

        with ExitStack() as p12:
            hT = sbuf(p12, "hT", [128, 8, 36 * 128], BF16)
            wctr = [0]
            wst = [sbuf(p12, "wst%d" % i, [128, 8, 128], F32) for i in range(1)]
            p1x = p12.enter_context(ExitStack())
            xt = [sbuf(p1x, "xt%d" % i, [128, D], F32) for i in range(3)]
            xb = [sbuf(p1x, "xb%d" % i, [128, D], BF16) for i in range(3)]
            for e_ in range(36):
                if e_ < 2:
                    src = xhalo[e_ * 128:(e_ + 1) * 128, :]
                elif e_ >= 34:
                    src = xhalo[(e_ - 32) * 128:(e_ - 31) * 128, :]
                else:
                    src = xloc[(e_ - 2) * 128:(e_ - 1) * 128, :]
                norm_tile(src, hT[:, :, e_ * 128:(e_ + 1) * 128], ("hT", e_), V_NMW, xt, xb, e_ % 4, scale_eng="dve")
            if "hT" in dbg_out:
                hTf = sbuf(p1x, "hTf", [128, 8, 512], F32)
                A("dve", lambda e: e.tensor_copy(out=hTf[:], in_=hT[:, :, 256:768]), reads=[("hT", 2), ("hT", 3), ("hT", 4), ("hT", 5)],
                  writes=["hTf"])
                dump("hT", hTf[:], "hTf")


            def gate_batch(G, zbank, lcol, bwd, qsrc, qkeys, tag, ttag=None):
                s1, s2, g, b, bp = G["s1"], G["s2"], G["g"], G["b"], G["bp"]
                k32 = b
                eb = s1
                ttag = ttag or tag
                _al = {"eb": "s1", "k32": "b"}
                _tmp = ("s1", "s2", "g", "b", "bp", "bb")
                K = lambda n: ((ttag if _al.get(n, n) in _tmp else tag), _al.get(n, n))
                A("act", lambda e: e.activation(out=s1[:], in_=ps[zbank][:], func=AF.Exp, scale=-1.0), reads=[PS(zbank)], writes=[K("s1")])
                A("act", lambda e: e.activation(out=s2[:], in_=s1[:], func=AF.Ln, bias=stats[:, 103, 1:2]), reads=[K("s1"), "epsc"], writes=[K("s2")])
                A("act", lambda e: e.activation(out=s1[:], in_=s2[:], func=AF.Exp, scale=-1.0), reads=[K("s2")], writes=[K("s1")])
                A("act", lambda e: e.activation(out=g[:], in_=s1[:], func=AF.Ln, scale=oml[:, lcol:lcol + 1],
                                                bias=lbv[:, lcol:lcol + 1]),
                  reads=[K("s1"), "oml", "lbv"], writes=[K("g")])
                A("pool", lambda e: e.tensor_scalar(out=s2[:], in0=s1[:], scalar1=noml[:, lcol:lcol + 1], scalar2=oml[:, lcol:lcol + 1],
                                                    op0=ALU.mult, op1=ALU.add),
                  reads=[K("s1"), "oml", "noml"], writes=[K("s2")])
                A("dve", lambda e: e.tensor_tensor_scan(out=b[:], data0=resetm, data1=g[:], initial=0.0,
                                                        op0=ALU.mult, op1=ALU.add),
                  reads=[K("g"), "cf"], writes=[K("b")])
                b3 = b[:].rearrange("p (t s) -> p t s", t=4)
                g3 = g[:].rearrange("p (t s) -> p t s", t=4)
                bp3 = bp[:].rearrange("p (t s) -> p t s", t=4)
                dec = G["dec"]
                if not bwd:
                    A("dve", lambda e: e.tensor_tensor(out=bp3, in0=b3, in1=b3[:, :, 63:64].to_broadcast([128, 4, 128]),
                                                       op=ALU.subtract), reads=[K("b")], writes=[K("bp")])
                    A("act", lambda e: e.activation(out=dec[:, :, 0:2], in_=b3[:, :, 63:128:64], func=AF.Exp),
                      reads=[K("b")], writes=[K("dec0"), K("dec1")])
                    G["ci"] = (1, 0)
                    A("act", lambda e: e.activation(out=dec[:, :, 2:3], in_=bp3[:, :, 127:128], func=AF.Exp),
                      reads=[K("bp")], writes=[K("dec2")])
                else:
                    bb3 = G["bb"][:].rearrange("p (t s) -> p t s", t=4)
                    A("dve", lambda e: e.tensor_tensor(out=bb3, in0=g3, in1=b3, op=ALU.subtract),
                      reads=[K("g"), K("b")], writes=[K("bb")])
                    A("dve", lambda e: e.tensor_tensor(out=bp3, in0=bb3, in1=bb3[:, :, 64:65].to_broadcast([128, 4, 128]),
                                                       op=ALU.subtract), reads=[K("bb")], writes=[K("bp")])
                    A("dve", lambda e: e.tensor_tensor(out=dec[:, :, 0:2], in0=bb3[:, :, 0:128:64],
                                                       in1=b3[:, :, 127:128].to_broadcast([128, 4, 2]), op=ALU.add),
                      reads=[K("bb"), K("b")], writes=[K("dec0"), K("dec1")])
                    A("act", lambda e: e.activation(out=dec[:, :, 0:2], in_=dec[:, :, 0:2], func=AF.Exp),
                      reads=[K("dec0"), K("dec1")], writes=[K("dec0"), K("dec1")])
                    G["ci"] = (0, 1)
                    A("act", lambda e: e.activation(out=dec[:, :, 2:3], in_=bp3[:, :, 0:1], func=AF.Exp),
                      reads=[K("bp")], writes=[K("dec2")])
                A("act", lambda e: e.activation(out=eb[:], in_=bp[:], func=AF.Exp, scale=-1.0), reads=[K("bp")], writes=[K("eb")])
                A("dve", lambda e: e.tensor_tensor(out=k32[:], in0=s2[:], in1=eb[:], op=ALU.mult),
                  reads=[K("s2"), K("eb")], writes=[K("k32")])
                if G.get("kt") is not None:
                    A("pool", lambda e: e.tensor_copy(out=G["kt"][:], in_=k32[:]), reads=[K("k32")], writes=[K("kt")])
                A("pool", lambda e: e.tensor_tensor(out=G["khT"][:].rearrange("p (t s) -> p t s", t=4),
                                                   in0=k32[:].rearrange("p (t s) -> p t s", t=4),
                                                   in1=dec[:, :, 2:3].to_broadcast([128, 4, 128]), op=ALU.mult),
                  reads=[K("k32"), K("dec2")], writes=[K("khT")])
                if qsrc is not None:
                    A("act", lambda e: e.activation(out=eb[:], in_=bp[:], func=AF.Exp), reads=[K("bp"), K("k32")], writes=[K("eb")])
                    A("pool", lambda e: e.tensor_tensor(out=G["qt"][:], in0=qsrc, in1=eb[:], op=ALU.mult),
                      reads=[K("eb")] + list(qkeys), writes=[K("qt")])

            def alloc_gate_tmp(stack, tag, bwd):
                T = {}
                for n in ["s1", "s2", "g", "b", "bp"] + (["bb"] if bwd else []):
                    T[n] = sbuf(stack, "%s_%s" % (tag, n), [128, 512], F32)
                return T

            def alloc_gate(stack, tag, with_q, bwd=False, tmp=True):
                G = {}
                if tmp:
                    G.update(alloc_gate_tmp(stack, tag, bwd))
                G["khT"] = sbuf(stack, tag + "_khT", [128, 512], BF16)
                G["kh"] = sbuf(stack, tag + "_kh", [128, 4, 128], BF16)
                G["dec"] = sbuf(stack, tag + "_dec", [128, 4, 3], F32)
                if with_q:
                    G["kt"] = sbuf(stack, tag + "_kt", [128, 512], BF16)
                    G["kt2"] = sbuf(stack, tag + "_kt2", [128, 512], BF16)
                    G["qt"] = sbuf(stack, tag + "_qt", [128, 512], BF16)
                    G["am"] = sbuf(stack, tag + "_am", [128, 4, 128], BF16)
                    G["sin"] = sbuf(stack, tag + "_sin", [128, 4, 128], BF16)
                return G

            def khat_transpose(G, tag, pbank):
                for t in range(4):
                    A("pe", lambda e, t=t: e.transpose(out=psb[pbank][:, t * 128:(t + 1) * 128],
                                                       in_=G["khT"][:, t * 128:(t + 1) * 128], identity=identb[:]),
                      reads=[(tag, "khT"), "identb"], writes=[PS(pbank)])
                A("act", lambda e: e.activation(out=G["kh"][:].rearrange("p t d -> p (t d)"), in_=psb[pbank][:, 0:512], func=AF.Copy),
                  reads=[PS(pbank)], writes=[(tag, "kh")])

            if do_hgrn and do_pre:
                with ExitStack() as pp:
                    wo = sbuf(pp, "wo", [128, 8, 1024], BF16)
                    for c in range(8):
                        load_cast(wo[:, :, c * 128:(c + 1) * 128], ("wo", c), w_oth_v[:, :, c * 128:(c + 1) * 128], wst, 1, wctr)
                    hTo = [sbuf(pp, "hTo%d" % i, [128, 8, 512], BF16) for i in range(2)]
                    Gps = [alloc_gate(pp, "gp%d" % i, False, bwd=False, tmp=True) for i in range(2)]
                    vtos = [sbuf(pp, "vto%d" % i, [128, 4, 128], BF16) for i in range(2)]
                    for jb in range(NB):
                        ho = hTo[jb % 2]
                        hk = "hTo%d" % (jb % 2)
                        for t in range(4):
                            r0 = (jb * 4 + t) * 128
                            norm_tile(xoth[r0:r0 + 128, :], ho[:, :, t * 128:(t + 1) * 128], hk, V_NMW, xt, xb, t % 2)
                        for h in range(4):
                            par = h % 2
                            Gp, gtag = Gps[par], "gp%d" % par
                            vto, vk = vtos[par], "vto%d" % par
                            zbk, vbk = 2 + 4 * par, 3 + 4 * par
                            for k in range(8):
                                A("pe", lambda e, k=k, h=h: e.matmul(ps[zbk][:], lhsT=wo[:, k, h * 128:(h + 1) * 128], rhs=ho[:, k, :],
                                                                     start=(k == 0), stop=(k == 7)),
                                  reads=[hk, ("wo", h)], writes=[PS(zbk)])
                            for t in range(4):
                                for k in range(8):
                                    A("pe", lambda e, k=k, h=h, t=t: e.matmul(ps[vbk][:, t * 128:(t + 1) * 128],
                                                                              lhsT=ho[:, k, t * 128:(t + 1) * 128],
                                                                              rhs=wo[:, k, 512 + h * 128:512 + (h + 1) * 128],
                                                                              start=(k == 0), stop=(k == 7)),
                                      reads=[hk, ("wo", 4 + h)], writes=[PS(vbk)])
                            A("act", lambda e: e.activation(out=vto[:].rearrange("p t d -> p (t d)"), in_=ps[vbk][:], func=AF.Copy),
                              reads=[PS(vbk)], writes=[vk])
                            gate_batch(Gp, zbk, 8 + h, False, None, None, gtag)
                            khat_transpose(Gp, gtag, 4)
                            for t in range(4):
                                A("pe", lambda e, t=t: e.matmul(ps[5][:, t * 128:(t + 1) * 128], lhsT=Gp["kh"][:, t, :], rhs=vto[:, t, :],
                                                                start=True, stop=True),
                                  reads=[(gtag, "kh"), vk], writes=[PS(5)])
                            for t in range(4):
                                A("dve", lambda e, t=t, h=h: e.scalar_tensor_tensor(out=Spre[:, h, :], in0=Spre[:, h, :],
                                                                                    scalar=Gp["dec"][:, t, Gp["ci"][0]:Gp["ci"][0] + 1],
                                                                                    in1=ps[5][:, t * 128:(t + 1) * 128],
                                                                                    op0=ALU.mult, op1=ALU.add),
                                  reads=[PS(5), (gtag, "dec0"), ("Spre", h), "Spre"], writes=[("Spre", h)])
                S.barrier()
            dump("Spre", Spre[:].rearrange("p h d -> p (h d)"), "Spre")
            S.barrier()
            p1x.close()

            if do_hgrn:
                with ExitStack() as ph:
                    whd = [sbuf(ph, "whd%d" % i, [128, 8, 5, 128], BF16) for i in range(2)]
                    qsT = sbuf(ph, "qsT", [128, 4096], BF16)
                    vtm = sbuf(ph, "vtm", [128, NT, 128], BF16)
                    oacc = sbuf(ph, "oacc", [128, 4096], F32)
                    Sst = [sbuf(ph, "Sst%d" % i, [128, 128], F32) for i in range(2)]
                    Gsh = [alloc_gate(ph, "gf", True, tmp=False), alloc_gate(ph, "gb", True, tmp=False)]
                    Gtm = [[alloc_gate_tmp(ph, "gf%d" % i, False) for i in range(2)],
                           [alloc_gate_tmp(ph, "gb%d" % i, True) for i in range(2)]]
                    sgate = sbuf(ph, "sgate", [128, 512], F32)
                    sq = sbuf(ph, "sq", [128, 512], F32)
                    rs = sbuf(ph, "rs", [128, 512], F32)
                    cblk = [sbuf(ph, "cblk%d" % i, [128, 512], BF16) for i in range(2)]
                    colbase = [0, 512, 1024, 1536, 2048]

                    def load_head_w(h):
                        wb = whd[h % 2]
                        for ci, cb in enumerate(colbase):
                            load_cast(wb[:, :, ci, :], ("whd", h % 2, ci), w_in_v[:, :, cb + h * 128:cb + (h + 1) * 128], wst, 1, wctr)

                    def hcols(j):
                        return slice((2 + 4 * j) * 128, (2 + 4 * j + 4) * 128)

                    def hkeys(j):
                        return [("hT", 2 + 4 * j + t) for t in range(4)]

                    def proj_fm(bank, wb, wkey, ci, j):
                        for k in range(8):
                            A("pe", lambda e, k=k: e.matmul(ps[bank][:], lhsT=wb[:, k, ci, :], rhs=hT[:, k, hcols(j)],
                                                            start=(k == 0), stop=(k == 7)),
                              reads=hkeys(j) + [wkey], writes=[PS(bank)])

                    A("dve", lambda e: e.memset(ps[6][:], 0.0), writes=[PS(6)])
                    A("pool", lambda e: e.memset(Gsh[0]["kt2"][:], 0.0), writes=["kt2z", ("gf", "kt2")])
                    A("pool", lambda e: e.memset(Gsh[1]["kt2"][:], 0.0), reads=["kt2z"], writes=["kt2z", ("gb", "kt2")])
                    load_head_w(0)
                    for h in range(4):
                        wb = whd[h % 2]
                        WK = lambda ci: ("whd", h % 2, ci)
                        if h + 1 < 4:
                            load_head_w(h + 1)
                        for j in range(NB):
                            proj_fm(0 + j % 2, wb, WK(0), 0, j)
                            ta_, tk = ((sq, "sq"), (rs, "rs"))[j % 2]
                            A("act", lambda e, j=j, ta_=ta_: e.activation(out=ta_[:], in_=ps[j % 2][:], func=AF.Exp, scale=-1.0),
                              reads=[PS(j % 2)], writes=[tk])
                            A("act", lambda e, ta_=ta_: e.activation(out=ta_[:], in_=ta_[:], func=AF.Ln, bias=stats[:, 103, 1:2]),
                              reads=[tk, "epsc"], writes=[tk])
                            A("act", lambda e, ta_=ta_: e.activation(out=ta_[:], in_=ta_[:], func=AF.Exp, scale=-1.0), reads=[tk], writes=[tk])
                            A("dve", lambda e, j=j, ta_=ta_: e.tensor_tensor(out=qsT[:, j * 512:(j + 1) * 512], in0=ps[j % 2][:], in1=ta_[:],
                                                                            op=ALU.mult),
                              reads=[PS(j % 2), tk], writes=[("qsT", j)])
                            bank = 2 + j % 2
                            for t in range(4):
                                et = 2 + 4 * j + t
                                for k in range(8):
                                    A("pe", lambda e, k=k, t=t, et=et, bank=bank: e.matmul(
                                        ps[bank][:, t * 128:(t + 1) * 128], lhsT=hT[:, k, et * 128:(et + 1) * 128],
                                        rhs=wb[:, k, 3, :], start=(k == 0), stop=(k == 7)),
                                      reads=[("hT", et), WK(3)], writes=[PS(bank)])
                            A("dve", lambda e, j=j, bank=bank: e.tensor_copy(out=vtm[:, 4 * j:4 * j + 4, :].rearrange("p t d -> p (t d)"),
                                                                            in_=ps[bank][:]),
                              reads=[PS(bank)], writes=[("vtm", j)])
                        A("dve", lambda e, h=h: e.tensor_scalar(out=Sst[0][:], in0=Spre[:, h, :], scalar1=vecs[:, V_AL:V_AL + 1],
                                                                scalar2=None, op0=ALU.mult),
                          reads=["Spre", "vecs"], writes=["Sst0"])
                        A("dve", lambda e, h=h: e.tensor_scalar(out=Sst[1][:], in0=Spre[:, h, :], scalar1=vecs[:, V_BE:V_BE + 1],
                                                                scalar2=None, op0=ALU.mult),
                          reads=["Spre", "vecs"], writes=["Sst1"])
                        for step in range(NB):
                            for d in range(2):
                                j = step if d == 0 else NB - 1 - step
                                G = dict(Gsh[d])
                                G.update(Gtm[d][step % 2])
                                tag = "gf" if d == 0 else "gb"
                                ttag = tag + str(step % 2)
                                zb = 4 + d
                                proj_fm(zb, wb, WK(1 + d), 1 + d, j)
                                gate_batch(G, zb, d * 4 + h, d == 1, qsT[:, j * 512:(j + 1) * 512], [("qsT", j)], tag, ttag)
                                zb = 6
                                khat_transpose(G, tag, 3)
                                for t in range(4):
                                    A("pe", lambda e, t=t, j=j, G=G: e.matmul(ps[7][:, t * 128:(t + 1) * 128], lhsT=G["kh"][:, t, :],
                                                                              rhs=vtm[:, 4 * j + t, :], start=True, stop=True),
                                      reads=[(tag, "kh"), ("vtm", j)], writes=[PS(7)])
                                hsl = slice(0, 64) if d == 0 else slice(64, 128)
                                A("pool", lambda e, G=G, hsl=hsl: e.tensor_copy(
                                    out=G["kt2"][:].rearrange("p (t s) -> p t s", t=4)[:, :, hsl],
                                    in_=G["b"][:].rearrange("p (t s) -> p t s", t=4)[:, :, hsl]),
                                  reads=[(ttag, "b"), "kt2z"], writes=[(tag, "kt2")])
                                for t in range(4):
                                    c0 = t * 128
                                    l1, l2 = (G["kt2"], G["kt"]) if d == 0 else (G["kt"], G["kt2"])
                                    k1, k2 = ((tag, "kt2"), (tag, "kt")) if d == 0 else ((tag, "kt"), (tag, "kt2"))
                                    A("pe", lambda e, c0=c0, l1=l1, G=G: e.matmul(ps[zb][:, c0:c0 + 64], lhsT=l1[:, c0:c0 + 128],
                                                                                  rhs=G["qt"][:, c0:c0 + 64], start=True, stop=True),
                                      reads=[k1, (tag, "qt")], writes=[PS(zb)])
                                    A("pe", lambda e, c0=c0, l2=l2, G=G: e.matmul(ps[zb][:, c0 + 64:c0 + 128], lhsT=l2[:, c0:c0 + 128],
                                                                                  rhs=G["qt"][:, c0 + 64:c0 + 128], start=True, stop=True),
                                      reads=[k2, (tag, "qt")], writes=[PS(zb)])
                                mk = maskf if d == 0 else maskb
                                A("dve", lambda e, G=G, mk=mk: e.tensor_tensor(out=G["am"][:], in0=ps[zb][:].rearrange("p (t s) -> p t s", t=4),
                                                                               in1=mk[:].unsqueeze(1).to_broadcast([128, 4, 128]),
                                                                               op=ALU.mult),
                                  reads=[PS(zb), "maskf", "maskb"], writes=[(tag, "am")])
                                order = range(4) if d == 0 else range(3, -1, -1)
                                Sd, sk = Sst[d], "Sst%d" % d
                                for t in order:
                                    A("act", lambda e, t=t, G=G, Sd=Sd: e.activation(out=G["sin"][:, t, :], in_=Sd[:], func=AF.Copy,
                                                                                     scale=G["dec"][:, t, G["ci"][1]:G["ci"][1] + 1]),
                                      reads=[sk, (tag, "dec1")], writes=[(tag, "sin", t)])
                                    A("dve", lambda e, t=t, G=G, Sd=Sd: e.scalar_tensor_tensor(out=Sd[:], in0=Sd[:], scalar=G["dec"][:, t, G["ci"][0]:G["ci"][0] + 1],
                                                                                               in1=ps[7][:, t * 128:(t + 1) * 128],
                                                                                               op0=ALU.mult, op1=ALU.add),
                                      reads=[PS(7), (tag, "dec0"), sk, (tag, "sin", t)], writes=[sk])
                                ob = d
                                for t in range(4):
                                    A("pe", lambda e, t=t, j=j, G=G: e.matmul(ps[ob][:, t * 128:(t + 1) * 128], lhsT=vtm[:, 4 * j + t, :],
                                                                              rhs=G["am"][:, t, :], start=True, stop=False),
                                      reads=[(tag, "am"), ("vtm", j)], writes=[PS(ob)])
                                    A("pe", lambda e, t=t, G=G: e.matmul(ps[ob][:, t * 128:(t + 1) * 128], lhsT=G["sin"][:, t, :],
                                                                         rhs=G["qt"][:, t * 128:(t + 1) * 128], start=False, stop=True),
                                      reads=[(tag, "sin", t), (tag, "qt")], writes=[PS(ob)])
                                first = (step < NB // 2)
                                if first:
                                    A("act", lambda e, j=j: e.activation(out=oacc[:, j * 512:(j + 1) * 512], in_=ps[ob][:], func=AF.Copy),
                                      reads=[PS(ob)], writes=[("oacc", j)])
                                else:
                                    A("dve", lambda e, j=j: e.tensor_tensor(out=oacc[:, j * 512:(j + 1) * 512], in0=oacc[:, j * 512:(j + 1) * 512],
                                                                            in1=ps[ob][:], op=ALU.add),
                                      reads=[PS(ob), ("oacc", j)], writes=[("oacc", j)])
                        for j in range(NB):
                            proj_fm(2, wb, WK(4), 4, j)
                            A("act", lambda e: e.activation(out=sgate[:], in_=ps[2][:], func=AF.Exp, scale=-1.0), reads=[PS(2)], writes=["sgate"])
                            A("act", lambda e: e.activation(out=sgate[:], in_=sgate[:], func=AF.Ln, bias=stats[:, 103, 1:2]),
                              reads=["sgate", "epsc"], writes=["sgate"])
                            A("act", lambda e: e.activation(out=sgate[:], in_=sgate[:], func=AF.Exp, scale=-1.0), reads=["sgate"], writes=["sgate"])
                            A("dve", lambda e: e.tensor_tensor(out=sgate[:], in0=ps[2][:], in1=sgate[:], op=ALU.mult),
                              reads=[PS(2), "sgate"], writes=["sgate"])
                            oj = oacc[:, j * 512:(j + 1) * 512]
                            A("pool", lambda e, oj=oj: e.tensor_tensor(out=sq[:], in0=oj, in1=oj, op=ALU.mult),
                              reads=[("oacc", j)], writes=["sq"])
                            A("pe", lambda e: e.matmul(ps[3][:], lhsT=onesd, rhs=sq[:], start=True, stop=True),
                              reads=["sq", "cf"], writes=[PS(3)])
                            A("act", lambda e: e.activation(out=rs[:], in_=ps[3][:], func=AF.Ln, bias=stats[:, 103, 0:1]),
                              reads=[PS(3), "epsc"], writes=["rs"])
                            A("act", lambda e: e.activation(out=rs[:], in_=rs[:], func=AF.Exp, scale=-0.5), reads=["rs"], writes=["rs"])
                            A("dve", lambda e, oj=oj: e.scalar_tensor_tensor(out=sq[:], in0=oj, scalar=vecs[:, V_GW:V_GW + 1], in1=rs[:],
                                                                             op0=ALU.mult, op1=ALU.mult),
                              reads=[("oacc", j), "rs", "vecs", "sq"], writes=["sq"])
                            cb_, ck = cblk[j % 2], "cblk%d" % (j % 2)
                            A("dve", lambda e, cb_=cb_: e.tensor_tensor(out=cb_[:], in0=sq[:], in1=sgate[:], op=ALU.mult),
                              reads=["sq", "sgate"], writes=[ck])
                            A("sp", lambda e, cb_=cb_, j=j, h=h: e.dma_start(out=catT[h, :, j * 512:(j + 1) * 512], in_=cb_[:]),
                              reads=[ck], writes=[("catT", h, j)], dma_key=ck)
                            if h == 0 and j == 0:
                                dump("oacc", oacc[:, 0:512], ("oacc", 0))
                S.barrier()

            if do_na:
                with ExitStack() as pa:
                    wna = [sbuf(pa, "wna%d" % i, [128, 8, 3, 128], BF16) for i in range(2)]
                    knT2 = [sbuf(pa, "knT%d" % i, [128, 36 * 128], BF16) for i in range(2)]
                    qnT2 = [sbuf(pa, "qnT%d" % i, [128, 4096], BF16) for i in range(2)]
                    vaug2 = [sbuf(pa, "vaug%d" % i, [128, 36, 2, 65], BF16) for i in range(2)]
                    EB = sbuf(pa, "EB", [128, 2, NSLOT * 128], BF16)
                    ebst = [sbuf(pa, "ebst%d" % i, [128, 6 * 128], F32) for i in range(2)]
                    qf = sbuf(pa, "qf", [128, 512], F32)
                    sq = sbuf(pa, "nsq", [128, 512], F32)
                    rs = sbuf(pa, "nrs", [128, 512], F32)
                    Eb = [sbuf(pa, "Eb%d" % i, [128, 6 * 128], BF16) for i in range(3)]
                    onat = [sbuf(pa, "onat%d" % i, [128, 128], BF16) for i in range(2)]
                    rcp = sbuf(pa, "rcp", [128, 8], F32)
                    cblk = [sbuf(pa, "ncblk%d" % i, [128, 512], BF16) for i in range(2)]
                    ebctr = [0]
                    A("pool", lambda e: e.memset(vaug2[0][:], 1.0), writes=[("vaug", 0)])
                    A("pool", lambda e: e.memset(vaug2[1][:], 1.0), writes=[("vaug", 1)])
                    wcf = [sbuf(pa, "wcf%d" % i, [128, D], F32) for i in range(3)]
                    wcb = [sbuf(pa, "wcb%d" % i, [128, D], BF16) for i in range(3)]
                    pieces = []
                    for k in range(8):
                        pieces.append((w_out[k * 128:(k + 1) * 128, :], wsc_o[:, k, :], D))
                    for k in range(8):
                        for (c0, cw) in ((0, 1024), (1024, 1024), (2048, 768)):
                            pieces.append((w_gate[k * 128:(k + 1) * 128, c0:c0 + cw], wsc_g[:, k, c0:c0 + cw], cw))
                            pieces.append((w_up[k * 128:(k + 1) * 128, c0:c0 + cw], wsc_u[:, k, c0:c0 + cw], cw))
                    for u in range(NU):
                        pieces.append((w_down[u * 128:(u + 1) * 128, :], wsc_d[:, u, :], D))
                    for i, (src_, dst_, cw) in enumerate(pieces):
                        f_, fk = wcf[i % 3], "wcf%d" % (i % 3)
                        b_, bk = wcb[i % 3], "wcb%d" % (i % 3)
                        A("pool", lambda e, f_=f_, src_=src_, cw=cw: e.dma_start(out=f_[:, 0:cw], in_=src_), writes=[fk], dma_key=fk)
                        A("pool", lambda e, f_=f_, b_=b_, cw=cw: e.tensor_copy(out=b_[:, 0:cw], in_=f_[:, 0:cw]), reads=[fk], writes=[bk])
                        A("pool", lambda e, b_=b_, dst_=dst_, cw=cw: e.dma_start(out=dst_, in_=b_[:, 0:cw]), reads=[bk], dma_key=bk)

                    nacol = [2560, 3072, 3584]

                    def load_na_w(hp):
                        wb = wna[hp % 2]
                        for ci, cb in enumerate(nacol):
                            load_cast(wb[:, :, ci, :], ("wna", hp % 2, ci), w_in_v[:, :, cb + hp * 128:cb + (hp + 1) * 128], wst, 1, wctr, eng="act")

                    def qknorm(bank, ncols, wcol, dst, dkey):
                        A("act", lambda e: e.activation(out=qf[:, 0:ncols], in_=ps[bank][:, 0:ncols], func=AF.Copy),
                          reads=[PS(bank)], writes=["qf"])
                        A("act", lambda e: e.activation(out=sq[:, 0:ncols], in_=ps[bank][:, 0:ncols], func=AF.Square),
                          reads=[PS(bank)], writes=["nsq"])
                        A("pe", lambda e: e.matmul(ps[7][:, 0:ncols], lhsT=blk64, rhs=sq[:, 0:ncols], start=True, stop=True),
                          reads=["nsq", "cf"], writes=[PS(7)])
                        A("act", lambda e: e.activation(out=rs[:, 0:ncols], in_=ps[7][:, 0:ncols], func=AF.Ln, bias=stats[:, 103, 0:1]),
                          reads=[PS(7), "epsc"], writes=["nrs"])
                        A("act", lambda e: e.activation(out=rs[:, 0:ncols], in_=rs[:, 0:ncols], func=AF.Exp, scale=-0.5),
                          reads=["nrs"], writes=["nrs"])
                        A("dve", lambda e: e.scalar_tensor_tensor(out=dst, in0=qf[:, 0:ncols], scalar=wqk[:, wcol:wcol + 1],
                                                                  in1=rs[:, 0:ncols], op0=ALU.mult, op1=ALU.mult),
                          reads=["qf", "nrs", "wqk"], writes=[dkey])

                    load_na_w(0)
                    for hp in range(4):
                        wb = wna[hp % 2]
                        WK = lambda ci: ("wna", hp % 2, ci)
                        knT, qnT, vaug = knT2[hp % 2], qnT2[hp % 2], vaug2[hp % 2]
                        hb = hp % 2
                        if hp + 1 < 4:
                            load_na_w(hp + 1)
                        for hh in range(2):
                            for (s0, ns) in ((0, 5), (5, 6), (11, 5), (16, 5), (21, 6)):
                                i = ebctr[0]
                                ebctr[0] += 1
                                sg, sk = ebst[i % 2], "ebst%d" % (i % 2)
                                A("sp", lambda e, sg=sg, s0=s0, ns=ns, hh=hh: e.dma_start(
                                    out=sg[:, 0:ns * 128], in_=nab[hp * 2 + hh, :, s0 * 128:(s0 + ns) * 128]),
                                  writes=[sk], dma_key=sk)
                                A("act", lambda e, sg=sg, s0=s0, ns=ns, hh=hh: e.activation(
                                    out=EB[:, hh, s0 * 128:(s0 + ns) * 128], in_=sg[:, 0:ns * 128], func=AF.Exp),
                                  reads=[sk], writes=[("EB", hh, s0)])
                        pieces = [(0, 2), (34, 2)] + [(2 + 4 * j, 4) for j in range(NB)]
                        for pi, (e0, ntl) in enumerate(pieces):
                            ncols = ntl * 128
                            cs = slice(e0 * 128, (e0 + ntl) * 128)
                            hk = [("hT", e0 + t) for t in range(ntl)]
                            kb = pi % 2
                            for k in range(8):
                                A("pe", lambda e, k=k, kb=kb, cs=cs, ncols=ncols: e.matmul(ps[kb][:, 0:ncols], lhsT=wb[:, k, 1, :],
                                                                                           rhs=hT[:, k, cs], start=(k == 0), stop=(k == 7)),
                                  reads=hk + [WK(1)], writes=[PS(kb)])
                            qknorm(kb, ncols, 1, knT[:, cs], ("knT", hb, pi))
                            vb = 2 + pi % 2
                            for t in range(ntl):
                                et = e0 + t
                                for k in range(8):
                                    A("pe", lambda e, k=k, t=t, et=et, vb=vb: e.matmul(
                                        ps[vb][:, t * 128:(t + 1) * 128], lhsT=hT[:, k, et * 128:(et + 1) * 128],
                                        rhs=wb[:, k, 2, :], start=(k == 0), stop=(k == 7)),
                                      reads=[("hT", et), WK(2)], writes=[PS(vb)])
                            A("dve", lambda e, vb=vb, e0=e0, ntl=ntl: e.tensor_copy(
                                out=vaug[:, e0:e0 + ntl, :, 0:64],
                                in_=ps[vb][:, 0:ntl * 128].rearrange("p (t h d) -> p t h d", t=ntl, h=2)),
                              reads=[PS(vb), ("vaug", hb)], writes=[("vaug", hb, pi)])
                            if e0 >= 2 and e0 < 34:
                                j = (e0 - 2) // 4
                                qb = 4 + pi % 2
                                for k in range(8):
                                    A("pe", lambda e, k=k, qb=qb, cs=cs: e.matmul(ps[qb][:], lhsT=wb[:, k, 0, :], rhs=hT[:, k, cs],
                                                                                  start=(k == 0), stop=(k == 7)),
                                      reads=hk + [WK(0)], writes=[PS(qb)])
                                qknorm(qb, 512, 0, qnT[:, j * 512:(j + 1) * 512], ("qnT", hb, j))
                        allk = [("knT", hb, pi) for pi in range(len(pieces))]
                        allv = [("vaug", hb, pi) for pi in range(len(pieces))]
                        for p in range(NT):
                            s0, keys = na_chunks(p)
                            nch = len(keys)
                            for hh in range(2):
                                it = p * 2 + hh
                                hs = slice(hh * 64, (hh + 1) * 64)
                                r3 = it % 3
                                ba, bb_ = 2 * r3, 2 * r3 + 1
                                for c, ke in enumerate(keys):
                                    if c < 4:
                                        dst, bank = ps[ba][:, c * 128:(c + 1) * 128], ba
                                    else:
                                        dst, bank = ps[bb_][:, 128 + (c - 4) * 128:128 + (c - 3) * 128], bb_
                                    A("pe", lambda e, dst=dst, ke=ke, hs=hs: e.matmul(
                                        dst, lhsT=knT[hs, ke * 128:(ke + 1) * 128],
                                        rhs=qnT[hs, p * 128:(p + 1) * 128], start=True, stop=True),
                                      reads=allk + [("qnT", hb, p // 4)], writes=[PS(bank)])
                                Eb_, ek = Eb[r3], "Eb%d" % r3
                                A("act", lambda e, Eb_=Eb_, ba=ba: e.activation(out=Eb_[:, 0:512], in_=ps[ba][:], func=AF.Exp),
                                  reads=[PS(ba)], writes=[ek])
                                A("act", lambda e, Eb_=Eb_, nch=nch, bb_=bb_: e.activation(out=Eb_[:, 512:nch * 128],
                                                                                           in_=ps[bb_][:, 128:128 + (nch - 4) * 128], func=AF.Exp),
                                  reads=[PS(bb_), ek], writes=[ek])
                                A("dve", lambda e, Eb_=Eb_, nch=nch, s0=s0, hh=hh: e.tensor_tensor(
                                    out=Eb_[:, 0:nch * 128], in0=Eb_[:, 0:nch * 128], in1=EB[:, hh, s0 * 128:(s0 + nch) * 128], op=ALU.mult),
                                  reads=[ek, ("EB", hh, s0)], writes=[ek])
                                ob = bb_
                                for c, ke in enumerate(keys):
                                    A("pe", lambda e, c=c, ke=ke, Eb_=Eb_, hh=hh, ob=ob: e.matmul(
                                        ps[ob][:, 0:65], lhsT=Eb_[:, c * 128:(c + 1) * 128], rhs=vaug[:, ke, hh, :],
                                        start=(c == 0), stop=(c == nch - 1)),
                                      reads=[ek] + allv, writes=[PS(ob)])
                                rc = it % 6
                                A("dve", lambda e, ob=ob, rc=rc: e.reciprocal(out=rcp[:, rc:rc + 1], in_=ps[ob][:, 64:65]),
                                  reads=[PS(ob)], writes=[("rcp", rc)])
                                on_, ok_ = onat[p % 2], "onat%d" % (p % 2)
                                A("dve", lambda e, ob=ob, hh=hh, on_=on_, rc=rc: e.tensor_scalar(
                                    out=on_[:, hh * 64:(hh + 1) * 64], in0=ps[ob][:, 0:64], scalar1=rcp[:, rc:rc + 1], scalar2=None, op0=ALU.mult),
                                  reads=[PS(ob), ("rcp", rc)], writes=[(ok_, hh)])
                            on_, ok_ = onat[p % 2], "onat%d" % (p % 2)
                            A("pe", lambda e, on_=on_, p=p: e.transpose(out=psb[6][:, (p % 4) * 128:(p % 4 + 1) * 128], in_=on_[:], identity=identb[:]),
                              reads=[(ok_, 0), (ok_, 1), "identb"], writes=[PS(6)])
                            if p % 4 == 3:
                                j = p // 4
                                cb_, ck = cblk[j % 2], "ncblk%d" % (j % 2)
                                A("act", lambda e, cb_=cb_: e.activation(out=cb_[:], in_=psb[6][:, 0:512], func=AF.Copy),
                                  reads=[PS(6)], writes=[ck])
                                A("sp", lambda e, cb_=cb_, j=j, hp=hp: e.dma_start(out=catT[4 + hp, :, j * 512:(j + 1) * 512], in_=cb_[:]),
                                  reads=[ck], writes=[("catT", 4 + hp, j)], dma_key=ck)
                S.barrier()
        S.barrier()
        if "catT" in dbg_out:
            A("sp", lambda e: e.dma_start(out=dbg_out["catT"], in_=catT), dma_key="dbg_catT")

        with ExitStack() as p3:
            wo_b = sbuf(p3, "wo_b", [128, 8, D], BF16)
            wg_b = sbuf(p3, "wg_b", [128, 8, DFF], BF16)
            wu_b = sbuf(p3, "wu_b", [128, 8, DFF], BF16)
            wdr = [sbuf(p3, "wdr%d" % i, [128, D], BF16) for i in range(6)]
            cat = [sbuf(p3, "cat%d" % i, [128, 8, 512], BF16) for i in range(2)]
            x1 = [sbuf(p3, "x1_%d" % i, [128, D], F32) for i in range(6)]
            xb3 = [sbuf(p3, "xb%d" % i, [128, D], BF16) for i in range(2)]
            h2T = sbuf(p3, "h2T", [128, 8, 512], BF16)
            actT = sbuf(p3, "actT", [128, NU, 512], BF16)
            sg3 = [sbuf(p3, "sg3_%d" % i, [128, 512], BF16) for i in range(2)]
            for kq in range(4):
                A("sp", lambda e, kq=kq: e.dma_start(out=wo_b[:, 2 * kq:2 * kq + 2, :], in_=wsc_o[:, 2 * kq:2 * kq + 2, :]),
                  writes=[("wo_b", kq)], dma_key=("wo_b", kq))
            for kq in range(4):
                A("sp", lambda e, kq=kq: e.dma_start(out=wg_b[:, 2 * kq:2 * kq + 2, :], in_=wsc_g[:, 2 * kq:2 * kq + 2, :]),
                  writes=[("wg_b", kq)], dma_key=("wg_b", kq))
                A("sp", lambda e, kq=kq: e.dma_start(out=wu_b[:, 2 * kq:2 * kq + 2, :], in_=wsc_u[:, 2 * kq:2 * kq + 2, :]),
                  writes=[("wu_b", kq)], dma_key=("wu_b", kq))
            wdc = [0]
            x1c = [0]
            for j in range(NB):
                cj, cjk = cat[j % 2], "cat%d" % (j % 2)
                if do_hgrn or do_na:
                    for k in range(8):
                        if (k < 4 and do_hgrn) or (k >= 4 and do_na):
                            A("sp", lambda e, k=k, cj=cj, j=j: e.dma_start(out=cj[:, k, :], in_=catT[k, :, j * 512:(j + 1) * 512]),
                              reads=[("catT", k, j)], writes=[(cjk, k)], dma_key=(cjk, k))
                xts = []
                for t in range(4):
                    tt = j * 4 + t
                    i = x1c[0]
                    x1c[0] += 1
                    xs, xk = x1[i % 6], "x1_%d" % (i % 6)
                    xts.append((xs, xk))
                    A("sp", lambda e, xs=xs, tt=tt: e.dma_start(out=xs[:], in_=xloc[tt * 128:(tt + 1) * 128, :]), writes=[xk], dma_key=xk)
                    if do_hgrn or do_na:
                        ks = [k for k in range(8) if (k < 4 and do_hgrn) or (k >= 4 and do_na)]
                        for half in range(2):
                            for ki, k in enumerate(ks):
                                A("pe", lambda e, k=k, half=half, t=t, cj=cj, ki=ki, nk=len(ks): e.matmul(
                                    ps[half][:], lhsT=cj[:, k, t * 128:(t + 1) * 128], rhs=wo_b[:, k, half * 512:(half + 1) * 512],
                                    start=(ki == 0), stop=(ki == nk - 1)),
                                  reads=[(cjk, k), ("wo_b", k // 2)],
                                  writes=[PS(half)])
                            A("dve", lambda e, xs=xs, half=half: e.tensor_tensor(out=xs[:, half * 512:(half + 1) * 512],
                                                                                 in0=xs[:, half * 512:(half + 1) * 512], in1=ps[half][:], op=ALU.add),
                              reads=[PS(half), xk], writes=[xk])
                    norm_tile(None, h2T[:, :, t * 128:(t + 1) * 128], ("h2T", t), V_NFW, None, xb3, 2, src_key=xk, x_keep=xs)
                hkeys3 = [("h2T", t) for t in range(4)]
                for u in range(NU):
                    gb, ub = 3 + 2 * (u % 2), 4 + 2 * (u % 2)
                    for k in range(8):
                        A("pe", lambda e, k=k, u=u, gb=gb: e.matmul(ps[gb][:], lhsT=wg_b[:, k, u * 128:(u + 1) * 128], rhs=h2T[:, k, :],
                                                                    start=(k == 0), stop=(k == 7)),
                          reads=hkeys3 + [("wg_b", k // 2)], writes=[PS(gb)])
                    for k in range(8):
                        A("pe", lambda e, k=k, u=u, ub=ub: e.matmul(ps[ub][:], lhsT=wu_b[:, k, u * 128:(u + 1) * 128], rhs=h2T[:, k, :],
                                                                    start=(k == 0), stop=(k == 7)),
                          reads=hkeys3 + [("wu_b", k // 2)], writes=[PS(ub)])
                    sg_, sgk = sg3[u % 2], "sg3_%d" % (u % 2)
                    A("act", lambda e, sg_=sg_, gb=gb: e.activation(out=sg_[:], in_=ps[gb][:], func=AF.Silu), reads=[PS(gb)], writes=[sgk])
                    A("dve", lambda e, sg_=sg_, ub=ub, u=u: e.tensor_tensor(out=actT[:, u, :], in0=sg_[:], in1=ps[ub][:], op=ALU.mult),
                      reads=[PS(ub), sgk], writes=[("actT", u)])
                wds = []
                for u in range(NU):
                    i = wdc[0]
                    wdc[0] += 1
                    wd_, wdk = wdr[i % 6], "wdr%d" % (i % 6)
                    A("sp", lambda e, wd_=wd_, u=u: e.dma_start(out=wd_[:], in_=wsc_d[:, u, :]), writes=[wdk], dma_key=wdk)
                    for t in range(4):
                        for half in range(2):
                            A("pe", lambda e, t=t, half=half, u=u, wd_=wd_: e.matmul(
                                dwn_view(ps, t, half), lhsT=actT[:, u, t * 128:(t + 1) * 128], rhs=wd_[:, half * 512:(half + 1) * 512],
                                start=(u == 0), stop=(u == NU - 1)),
                              reads=[("actT", u), wdk], writes=[("ps", t * 2 + half)])
                for t in range(4):
                    xs, xk = xts[t]
                    tt = j * 4 + t
                    for half in range(2):
                        A("dve", lambda e, xs=xs, half=half, t=t: e.tensor_tensor(out=xs[:, half * 512:(half + 1) * 512],
                                                                                  in0=xs[:, half * 512:(half + 1) * 512],
                                                                                  in1=dwn_view(ps, t, half), op=ALU.add),
                          reads=[("ps", t * 2 + half), xk], writes=[xk])
                    A("pool", lambda e, xs=xs, tt=tt: e.dma_start(out=y[tt * 128:(tt + 1) * 128, :], in_=xs[:]), reads=[xk], dma_key=("yst", xk))
        S.emit_all(st)
    return nc


def dwn_view(ps, t, half):
    return ps[t * 2 + half][:]


def _consts():
    c = np.zeros((128, 5 * 128 + 512), np.float32)
    c[:, 0:128] = np.eye(128, dtype=np.float32)
    s_ = np.arange(128)[:, None]
    t_ = np.arange(128)[None, :]
    c[:, 128:256] = (s_ <= t_).astype(np.float32)
    c[:, 256:384] = (s_ >= t_).astype(np.float32)
    c[:, 384:512] = 1.0 / 128.0
    blk = np.zeros((128, 128), np.float32)
    blk[:64, :64] = 1.0 / 64.0
    blk[64:, 64:] = 1.0 / 64.0
    c[:, 512:640] = blk
    rm = np.ones(512, np.float32)
    rm[0::128] = 0.0
    c[:, 640:1152] = rm[None, :]
    return c


MASKV = -30000.0


def _na_table(rpb, row_base, rows):
    out = np.full((8, 128, NSLOT, 128), MASKV, np.float32)
    kr = np.arange(128) // 64
    kc = np.arange(128) % 64
    qr = np.arange(128) // 64
    qc = np.arange(128) % 64
    cs = np.clip(qc - 8, 0, 64 - 16)

    def fill(p, slot_base, keys):
        R = row_base + 2 * p + qr
        rs = np.clip(R - 4, 0, rows - 8)
        for c, ke in enumerate(keys):
            kt = ke - 2
            KR = row_base + 2 * kt + kr
            valid = ((KR[:, None] >= rs[None, :]) & (KR[:, None] < rs[None, :] + 8) &
                     (kc[:, None] >= cs[None, :]) & (kc[:, None] < cs[None, :] + 16))
            dr = np.clip(KR[:, None] - R[None, :] + 7, 0, 14)
            dc = np.clip(kc[:, None] - qc[None, :] + 15, 0, 30)
            vals = rpb[:, dr, dc]
            out[:, :, slot_base + c, :] = np.where(valid[None], vals, np.float32(MASKV))

    s0, keys = na_chunks(5)
    fill(5, s0, keys)
    for p in (0, 1, 30, 31):
        s0, keys = na_chunks(p)
        fill(p, s0, keys)
    return out.reshape(8, 128, NSLOT * 128)


def make_core_inputs(c, inp):
    f32 = np.float32
    w_in = np.asarray(inp["w_in"][0], f32)
    lbraw = np.asarray(inp["hgrn_lb"], f32)
    if c < 4:
        seq = np.asarray(inp["x_prompt"][c], f32)
        xloc = seq
        xhalo = np.zeros((512, D), f32)
        xoth = np.zeros((4096, D), f32)
        osel, alpha, beta = 0, 0.0, 0.0
        row_base, rows = 0, 64
    else:
        sidx, half = (c - 4) // 2, (c - 4) % 2
        seq = np.asarray(inp["x_sample"][sidx], f32)
        xhalo = np.zeros((512, D), f32)
        rows = 128
        if half == 0:
            xloc = seq[:4096]
            xhalo[256:512] = seq[4096:4352]
            xoth = seq[4096:][::-1]
            osel, alpha, beta = 1, 0.0, 1.0
            row_base = 0
        else:
            xloc = seq[4096:]
            xhalo[0:256] = seq[3840:4096]
            xoth = seq[:4096]
            osel, alpha, beta = 0, 1.0, 0.0
            row_base = 64
    fcol = 512 + osel * 512
    w_oth = np.concatenate([w_in[:, fcol:fcol + 512], w_in[:, 1536:2048]], axis=1)
    vecs = np.zeros((128, NV), f32)
    vecs[:, V_NMW:V_NMW + 8] = np.asarray(inp["norm_mix_w"][0], f32).reshape(8, 128).T
    vecs[:, V_NFW:V_NFW + 8] = np.asarray(inp["norm_ffn_w"][0], f32).reshape(8, 128).T
    lb4 = lbraw.reshape(2, 2, 4, 128)
    for sl in range(2):
        for d in range(2):
            for h in range(4):
                vecs[:, V_LB + sl * 8 + d * 4 + h] = lb4[sl, d, h]
        for h in range(4):
            vecs[:, V_LBO + sl * 4 + h] = lb4[sl, osel, h]
    vecs[:, V_GW] = np.asarray(inp["hgrn_gnorm_w"][0], f32)
    vecs[:, V_WQ] = np.tile(np.asarray(inp["na_q_norm_w"][0], f32), 2)
    vecs[:, V_WK] = np.tile(np.asarray(inp["na_k_norm_w"][0], f32), 2)
    vecs[:, V_AL] = alpha
    vecs[:, V_BE] = beta
    return {
        "xloc": np.ascontiguousarray(xloc), "xhalo": xhalo, "xoth": np.ascontiguousarray(xoth),
        "w_in": w_in, "w_oth": np.ascontiguousarray(w_oth),
        "w_out": np.asarray(inp["w_out"][0], f32), "w_gate": np.asarray(inp["w_gate"][0], f32),
        "w_up": np.asarray(inp["w_up"][0], f32), "w_down": np.asarray(inp["w_down"][0], f32),
        "vecs": vecs, "nab": _na_table(np.asarray(inp["na_rpb"][0], f32), row_base, rows),
        "consts": _consts(),
    }


_NC_CACHE = {}


def kernel(**inputs):
    key = "full"
    if key not in _NC_CACHE:
        _NC_CACHE[key] = build()
    nc = _NC_CACHE[key]
    in_maps = [make_core_inputs(c, inputs) for c in range(8)]
    res = run_bass_kernel_spmd(nc, in_maps, core_ids=list(range(8)))
    ys = [np.asarray(r["y"], np.float32) for r in res.results]
    y_prompt = np.stack(ys[0:4], axis=0)
    y_sample = np.stack([np.concatenate([ys[4], ys[5]], axis=0), np.concatenate([ys[6], ys[7]], axis=0)], axis=0)
    return (y_prompt, y_sample)
```

```python
import numpy as np
from contextlib import ExitStack
import concourse.bass as bass
import concourse.mybir as mybir
from concourse.bass_utils import run_bass_kernel_spmd

F32 = mybir.dt.float32
BF16 = mybir.dt.bfloat16
AF = mybir.ActivationFunctionType
ALU = mybir.AluOpType

ENGS = ("pe", "act", "dve", "pool", "sp")
EPS = 1e-6
NT = 32
NB = 8
D = 1024
DFF = 2816
NU = 22


class _Op:
    __slots__ = ("eng", "emit", "deps", "odeps", "signal", "count", "is_dma", "sem", "semval", "name",
                 "dur", "idx", "fin", "sched")


class _Rec:
    def __init__(self):
        self.call = None

    def __getattr__(self, name):
        def f(*a, **kw):
            self.call = (name, a, kw)
            return None
        return f


def _fsize(ap):
    try:
        return int(ap.free_size())
    except Exception:
        return 512


def _est_ns(eng, call, is_dma):
    name, a, kw = call
    if is_dma:
        try:
            nb = int(kw["out"].nbytes())
        except Exception:
            nb = 65536
        return 2000.0 + nb / 150.0
    if eng == "pe":
        if name == "transpose":
            return 64.0
        rhs = kw.get("rhs")
        n = _fsize(rhs) if rhs is not None else 512
        f = 4.0 if (rhs is not None and rhs.dtype == F32) else 1.0
        return f * max(n, 64) * 0.42 + 12.0
    out = kw.get("out")
    n = _fsize(out) if out is not None else (_fsize(a[0]) if a else 512)
    if eng == "act":
        return 220.0 + n * 0.72
    if eng == "dve":
        if name in ("tensor_tensor_scan",):
            return 120.0 + 2.1 * n
        if name in ("tensor_tensor", "scalar_tensor_tensor"):
            return 120.0 + 1.3 * n
        return 120.0 + 1.05 * n
    if eng == "pool":
        return 250.0 + 2.1 * n
    return 100.0


class Sched:
    def __init__(self, nc, reorder=True, window=400):
        self.nc = nc
        self.ops = []
        self.last_w = {}
        self.readers = {}
        self.dma_cnt = {}
        self.fence = {}
        self.reorder = reorder
        self.window = window

    def _mk(self, eng, emit, name=None):
        op = _Op()
        op.eng = eng
        op.emit = emit
        op.signal = False
        op.count = 0
        op.is_dma = False
        op.sem = None
        op.semval = 0
        op.name = name
        op.deps = []
        op.odeps = []
        op.dur = 0.0
        op.idx = len(self.ops)
        op.fin = None
        op.sched = False
        return op

    def add(self, eng, emit, reads=(), writes=(), dma_key=None, name=None):
        rec = _Rec()
        emit(rec)
        call = rec.call
        assert call is not None
        op = self._mk(eng, (lambda h, call=call: getattr(h, call[0])(*call[1], **call[2])), name)
        deps = []
        for r in reads:
            w = self.last_w.get(r)
            if w is not None:
                deps.append(w)
        for w_ in writes:
            w = self.last_w.get(w_)
            if w is not None:
                deps.append(w)
            deps.extend(self.readers.get(w_, ()))
        if dma_key is not None:
            op.is_dma = True
            k = self.dma_cnt.get(dma_key, 0) + 1
            self.dma_cnt[dma_key] = k
            op.sem = dma_key
            op.semval = 16 * k
        op.dur = _est_ns(eng, call, op.is_dma)
        seen = set()
        for d in deps:
            if id(d) in seen or d is op:
                continue
            seen.add(id(d))
            if (not d.is_dma) and (not op.is_dma) and d.eng == eng and eng == "pe":
                op.odeps.append(d)
                continue
            op.deps.append(d)
        f = self.fence.get(eng)
        if f is not None:
            op.odeps.append(f)
        for d in op.deps:
            if not d.is_dma:
                d.signal = True
        for r in reads:
            self.readers.setdefault(r, []).append(op)
        for w_ in writes:
            self.last_w[w_] = op
            self.readers[w_] = []
        self.ops.append(op)
        return op

    def barrier(self):
        last = {}
        lastd = {}
        for op in self.ops:
            if op.is_dma:
                lastd[op.sem] = op
            elif op.emit is not None:
                last[op.eng] = op
        deps = list(last.values()) + list(lastd.values())
        allprev = list(self.ops)
        for e in ENGS:
            op = self._mk(e, None, "barrier")
            op.deps = list(deps)
            op.odeps = [o for o in allprev if o.eng == e and o.emit is not None][-1:]
            op.name = "barrier"
            for d in op.deps:
                if not d.is_dma:
                    d.signal = True
            self.ops.append(op)
            self.fence[e] = op
        self.last_w = {}
        self.readers = {}

    def _schedule(self):
        per = {e: [o for o in self.ops if o.eng == e] for e in ENGS}
        if not self.reorder:
            return per
        tail = {}
        for op in self.ops:
            tail[id(op)] = op.dur
        for op in reversed(self.ops):
            tl = tail[id(op)]
            for d in op.deps:
                v = d.dur + tl
                if v > tail[id(d)]:
                    tail[id(d)] = v
            for d in op.odeps:
                v = d.dur + tl
                if v > tail[id(d)]:
                    tail[id(d)] = v
        pending = {e: list(per[e]) for e in ENGS}
        out = {e: [] for e in ENGS}
        t = {e: 0.0 for e in ENGS}
        dma_free = [0.0]
        remaining = sum(len(v) for v in pending.values())
        W = self.window
        while remaining:
            best = None
            for e in ENGS:
                lst = pending[e]
                if not lst:
                    continue
                cand = None
                nready = 0
                lim = 0
                for op in lst:
                    lim += 1
                    if lim > W:
                        break
                    if op.name == "barrier" and op is not lst[0]:
                        break
                    ok = True
                    rdy = t[e]
                    for d in op.deps:
                        if not d.sched:
                            ok = False
                            break
                        df = d.fin + (150.0 if d.eng != e else 60.0)
                        if df > rdy:
                            rdy = df
                    if ok:
                        for d in op.odeps:
                            if not d.sched:
                                ok = False
                                break
                    if not ok:
                        if op.name == "barrier":
                            break
                        continue
                    if rdy <= t[e]:
                        tl = tail[id(op)]
                        if nready == 0 or tl > cand[2]:
                            cand = (rdy, op, tl)
                        nready += 1
                        if nready >= 96:
                            break
                    elif nready == 0 and (cand is None or rdy < cand[0]):
                        cand = (rdy, op, 0.0)
                if cand is not None and (best is None or cand[0] < best[0]):
                    best = (cand[0], e, cand[1])
            assert best is not None, "scheduler stuck"
            start, e, op = best
            pending[e].remove(op)
            op.sched = True
            if op.is_dma:
                issue = start + 60.0
                xfer = op.dur - 2000.0
                st_ = max(issue, dma_free[0])
                dma_free[0] = st_ + xfer
                op.fin = st_ + xfer + 2000.0
                t[e] = issue
            else:
                op.fin = start + op.dur
                t[e] = op.fin
            out[e].append(op)
            remaining -= 1
        self.sim_ns = max(t.values())
        return out

    def emit_all(self, stack):
        nc = self.nc
        lastd = {}
        for op in self.ops:
            if op.is_dma:
                lastd[op.sem] = op
        fin = self._mk("sp", None, "final")
        fin.name = "barrier"
        fin.deps = list(lastd.values())
        fin.odeps = [o for o in self.ops if o.eng == "sp"][-1:]
        self.ops.append(fin)

        per = self._schedule()
        eng_sem = {e: stack.enter_context(nc.semaphore("sem_" + e)) for e in ENGS}
        dma_sem = {}
        for i, k in enumerate(self.dma_cnt):
            dma_sem[k] = stack.enter_context(nc.semaphore("dsem%d" % i))
        for e in ENGS:
            c = 0
            for op in per[e]:
                if op.signal:
                    c += 1
                    op.count = c
        self.n_per = {e: len(per[e]) for e in ENGS}

        def run(e, handle):
            known = {}
            for op in per[e]:
                need = {}
                for d in op.deps:
                    if d.is_dma:
                        s, v = dma_sem[d.sem], d.semval
                    else:
                        s, v = eng_sem[d.eng], d.count
                    key = id(s)
                    if key not in need or need[key][1] < v:
                        need[key] = (s, v)
                for key, (s, v) in need.items():
                    if known.get(key, 0) >= v:
                        continue
                    handle.wait_ge(s, v)
                    known[key] = v
                if op.emit is None:
                    continue
                ins = op.emit(handle)
                if op.is_dma:
                    ins.then_inc(dma_sem[op.sem], 16)
                elif op.signal:
                    ins.then_inc(eng_sem[e], 1)

        block = stack.enter_context(nc.Block())

        @block.sync
        def _(h):
            run("sp", h)

        @block.scalar
        def _(h):
            run("act", h)

        @block.vector
        def _(h):
            run("dve", h)

        @block.gpsimd
        def _(h):
            run("pool", h)

        @block.tensor
        def _(h):
            run("pe", h)


V_NMW = 0
V_NFW = 8
V_LB = 16
V_LBO = 32
V_GW = 40
V_WQ = 41
V_WK = 42
V_AL = 43
V_BE = 44
NV = 48

NSLOT = 27
SLOT0 = {"gen": 0, 0: 5, 1: 11, 30: 16, 31: 21}


def na_chunks(p):
    keys = [p + c for c in range(5)]
    if p == 0:
        return SLOT0[0], keys + [3 + 2]
    if p == 31:
        return SLOT0[31], keys + [28 + 2]
    if p in (1, 30):
        return SLOT0[p], keys
    return SLOT0["gen"], keys


def build(do_hgrn=True, do_na=True, do_pre=True, dbg=()):
    nc = bass.Bass("TRN2", target_bir_lowering=False)
    dt_in = lambda n, s: nc.dram_tensor(n, s, F32, kind="ExternalInput").ap()
    xloc = dt_in("xloc", [4096, D])
    xhalo = dt_in("xhalo", [512, D])
    xoth = dt_in("xoth", [4096, D])
    w_in = dt_in("w_in", [D, 4096])
    w_oth = dt_in("w_oth", [D, 1024])
    w_out = dt_in("w_out", [D, D])
    w_gate = dt_in("w_gate", [D, DFF])
    w_up = dt_in("w_up", [D, DFF])
    w_down = dt_in("w_down", [DFF, D])
    vecs_d = dt_in("vecs", [128, NV])
    nab = dt_in("nab", [8, 128, NSLOT * 128])
    consts = dt_in("consts", [128, 5 * 128 + 512])
    y = nc.dram_tensor("y", [4096, D], F32, kind="ExternalOutput").ap()
    catT = nc.dram_tensor("catT", [8, 128, 4096], BF16, kind="Internal").ap()
    wsc_o = nc.dram_tensor("wsc_o", [128, 8, D], BF16, kind="Internal").ap()
    wsc_g = nc.dram_tensor("wsc_g", [128, 8, DFF], BF16, kind="Internal").ap()
    wsc_u = nc.dram_tensor("wsc_u", [128, 8, DFF], BF16, kind="Internal").ap()
    wsc_d = nc.dram_tensor("wsc_d", [128, NU, D], BF16, kind="Internal").ap()
    dbg_out = {}
    for name, shape in dbg:
        dbg_out[name] = nc.dram_tensor("dbg_" + name, list(shape), BF16 if name == "catT" else F32, kind="ExternalOutput").ap()

    w_in_v = w_in.rearrange("(k p) c -> p k c", p=128)
    w_oth_v = w_oth.rearrange("(k p) c -> p k c", p=128)
    w_out_v = w_out.rearrange("(k p) c -> p k c", p=128)
    w_gate_v = w_gate.rearrange("(k p) c -> p k c", p=128)
    w_up_v = w_up.rearrange("(k p) c -> p k c", p=128)
    w_down_v = w_down.rearrange("(u p) c -> p u c", p=128)

    with ExitStack() as st:
        S = Sched(nc)
        A = S.add

        _nm = [0]

        def sbuf(stack, name, shape, dt):
            _nm[0] += 1
            return stack.enter_context(nc.sbuf_tensor("sb%d_%s" % (_nm[0], name), shape, dt))

        ps = [st.enter_context(nc.psum_tensor("ps%d" % i, [128, 512], F32)) for i in range(8)]
        psb = [p[:].bitcast(BF16) for p in ps]
        PS = lambda i: ("ps", i)

        cf = sbuf(st, "cf", [128, 5 * 128 + 512], F32)
        vecs = sbuf(st, "vecs", [128, NV], F32)
        identb = sbuf(st, "identb", [128, 128], BF16)
        maskf = sbuf(st, "maskf", [128, 128], BF16)
        maskb = sbuf(st, "maskb", [128, 128], BF16)
        lbv = sbuf(st, "lbv", [128, 12], F32)
        oml = sbuf(st, "oml", [128, 12], F32)
        noml = sbuf(st, "noml", [128, 12], F32)
        lbd = sbuf(st, "lbd", [128, 12], F32)
        wqk = sbuf(st, "wqk", [128, 2], F32)
        Spre = sbuf(st, "Spre", [128, 4, 128], F32)
        stats = sbuf(st, "stats", [128, 104, 4], F32)
        onesd = cf[:, 384:512]
        blk64 = cf[:, 512:640]
        resetm = cf[:, 640:1152]

        A("sp", lambda e: e.dma_start(out=cf[:], in_=consts), writes=["cf"], dma_key="cf")
        A("sp", lambda e: e.dma_start(out=vecs[:], in_=vecs_d), writes=["vecs"], dma_key="vecs")
        A("dve", lambda e: e.tensor_copy(out=identb[:], in_=cf[:, 0:128]), reads=["cf"], writes=["identb"])
        A("dve", lambda e: e.tensor_copy(out=maskf[:], in_=cf[:, 128:256]), reads=["cf"], writes=["maskf"])
        A("dve", lambda e: e.tensor_copy(out=maskb[:], in_=cf[:, 256:384]), reads=["cf"], writes=["maskb"])
        A("dve", lambda e: e.tensor_tensor(out=lbd[:, 0:8], in0=vecs[:, V_LB:V_LB + 8], in1=vecs[:, V_LB + 8:V_LB + 16],
                                           op=ALU.subtract), reads=["vecs"], writes=["lbd"])
        A("dve", lambda e: e.tensor_tensor(out=lbd[:, 8:12], in0=vecs[:, V_LBO:V_LBO + 4], in1=vecs[:, V_LBO + 4:V_LBO + 8],
                                           op=ALU.subtract), reads=["vecs", "lbd"], writes=["lbd"])
        A("act", lambda e: e.activation(out=lbv[:], in_=lbd[:], func=AF.Sigmoid), reads=["lbd"], writes=["lbv"])
        A("act", lambda e: e.activation(out=oml[:], in_=lbd[:], func=AF.Sigmoid, scale=-1.0), reads=["lbd"], writes=["oml"])
        A("dve", lambda e: e.tensor_scalar(out=noml[:], in0=oml[:], scalar1=-1.0, scalar2=None, op0=ALU.mult),
          reads=["oml"], writes=["noml"])
        A("dve", lambda e: e.tensor_scalar(out=wqk[:, 0:1], in0=vecs[:, V_WQ:V_WQ + 1], scalar1=0.125, scalar2=None,
                                           op0=ALU.mult), reads=["vecs"], writes=["wqk"])
        A("dve", lambda e: e.tensor_copy(out=wqk[:, 1:2], in_=vecs[:, V_WK:V_WK + 1]), reads=["vecs", "wqk"], writes=["wqk"])
        A("pool", lambda e: e.memset(Spre[:], 0.0), writes=["Spre"])
        A("pool", lambda e: e.memset(stats[:, 103, 0:1], EPS), writes=["epsc"])
        A("pool", lambda e: e.memset(stats[:, 103, 1:2], 1.0), reads=["epsc"], writes=["epsc"])

        def dump(name, ap, key):
            if name in dbg_out:
                A("sp", lambda e: e.dma_start(out=dbg_out[name], in_=ap), reads=[key], dma_key="dbg_" + name)

        nt_ctr = [0]

        def norm_tile(src, dst3, dst_key, wcol, xt, xb, pbank, src_key=None, x_keep=None, scale_eng="act"):
            i = nt_ctr[0]
            nt_ctr[0] += 1
            sc = i % 103
            if src is not None:
                sl = i % len(xt)
                xs, xk = xt[sl], "xt%d" % sl
                A("sp", lambda e: e.dma_start(out=xs[:], in_=src), writes=[xk], dma_key=xk)
            else:
                xs, xk = x_keep, src_key
            bs, bk = xb[i % len(xb)], "xb%d" % (i % len(xb))
            stc = ("st", sc)
            A("act", lambda e: e.activation(out=bs[:], in_=xs[:], func=AF.Square, accum_out=stats[:, sc, 0:1]),
              reads=[xk], writes=[bk, stc])
            A("act", lambda e: e.activation(out=stats[:, sc, 2:3], in_=stats[:, sc, 0:1], func=AF.Ln, scale=1.0 / D, bias=stats[:, 103, 0:1]),
              reads=[stc, "epsc"], writes=[stc])
            A("act", lambda e: e.activation(out=stats[:, sc, 3:4], in_=stats[:, sc, 2:3], func=AF.Exp, scale=-0.5),
              reads=[stc], writes=[stc])
            if scale_eng == "dve":
                A("dve", lambda e: e.tensor_scalar(out=bs[:], in0=xs[:], scalar1=stats[:, sc, 3:4], scalar2=None, op0=ALU.mult),
                  reads=[xk, stc], writes=[bk])
            else:
                A("act", lambda e: e.activation(out=bs[:], in_=xs[:], func=AF.Copy, scale=stats[:, sc, 3:4]),
                  reads=[xk, stc], writes=[bk])
            for k in range(8):
                A("pe", lambda e, k=k: e.transpose(out=psb[pbank][:, k * 128:(k + 1) * 128], in_=bs[:, k * 128:(k + 1) * 128],
                                                    identity=identb[:]),
                  reads=[bk, "identb"], writes=[PS(pbank)])
            A("dve", lambda e: e.tensor_tensor(out=dst3, in0=psb[pbank][:, 0:1024].rearrange("p (k t) -> p k t", k=8),
                                               in1=vecs[:, wcol:wcol + 8].unsqueeze(2).to_broadcast([128, 8, 128]),
                                               op=ALU.mult),
              reads=[PS(pbank), "vecs"], writes=[dst_key])

        def load_cast(dst, dst_key, src, stage, nstage, ctr, eng="pool", shape=None):
            i = ctr[0]
            ctr[0] += 1
            sg, sk = stage[i % nstage], "wst%d" % (i % nstage)
            sv = sg[:] if shape is None else shape(sg)
            A("sp", lambda e: e.dma_start(out=sv, in_=src), writes=[sk], dma_key=sk)
            if eng == "act":
                A("act", lambda e: e.activation(out=dst, in_=sv, func=AF.Copy), reads=[sk], writes=[dst_key])
            else:
                A(eng, lambda e: e.tensor_copy(out=dst, in_=sv), reads=[sk], writes=[dst_key])

        with ExitStack() as p12:
            hT = sbuf(p12, "hT", [128, 8, 36 * 128], BF16)
            wctr = [0]
            wst = [sbuf(p12, "wst%d" % i, [128, 8, 128], F32) for i in range(1)]
            p1x = p12.enter_context(ExitStack())
            xt = [sbuf(p1x, "xt%d" % i, [128, D], F32) for i in range(3)]
            xb = [sbuf(p1x, "xb%d" % i, [128, D], BF16) for i in range(3)]
            for e_ in range(36):
                if e_ < 2:
                    src = xhalo[e_ * 128:(e_ + 1) * 128, :]
                elif e_ >= 34:
                    src = xhalo[(e_ - 32) * 128:(e_ - 31) * 128, :]
                else:
                    src = xloc[(e_ - 2) * 128:(e_ - 1) * 128, :]
                norm_tile(src, hT[:, :, e_ * 128:(e_ + 1) * 128], ("hT", e_), V_NMW, xt, xb, e_ % 4, scale_eng="dve")
            if "hT" in dbg_out:
                hTf = sbuf(p1x, "hTf", [128, 8, 512], F32)
                A("dve", lambda e: e.tensor_copy(out=hTf[:], in_=hT[:, :, 256:768]), reads=[("hT", 2), ("hT", 3), ("hT", 4), ("hT", 5)],
                  writes=["hTf"])
                dump("hT", hTf[:], "hTf")


            def gate_batch(G, zbank, lcol, bwd, qsrc, qkeys, tag, ttag=None):
                s1, s2, g, b, bp = G["s1"], G["s2"], G["g"], G["b"], G["bp"]
                k32 = b
                eb = s1
                ttag = ttag or tag
                _al = {"eb": "s1", "k32": "b"}
                _tmp = ("s1", "s2", "g", "b", "bp", "bb")
                K = lambda n: ((ttag if _al.get(n, n) in _tmp else tag), _al.get(n, n))
                A("act", lambda e: e.activation(out=s1[:], in_=ps[zbank][:], func=AF.Exp, scale=-1.0), reads=[PS(zbank)], writes=[K("s1")])
                A("act", lambda e: e.activation(out=s2[:], in_=s1[:], func=AF.Ln, bias=stats[:, 103, 1:2]), reads=[K("s1"), "epsc"], writes=[K("s2")])
                A("act", lambda e: e.activation(out=s1[:], in_=s2[:], func=AF.Exp, scale=-1.0), reads=[K("s2")], writes=[K("s1")])
                A("act", lambda e: e.activation(out=g[:], in_=s1[:], func=AF.Ln, scale=oml[:, lcol:lcol + 1],
                                                bias=lbv[:, lcol:lcol + 1]),
                  reads=[K("s1"), "oml", "lbv"], writes=[K("g")])
                A("pool", lambda e: e.tensor_scalar(out=s2[:], in0=s1[:], scalar1=noml[:, lcol:lcol + 1], scalar2=oml[:, lcol:lcol + 1],
                                                    op0=ALU.mult, op1=ALU.add),
                  reads=[K("s1"), "oml", "noml"], writes=[K("s2")])
                A("dve", lambda e: e.tensor_tensor_scan(out=b[:], data0=resetm, data1=g[:], initial=0.0,
                                                        op0=ALU.mult, op1=ALU.add),
                  reads=[K("g"), "cf"], writes=[K("b")])
                b3 = b[:].rearrange("p (t s) -> p t s", t=4)
                g3 = g[:].rearrange("p (t s) -> p t s", t=4)
                bp3 = bp[:].rearrange("p (t s) -> p t s", t=4)
                dec = G["dec"]
                if not bwd:
                    A("dve", lambda e: e.tensor_tensor(out=bp3, in0=b3, in1=b3[:, :, 63:64].to_broadcast([128, 4, 128]),
                                                       op=ALU.subtract), reads=[K("b")], writes=[K("bp")])
                    A("act", lambda e: e.activation(out=dec[:, :, 0:2], in_=b3[:, :, 63:128:64], func=AF.Exp),
                      reads=[K("b")], writes=[K("dec0"), K("dec1")])
                    G["ci"] = (1, 0)
                    A("act", lambda e: e.activation(out=dec[:, :, 2:3], in_=bp3[:, :, 127:128], func=AF.Exp),
                      reads=[K("bp")], writes=[K("dec2")])
                else:
                    bb3 = G["bb"][:].rearrange("p (t s) -> p t s", t=4)
                    A("dve", lambda e: e.tensor_tensor(out=bb3, in0=g3, in1=b3, op=ALU.subtract),
                      reads=[K("g"), K("b")], writes=[K("bb")])
                    A("dve", lambda e: e.tensor_tensor(out=bp3, in0=bb3, in1=bb3[:, :, 64:65].to_broadcast([128, 4, 128]),
                                                       op=ALU.subtract), reads=[K("bb")], writes=[K("bp")])
                    A("dve", lambda e: e.tensor_tensor(out=dec[:, :, 0:2], in0=bb3[:, :, 0:128:64],
                                                       in1=b3[:, :, 127:128].to_broadcast([128, 4, 2]), op=ALU.add),
                      reads=[K("bb"), K("b")], writes=[K("dec0"), K("dec1")])
                    A("act", lambda e: e.activation(out=dec[:, :, 0:2], in_=dec[:, :, 0:2], func=AF.Exp),
                      reads=[K("dec0"), K("dec1")], writes=[K("dec0"), K("dec1")])
                    G["ci"] = (0, 1)
                    A("act", lambda e: e.activation(out=dec[:, :, 2:3], in_=bp3[:, :, 0:1], func=AF.Exp),
                      reads=[K("bp")], writes=[K("dec2")])
                A("act", lambda e: e.activation(out=eb[:], in_=bp[:], func=AF.Exp, scale=-1.0), reads=[K("bp")], writes=[K("eb")])
                A("dve", lambda e: e.tensor_tensor(out=k32[:], in0=s2[:], in1=eb[:], op=ALU.mult),
                  reads=[K("s2"), K("eb")], writes=[K("k32")])
                if G.get("kt") is not None:
                    A("pool", lambda e: e.tensor_copy(out=G["kt"][:], in_=k32[:]), reads=[K("k32")], writes=[K("kt")])
                A("pool", lambda e: e.tensor_tensor(out=G["khT"][:].rearrange("p (t s) -> p t s", t=4),
                                                   in0=k32[:].rearrange("p (t s) -> p t s", t=4),
                                                   in1=dec[:, :, 2:3].to_broadcast([128, 4, 128]), op=ALU.mult),
                  reads=[K("k32"), K("dec2")], writes=[K("khT")])
                if qsrc is not None:
                    A("act", lambda e: e.activation(out=eb[:], in_=bp[:], func=AF.Exp), reads=[K("bp"), K("k32")], writes=[K("eb")])
                    A("pool", lambda e: e.tensor_tensor(out=G["qt"][:], in0=qsrc, in1=eb[:], op=ALU.mult),
                      reads=[K("eb")] + list(qkeys), writes=[K("qt")])

            def alloc_gate_tmp(stack, tag, bwd):
                T = {}
                for n in ["s1", "s2", "g", "b", "bp"] + (["bb"] if bwd else []):
                    T[n] = sbuf(stack, "%s_%s" % (tag, n), [128, 512], F32)
                return T

            def alloc_gate(stack, tag, with_q, bwd=False, tmp=True):
                G = {}
                if tmp:
                    G.update(alloc_gate_tmp(stack, tag, bwd))
                G["khT"] = sbuf(stack, tag + "_khT", [128, 512], BF16)
                G["kh"] = sbuf(stack, tag + "_kh", [128, 4, 128], BF16)
                G["dec"] = sbuf(stack, tag + "_dec", [128, 4, 3], F32)
                if with_q:
                    G["kt"] = sbuf(stack, tag + "_kt", [128, 512], BF16)
                    G["kt2"] = sbuf(stack, tag + "_kt2", [128, 512], BF16)
                    G["qt"] = sbuf(stack, tag + "_qt", [128, 512], BF16)
                    G["am"] = sbuf(stack, tag + "_am", [128, 4, 128], BF16)
                    G["sin"] = sbuf(stack, tag + "_sin", [128, 4, 128], BF16)
                return G

            def khat_transpose(G, tag, pbank):
                for t in range(4):
                    A("pe", lambda e, t=t: e.transpose(out=psb[pbank][:, t * 128:(t + 1) * 128],
                                                       in_=G["khT"][:, t * 128:(t + 1) * 128], identity=identb[:]),
                      reads=[(tag, "khT"), "identb"], writes=[PS(pbank)])
                A("act", lambda e: e.activation(out=G["kh"][:].rearrange("p t d -> p (t d)"), in_=psb[pbank][:, 0:512], func=AF.Copy),
                  reads=[PS(pbank)], writes=[(tag, "kh")])

            if do_hgrn and do_pre:
                with ExitStack() as pp:
                    wo = sbuf(pp, "wo", [128, 8, 1024], BF16)
                    for c in range(8):
                        load_cast(wo[:, :, c * 128:(c + 1) * 128], ("wo", c), w_oth_v[:, :, c * 128:(c + 1) * 128], wst, 1, wctr)
                    hTo = [sbuf(pp, "hTo%d" % i, [128, 8, 512], BF16) for i in range(2)]
                    Gps = [alloc_gate(pp, "gp%d" % i, False, bwd=False, tmp=True) for i in range(2)]
                    vtos = [sbuf(pp, "vto%d" % i, [128, 4, 128], BF16) for i in range(2)]
                    for jb in range(NB):
                        ho = hTo[jb % 2]
                        hk = "hTo%d" % (jb % 2)
                        for t in range(4):
                            r0 = (jb * 4 + t) * 128
                            norm_tile(xoth[r0:r0 + 128, :], ho[:, :, t * 128:(t + 1) * 128], hk, V_NMW, xt, xb, t % 2)
                        for h in range(4):
                            par = h % 2
                            Gp, gtag = Gps[par], "gp%d" % par
                            vto, vk = vtos[par], "vto%d" % par
                            zbk, vbk = 2 + 4 * par, 3 + 4 * par
                            for k in range(8):
                                A("pe", lambda e, k=k, h=h: e.matmul(ps[zbk][:], lhsT=wo[:, k, h * 128:(h + 1) * 128], rhs=ho[:, k, :],
                                                                     start=(k == 0), stop=(k == 7)),
                                  reads=[hk, ("wo", h)], writes=[PS(zbk)])
                            for t in range(4):
                                for k in range(8):
                                    A("pe", lambda e, k=k, h=h, t=t: e.matmul(ps[vbk][:, t * 128:(t + 1) * 128],
                                                                              lhsT=ho[:, k, t * 128:(t + 1) * 128],
                                                                              rhs=wo[:, k, 512 + h * 128:512 + (h + 1) * 128],
                                                                              start=(k == 0), stop=(k == 7)),
                                      reads=[hk, ("wo", 4 + h)], writes=[PS(vbk)])
                            A("act", lambda e: e.activation(out=vto[:].rearrange("p t d -> p (t d)"), in_=ps[vbk][:], func=AF.Copy),
                              reads=[PS(vbk)], writes=[vk])
                            gate_batch(Gp, zbk, 8 + h, False, None, None, gtag)
                            khat_transpose(Gp, gtag, 4)
                            for t in range(4):
                                A("pe", lambda e, t=t: e.matmul(ps[5][:, t * 128:(t + 1) * 128], lhsT=Gp["kh"][:, t, :], rhs=vto[:, t, :],
                                                                start=True, stop=True),
                                  reads=[(gtag, "kh"), vk], writes=[PS(5)])
                            for t in range(4):
                                A("dve", lambda e, t=t, h=h: e.scalar_tensor_tensor(out=Spre[:, h, :], in0=Spre[:, h, :],
                                                                                    scalar=Gp["dec"][:, t, Gp["ci"][0]:Gp["ci"][0] + 1],
                                                                                    in1=ps[5][:, t * 128:(t + 1) * 128],
                                                                                    op0=ALU.mult, op1=ALU.add),
                                  reads=[PS(5), (gtag, "dec0"), ("Spre", h), "Spre"], writes=[("Spre", h)])
                S.barrier()
            dump("Spre", Spre[:].rearrange("p h d -> p (h d)"), "Spre")
            S.barrier()
            p1x.close()

            if do_hgrn:
                with ExitStack() as ph:
                    whd = [sbuf(ph, "whd%d" % i, [128, 8, 5, 128], BF16) for i in range(2)]
                    qsT = sbuf(ph, "qsT", [128, 4096], BF16)
                    vtm = sbuf(ph, "vtm", [128, NT, 128], BF16)
                    oacc = sbuf(ph, "oacc", [128, 4096], F32)
                    Sst = [sbuf(ph, "Sst%d" % i, [128, 128], F32) for i in range(2)]
                    Gsh = [alloc_gate(ph, "gf", True, tmp=False), alloc_gate(ph, "gb", True, tmp=False)]
                    Gtm = [[alloc_gate_tmp(ph, "gf%d" % i, False) for i in range(2)],
                           [alloc_gate_tmp(ph, "gb%d" % i, True) for i in range(2)]]
                    sgate = sbuf(ph, "sgate", [128, 512], F32)
                    sq = sbuf(ph, "sq", [128, 512], F32)
                    rs = sbuf(ph, "rs", [128, 512], F32)
                    cblk = [sbuf(ph, "cblk%d" % i, [128, 512], BF16) for i in range(2)]
                    colbase = [0, 512, 1024, 1536, 2048]

                    def load_head_w(h):
                        wb = whd[h % 2]
                        for ci, cb in enumerate(colbase):
                            load_cast(wb[:, :, ci, :], ("whd", h % 2, ci), w_in_v[:, :, cb + h * 128:cb + (h + 1) * 128], wst, 1, wctr)

                    def hcols(j):
                        return slice((2 + 4 * j) * 128, (2 + 4 * j + 4) * 128)

                    def hkeys(j):
                        return [("hT", 2 + 4 * j + t) for t in range(4)]

                    def proj_fm(bank, wb, wkey, ci, j):
                        for k in range(8):
                            A("pe", lambda e, k=k: e.matmul(ps[bank][:], lhsT=wb[:, k, ci, :], rhs=hT[:, k, hcols(j)],
                                                            start=(k == 0), stop=(k == 7)),
                              reads=hkeys(j) + [wkey], writes=[PS(bank)])

                    A("dve", lambda e: e.memset(ps[6][:], 0.0), writes=[PS(6)])
                    A("pool", lambda e: e.memset(Gsh[0]["kt2"][:], 0.0), writes=["kt2z", ("gf", "kt2")])
                    A("pool", lambda e: e.memset(Gsh[1]["kt2"][:], 0.0), reads=["kt2z"], writes=["kt2z", ("gb", "kt2")])
                    load_head_w(0)
                    for h in range(4):
                        wb = whd[h % 2]
                        WK = lambda ci: ("whd", h % 2, ci)
                        if h + 1 < 4:
                            load_head_w(h + 1)
                        for j in range(NB):
                            proj_fm(0 + j % 2, wb, WK(0), 0, j)
                            ta_, tk = ((sq, "sq"), (rs, "rs"))[j % 2]
                            A("act", lambda e, j=j, ta_=ta_: e.activation(out=ta_[:], in_=ps[j % 2][:], func=AF.Exp, scale=-1.0),
                              reads=[PS(j % 2)], writes=[tk])
                            A("act", lambda e, ta_=ta_: e.activation(out=ta_[:], in_=ta_[:], func=AF.Ln, bias=stats[:, 103, 1:2]),
                              reads=[tk, "epsc"], writes=[tk])
                            A("act", lambda e, ta_=ta_: e.activation(out=ta_[:], in_=ta_[:], func=AF.Exp, scale=-1.0), reads=[tk], writes=[tk])
                            A("dve", lambda e, j=j, ta_=ta_: e.tensor_tensor(out=qsT[:, j * 512:(j + 1) * 512], in0=ps[j % 2][:], in1=ta_[:],
                                                                            op=ALU.mult),
                              reads=[PS(j % 2), tk], writes=[("qsT", j)])
                            bank = 2 + j % 2
                            for t in range(4):
                                et = 2 + 4 * j + t
                                for k in range(8):
                                    A("pe", lambda e, k=k, t=t, et=et, bank=bank: e.matmul(
                                        ps[bank][:, t * 128:(t + 1) * 128], lhsT=hT[:, k, et * 128:(et + 1) * 128],
                                        rhs=wb[:, k, 3, :], start=(k == 0), stop=(k == 7)),
                                      reads=[("hT", et), WK(3)], writes=[PS(bank)])
                            A("dve", lambda e, j=j, bank=bank: e.tensor_copy(out=vtm[:, 4 * j:4 * j + 4, :].rearrange("p t d -> p (t d)"),
                                                                            in_=ps[bank][:]),
                              reads=[PS(bank)], writes=[("vtm", j)])
                        A("dve", lambda e, h=h: e.tensor_scalar(out=Sst[0][:], in0=Spre[:, h, :], scalar1=vecs[:, V_AL:V_AL + 1],
                                                                scalar2=None, op0=ALU.mult),
                          reads=["Spre", "vecs"], writes=["Sst0"])
                        A("dve", lambda e, h=h: e.tensor_scalar(out=Sst[1][:], in0=Spre[:, h, :], scalar1=vecs[:, V_BE:V_BE + 1],
                                                                scalar2=None, op0=ALU.mult),
                          reads=["Spre", "vecs"], writes=["Sst1"])
                        for step in range(NB):
                            for d in range(2):
                                j = step if d == 0 else NB - 1 - step
                                G = dict(Gsh[d])
                                G.update(Gtm[d][step % 2])
                                tag = "gf" if d == 0 else "gb"
                                ttag = tag + str(step % 2)
                                zb = 4 + d
                                proj_fm(zb, wb, WK(1 + d), 1 + d, j)
                                gate_batch(G, zb, d * 4 + h, d == 1, qsT[:, j * 512:(j + 1) * 512], [("qsT", j)], tag, ttag)
                                zb = 6
                                khat_transpose(G, tag, 3)
                                for t in range(4):
                                    A("pe", lambda e, t=t, j=j, G=G: e.matmul(ps[7][:, t * 128:(t + 1) * 128], lhsT=G["kh"][:, t, :],
                                                                              rhs=vtm[:, 4 * j + t, :], start=True, stop=True),
                                      reads=[(tag, "kh"), ("vtm", j)], writes=[PS(7)])
                                hsl = slice(0, 64) if d == 0 else slice(64, 128)
                                A("pool", lambda e, G=G, hsl=hsl: e.tensor_copy(
                                    out=G["kt2"][:].rearrange("p (t s) -> p t s", t=4)[:, :, hsl],
                                    in_=G["b"][:].rearrange("p (t s) -> p t s", t=4)[:, :, hsl]),
                                  reads=[(ttag, "b"), "kt2z"], writes=[(tag, "kt2")])
                                for t in range(4):
                                    c0 = t * 128
                                    l1, l2 = (G["kt2"], G["kt"]) if d == 0 else (G["kt"], G["kt2"])
                                    k1, k2 = ((tag, "kt2"), (tag, "kt")) if d == 0 else ((tag, "kt"), (tag, "kt2"))
                                    A("pe", lambda e, c0=c0, l1=l1, G=G: e.matmul(ps[zb][:, c0:c0 + 64], lhsT=l1[:, c0:c0 + 128],
                                                                                  rhs=G["qt"][:, c0:c0 + 64], start=True, stop=True),
                                      reads=[k1, (tag, "qt")], writes=[PS(zb)])
                                    A("pe", lambda e, c0=c0, l2=l2, G=G: e.matmul(ps[zb][:, c0 + 64:c0 + 128], lhsT=l2[:, c0:c0 + 128],
                                                                                  rhs=G["qt"][:, c0 + 64:c0 + 128], start=True, stop=True),
                                      reads=[k2, (tag, "qt")], writes=[PS(zb)])
                                mk = maskf if d == 0 else maskb
                                A("dve", lambda e, G=G, mk=mk: e.tensor_tensor(out=G["am"][:], in0=ps[zb][:].rearrange("p (t s) -> p t s", t=4),
                                                                               in1=mk[:].unsqueeze(1).to_broadcast([128, 4, 128]),
                                                                               op=ALU.mult),
                                  reads=[PS(zb), "maskf", "maskb"], writes=[(tag, "am")])
                                order = range(4) if d == 0 else range(3, -1, -1)
                                Sd, sk = Sst[d], "Sst%d" % d
                                for t in order:
                                    A("act", lambda e, t=t, G=G, Sd=Sd: e.activation(out=G["sin"][:, t, :], in_=Sd[:], func=AF.Copy,
                                                                                     scale=G["dec"][:, t, G["ci"][1]:G["ci"][1] + 1]),
                                      reads=[sk, (tag, "dec1")], writes=[(tag, "sin", t)])
                                    A("dve", lambda e, t=t, G=G, Sd=Sd: e.scalar_tensor_tensor(out=Sd[:], in0=Sd[:], scalar=G["dec"][:, t, G["ci"][0]:G["ci"][0] + 1],
                                                                                               in1=ps[7][:, t * 128:(t + 1) * 128],
                                                                                               op0=ALU.mult, op1=ALU.add),
                                      reads=[PS(7), (tag, "dec0"), sk, (tag, "sin", t)], writes=[sk])
                                ob = d
                                for t in range(4):
                                    A("pe", lambda e, t=t, j=j, G=G: e.matmul(ps[ob][:, t * 128:(t + 1) * 128], lhsT=vtm[:, 4 * j + t, :],
                                                                              rhs=G["am"][:, t, :], start=True, stop=False),
                                      reads=[(tag, "am"), ("vtm", j)], writes=[PS(ob)])
                                    A("pe", lambda e, t=t, G=G: e.matmul(ps[ob][:, t * 128:(t + 1) * 128], lhsT=G["sin"][:, t, :],
                                                                         rhs=G["qt"][:, t * 128:(t + 1) * 128], start=False, stop=True),
                                      reads=[(tag, "sin", t), (tag, "qt")], writes=[PS(ob)])
                                first = (step < NB // 2)
                                if first:
                                    A("act", lambda e, j=j: e.activation(out=oacc[:, j * 512:(j + 1) * 512], in_=ps[ob][:], func=AF.Copy),
                                      reads=[PS(ob)], writes=[("oacc", j)])
                                else:
                                    A("dve", lambda e, j=j: e.tensor_tensor(out=oacc[:, j * 512:(j + 1) * 512], in0=oacc[:, j * 512:(j + 1) * 512],
                                                                            in1=ps[ob][:], op=ALU.add),
                                      reads=[PS(ob), ("oacc", j)], writes=[("oacc", j)])
                        for j in range(NB):
                            proj_fm(2, wb, WK(4), 4, j)
                            A("act", lambda e: e.activation(out=sgate[:], in_=ps[2][:], func=AF.Exp, scale=-1.0), reads=[PS(2)], writes=["sgate"])
                            A("act", lambda e: e.activation(out=sgate[:], in_=sgate[:], func=AF.Ln, bias=stats[:, 103, 1:2]),
                              reads=["sgate", "epsc"], writes=["sgate"])
                            A("act", lambda e: e.activation(out=sgate[:], in_=sgate[:], func=AF.Exp, scale=-1.0), reads=["sgate"], writes=["sgate"])
                            A("dve", lambda e: e.tensor_tensor(out=sgate[:], in0=ps[2][:], in1=sgate[:], op=ALU.mult),
                              reads=[PS(2), "sgate"], writes=["sgate"])
                            oj = oacc[:, j * 512:(j + 1) * 512]
                            A("pool", lambda e, oj=oj: e.tensor_tensor(out=sq[:], in0=oj, in1=oj, op=ALU.mult),
                              reads=[("oacc", j)], writes=["sq"])
                            A("pe", lambda e: e.matmul(ps[3][:], lhsT=onesd, rhs=sq[:], start=True, stop=True),
                              reads=["sq", "cf"], writes=[PS(3)])
                            A("act", lambda e: e.activation(out=rs[:], in_=ps[3][:], func=AF.Ln, bias=stats[:, 103, 0:1]),
                              reads=[PS(3), "epsc"], writes=["rs"])
                            A("act", lambda e: e.activation(out=rs[:], in_=rs[:], func=AF.Exp, scale=-0.5), reads=["rs"], writes=["rs"])
                            A("dve", lambda e, oj=oj: e.scalar_tensor_tensor(out=sq[:], in0=oj, scalar=vecs[:, V_GW:V_GW + 1], in1=rs[:],
                                                                             op0=ALU.mult, op1=ALU.mult),
                              reads=[("oacc", j), "rs", "vecs", "sq"], writes=["sq"])
                            cb_, ck = cblk[j % 2], "cblk%d" % (j % 2)
                            A("dve", lambda e, cb_=cb_: e.tensor_tensor(out=cb_[:], in0=sq[:], in1=sgate[:], op=ALU.mult),
                              reads=["sq", "sgate"], writes=[ck])
                            A("sp", lambda e, cb_=cb_, j=j, h=h: e.dma_start(out=catT[h, :, j * 512:(j + 1) * 512], in_=cb_[:]),
                              reads=[ck], writes=[("catT", h, j)], dma_key=ck)
                            if h == 0 and j == 0:
                                dump("oacc", oacc[:, 0:512], ("oacc", 0))
                S.barrier()

            if do_na:
                with ExitStack() as pa:
                    wna = [sbuf(pa, "wna%d" % i, [128, 8, 3, 128], BF16) for i in range(2)]
                    knT2 = [sbuf(pa, "knT%d" % i, [128, 36 * 128], BF16) for i in range(2)]
                    qnT2 = [sbuf(pa, "qnT%d" % i, [128, 4096], BF16) for i in range(2)]
                    vaug2 = [sbuf(pa, "vaug%d" % i, [128, 36, 2, 65], BF16) for i in range(2)]
                    EB = sbuf(pa, "EB", [128, 2, NSLOT * 128], BF16)
                    ebst = [sbuf(pa, "ebst%d" % i, [128, 6 * 128], F32) for i in range(2)]
                    qf = sbuf(pa, "qf", [128, 512], F32)
                    sq = sbuf(pa, "nsq", [128, 512], F32)
                    rs = sbuf(pa, "nrs", [128, 512], F32)
                    Eb = [sbuf(pa, "Eb%d" % i, [128, 6 * 128], BF16) for i in range(3)]
                    onat = [sbuf(pa, "onat%d" % i, [128, 128], BF16) for i in range(2)]
                    rcp = sbuf(pa, "rcp", [128, 8], F32)
                    cblk = [sbuf(pa, "ncblk%d" % i, [128, 512], BF16) for i in range(2)]
                    ebctr = [0]
                    A("pool", lambda e: e.memset(vaug2[0][:], 1.0), writes=[("vaug", 0)])
                    A("pool", lambda e: e.memset(vaug2[1][:], 1.0), writes=[("vaug", 1)])
                    wcf = [sbuf(pa, "wcf%d" % i, [128, D], F32) for i in range(3)]
                    wcb = [sbuf(pa, "wcb%d" % i, [128, D], BF16) for i in range(3)]
                    pieces = []
                    for k in range(8):
                        pieces.append((w_out[k * 128:(k + 1) * 128, :], wsc_o[:, k, :], D))
                    for k in range(8):
                        for (c0, cw) in ((0, 1024), (1024, 1024), (2048, 768)):
                            pieces.append((w_gate[k * 128:(k + 1) * 128, c0:c0 + cw], wsc_g[:, k, c0:c0 + cw], cw))
                            pieces.append((w_up[k * 128:(k + 1) * 128, c0:c0 + cw], wsc_u[:, k, c0:c0 + cw], cw))
                    for u in range(NU):
                        pieces.append((w_down[u * 128:(u + 1) * 128, :], wsc_d[:, u, :], D))
                    for i, (src_, dst_, cw) in enumerate(pieces):
                        f_, fk = wcf[i % 3], "wcf%d" % (i % 3)
                        b_, bk = wcb[i % 3], "wcb%d" % (i % 3)
                        A("pool", lambda e, f_=f_, src_=src_, cw=cw: e.dma_start(out=f_[:, 0:cw], in_=src_), writes=[fk], dma_key=fk)
                        A("pool", lambda e, f_=f_, b_=b_, cw=cw: e.tensor_copy(out=b_[:, 0:cw], in_=f_[:, 0:cw]), reads=[fk], writes=[bk])
                        A("pool", lambda e, b_=b_, dst_=dst_, cw=cw: e.dma_start(out=dst_, in_=b_[:, 0:cw]), reads=[bk], dma_key=bk)

                    nacol = [2560, 3072, 3584]

                    def load_na_w(hp):
                        wb = wna[hp % 2]
                        for ci, cb in enumerate(nacol):
                            load_cast(wb[:, :, ci, :], ("wna", hp % 2, ci), w_in_v[:, :, cb + hp * 128:cb + (hp + 1) * 128], wst, 1, wctr, eng="act")

                    def qknorm(bank, ncols, wcol, dst, dkey):
                        A("act", lambda e: e.activation(out=qf[:, 0:ncols], in_=ps[bank][:, 0:ncols], func=AF.Copy),
                          reads=[PS(bank)], writes=["qf"])
                        A("act", lambda e: e.activation(out=sq[:, 0:ncols], in_=ps[bank][:, 0:ncols], func=AF.Square),
                          reads=[PS(bank)], writes=["nsq"])
                        A("pe", lambda e: e.matmul(ps[7][:, 0:ncols], lhsT=blk64, rhs=sq[:, 0:ncols], start=True, stop=True),
                          reads=["nsq", "cf"], writes=[PS(7)])
                        A("act", lambda e: e.activation(out=rs[:, 0:ncols], in_=ps[7][:, 0:ncols], func=AF.Ln, bias=stats[:, 103, 0:1]),
                          reads=[PS(7), "epsc"], writes=["nrs"])
                        A("act", lambda e: e.activation(out=rs[:, 0:ncols], in_=rs[:, 0:ncols], func=AF.Exp, scale=-0.5),
                          reads=["nrs"], writes=["nrs"])
                        A("dve", lambda e: e.scalar_tensor_tensor(out=dst, in0=qf[:, 0:ncols], scalar=wqk[:, wcol:wcol + 1],
                                                                  in1=rs[:, 0:ncols], op0=ALU.mult, op1=ALU.mult),
                          reads=["qf", "nrs", "wqk"], writes=[dkey])

                    load_na_w(0)
                    for hp in range(4):
                        wb = wna[hp % 2]
                        WK = lambda ci: ("wna", hp % 2, ci)
                        knT, qnT, vaug = knT2[hp % 2], qnT2[hp % 2], vaug2[hp % 2]
                        hb = hp % 2
                        if hp + 1 < 4:
                            load_na_w(hp + 1)
                        for hh in range(2):
                            for (s0, ns) in ((0, 5), (5, 6), (11, 5), (16, 5), (21, 6)):
                                i = ebctr[0]
                                ebctr[0] += 1
                                sg, sk = ebst[i % 2], "ebst%d" % (i % 2)
                                A("sp", lambda e, sg=sg, s0=s0, ns=ns, hh=hh: e.dma_start(
                                    out=sg[:, 0:ns * 128], in_=nab[hp * 2 + hh, :, s0 * 128:(s0 + ns) * 128]),
                                  writes=[sk], dma_key=sk)
                                A("act", lambda e, sg=sg, s0=s0, ns=ns, hh=hh: e.activation(
                                    out=EB[:, hh, s0 * 128:(s0 + ns) * 128], in_=sg[:, 0:ns * 128], func=AF.Exp),
                                  reads=[sk], writes=[("EB", hh, s0)])
                        pieces = [(0, 2), (34, 2)] + [(2 + 4 * j, 4) for j in range(NB)]
                        for pi, (e0, ntl) in enumerate(pieces):
                            ncols = ntl * 128
                            cs = slice(e0 * 128, (e0 + ntl) * 128)
                            hk = [("hT", e0 + t) for t in range(ntl)]
                            kb = pi % 2
                            for k in range(8):
                                A("pe", lambda e, k=k, kb=kb, cs=cs, ncols=ncols: e.matmul(ps[kb][:, 0:ncols], lhsT=wb[:, k, 1, :],
                                                                                           rhs=hT[:, k, cs], start=(k == 0), stop=(k == 7)),
                                  reads=hk + [WK(1)], writes=[PS(kb)])
                            qknorm(kb, ncols, 1, knT[:, cs], ("knT", hb, pi))
                            vb = 2 + pi % 2
                            for t in range(ntl):
                                et = e0 + t
                                for k in range(8):
                                    A("pe", lambda e, k=k, t=t, et=et, vb=vb: e.matmul(
                                        ps[vb][:, t * 128:(t + 1) * 128], lhsT=hT[:, k, et * 128:(et + 1) * 128],
                                        rhs=wb[:, k, 2, :], start=(k == 0), stop=(k == 7)),
                                      reads=[("hT", et), WK(2)], writes=[PS(vb)])
                            A("dve", lambda e, vb=vb, e0=e0, ntl=ntl: e.tensor_copy(
                                out=vaug[:, e0:e0 + ntl, :, 0:64],
                                in_=ps[vb][:, 0:ntl * 128].rearrange("p (t h d) -> p t h d", t=ntl, h=2)),
                              reads=[PS(vb), ("vaug", hb)], writes=[("vaug", hb, pi)])
                            if e0 >= 2 and e0 < 34:
                                j = (e0 - 2) // 4
                                qb = 4 + pi % 2
                                for k in range(8):
                                    A("pe", lambda e, k=k, qb=qb, cs=cs: e.matmul(ps[qb][:], lhsT=wb[:, k, 0, :], rhs=hT[:, k, cs],
                                                                                  start=(k == 0), stop=(k == 7)),
                                      reads=hk + [WK(0)], writes=[PS(qb)])
                                qknorm(qb, 512, 0, qnT[:, j * 512:(j + 1) * 512], ("qnT", hb, j))
                        allk = [("knT", hb, pi) for pi in range(len(pieces))]
                        allv = [("vaug", hb, pi) for pi in range(len(pieces))]
                        for p in range(NT):
                            s0, keys = na_chunks(p)
                            nch = len(keys)
                            for hh in range(2):
                                it = p * 2 + hh
                                hs = slice(hh * 64, (hh + 1) * 64)
                                r3 = it % 3
                                ba, bb_ = 2 * r3, 2 * r3 + 1
                                for c, ke in enumerate(keys):
                                    if c < 4:
                                        dst, bank = ps[ba][:, c * 128:(c + 1) * 128], ba
                                    else:
                                        dst, bank = ps[bb_][:, 128 + (c - 4) * 128:128 + (c - 3) * 128], bb_
                                    A("pe", lambda e, dst=dst, ke=ke, hs=hs: e.matmul(
                                        dst, lhsT=knT[hs, ke * 128:(ke + 1) * 128],
                                        rhs=qnT[hs, p * 128:(p + 1) * 128], start=True, stop=True),
                                      reads=allk + [("qnT", hb, p // 4)], writes=[PS(bank)])
                                Eb_, ek = Eb[r3], "Eb%d" % r3
                                A("act", lambda e, Eb_=Eb_, ba=ba: e.activation(out=Eb_[:, 0:512], in_=ps[ba][:], func=AF.Exp),
                                  reads=[PS(ba)], writes=[ek])
                                A("act", lambda e, Eb_=Eb_, nch=nch, bb_=bb_: e.activation(out=Eb_[:, 512:nch * 128],
                                                                                           in_=ps[bb_][:, 128:128 + (nch - 4) * 128], func=AF.Exp),
                                  reads=[PS(bb_), ek], writes=[ek])
                                A("dve", lambda e, Eb_=Eb_, nch=nch, s0=s0, hh=hh: e.tensor_tensor(
                                    out=Eb_[:, 0:nch * 128], in0=Eb_[:, 0:nch * 128], in1=EB[:, hh, s0 * 128:(s0 + nch) * 128], op=ALU.mult),
                                  reads=[ek, ("EB", hh, s0)], writes=[ek])
                                ob = bb_
                                for c, ke in enumerate(keys):
                                    A("pe", lambda e, c=c, ke=ke, Eb_=Eb_, hh=hh, ob=ob: e.matmul(
                                        ps[ob][:, 0:65], lhsT=Eb_[:, c * 128:(c + 1) * 128], rhs=vaug[:, ke, hh, :],
                                        start=(c == 0), stop=(c == nch - 1)),
                                      reads=[ek] + allv, writes=[PS(ob)])
                                rc = it % 6
                                A("dve", lambda e, ob=ob, rc=rc: e.reciprocal(out=rcp[:, rc:rc + 1], in_=ps[ob][:, 64:65]),
                                  reads=[PS(ob)], writes=[("rcp", rc)])
                                on_, ok_ = onat[p % 2], "onat%d" % (p % 2)
                                A("dve", lambda e, ob=ob, hh=hh, on_=on_, rc=rc: e.tensor_scalar(
                                    out=on_[:, hh * 64:(hh + 1) * 64], in0=ps[ob][:, 0:64], scalar1=rcp[:, rc:rc + 1], scalar2=None, op0=ALU.mult),
                                  reads=[PS(ob), ("rcp", rc)], writes=[(ok_, hh)])
                            on_, ok_ = onat[p % 2], "onat%d" % (p % 2)
                            A("pe", lambda e, on_=on_, p=p: e.transpose(out=psb[6][:, (p % 4) * 128:(p % 4 + 1) * 128], in_=on_[:], identity=identb[:]),
                              reads=[(ok_, 0), (ok_, 1), "identb"], writes=[PS(6)])
                            if p % 4 == 3:
                                j = p // 4
                                cb_, ck = cblk[j % 2], "ncblk%d" % (j % 2)
                                A("act", lambda e, cb_=cb_: e.activation(out=cb_[:], in_=psb[6][:, 0:512], func=AF.Copy),
                                  reads=[PS(6)], writes=[ck])
                                A("sp", lambda e, cb_=cb_, j=j, hp=hp: e.dma_start(out=catT[4 + hp, :, j * 512:(j + 1) * 512], in_=cb_[:]),
                                  reads=[ck], writes=[("catT", 4 + hp, j)], dma_key=ck)
                S.barrier()
        S.barrier()
        if "catT" in dbg_out:
            A("sp", lambda e: e.dma_start(out=dbg_out["catT"], in_=catT), dma_key="dbg_catT")

        with ExitStack() as p3:
            wo_b = sbuf(p3, "wo_b", [128, 8, D], BF16)
            wg_b = sbuf(p3, "wg_b", [128, 8, DFF], BF16)
            wu_b = sbuf(p3, "wu_b", [128, 8, DFF], BF16)
            wdr = [sbuf(p3, "wdr%d" % i, [128, D], BF16) for i in range(6)]
            cat = [sbuf(p3, "cat%d" % i, [128, 8, 512], BF16) for i in range(2)]
            x1 = [sbuf(p3, "x1_%d" % i, [128, D], F32) for i in range(6)]
            xb3 = [sbuf(p3, "xb%d" % i, [128, D], BF16) for i in range(2)]
            h2T = sbuf(p3, "h2T", [128, 8, 512], BF16)
            actT = sbuf(p3, "actT", [128, NU, 512], BF16)
            sg3 = [sbuf(p3, "sg3_%d" % i, [128, 512], BF16) for i in range(2)]
            for kq in range(4):
                A("sp", lambda e, kq=kq: e.dma_start(out=wo_b[:, 2 * kq:2 * kq + 2, :], in_=wsc_o[:, 2 * kq:2 * kq + 2, :]),
                  writes=[("wo_b", kq)], dma_key=("wo_b", kq))
            for kq in range(4):
                A("sp", lambda e, kq=kq: e.dma_start(out=wg_b[:, 2 * kq:2 * kq + 2, :], in_=wsc_g[:, 2 * kq:2 * kq + 2, :]),
                  writes=[("wg_b", kq)], dma_key=("wg_b", kq))
                A("sp", lambda e, kq=kq: e.dma_start(out=wu_b[:, 2 * kq:2 * kq + 2, :], in_=wsc_u[:, 2 * kq:2 * kq + 2, :]),
                  writes=[("wu_b", kq)], dma_key=("wu_b", kq))
            wdc = [0]
            x1c = [0]
            for j in range(NB):
                cj, cjk = cat[j % 2], "cat%d" % (j % 2)
                if do_hgrn or do_na:
                    for k in range(8):
                        if (k < 4 and do_hgrn) or (k >= 4 and do_na):
                            A("sp", lambda e, k=k, cj=cj, j=j: e.dma_start(out=cj[:, k, :], in_=catT[k, :, j * 512:(j + 1) * 512]),
                              reads=[("catT", k, j)], writes=[(cjk, k)], dma_key=(cjk, k))
                xts = []
                for t in range(4):
                    tt = j * 4 + t
                    i = x1c[0]
                    x1c[0] += 1
                    xs, xk = x1[i % 6], "x1_%d" % (i % 6)
                    xts.append((xs, xk))
                    A("sp", lambda e, xs=xs, tt=tt: e.dma_start(out=xs[:], in_=xloc[tt * 128:(tt + 1) * 128, :]), writes=[xk], dma_key=xk)
                    if do_hgrn or do_na:
                        ks = [k for k in range(8) if (k < 4 and do_hgrn) or (k >= 4 and do_na)]
                        for half in range(2):
                            for ki, k in enumerate(ks):
                                A("pe", lambda e, k=k, half=half, t=t, cj=cj, ki=ki, nk=len(ks): e.matmul(
                                    ps[half][:], lhsT=cj[:, k, t * 128:(t + 1) * 128], rhs=wo_b[:, k, half * 512:(half + 1) * 512],
                                    start=(ki == 0), stop=(ki == nk - 1)),
                                  reads=[(cjk, k), ("wo_b", k // 2)],
                                  writes=[PS(half)])
                            A("dve", lambda e, xs=xs, half=half: e.tensor_tensor(out=xs[:, half * 512:(half + 1) * 512],
                                                                                 in0=xs[:, half * 512:(half + 1) * 512], in1=ps[half][:], op=ALU.add),
                              reads=[PS(half), xk], writes=[xk])
                    norm_tile(None, h2T[:, :, t * 128:(t + 1) * 128], ("h2T", t), V_NFW, None, xb3, 2, src_key=xk, x_keep=xs)
                hkeys3 = [("h2T", t) for t in range(4)]
                for u in range(NU):
                    gb, ub = 3 + 2 * (u % 2), 4 + 2 * (u % 2)
                    for k in range(8):
                        A("pe", lambda e, k=k, u=u, gb=gb: e.matmul(ps[gb][:], lhsT=wg_b[:, k, u * 128:(u + 1) * 128], rhs=h2T[:, k, :],
                                                                    start=(k == 0), stop=(k == 7)),
                          reads=hkeys3 + [("wg_b", k // 2)], writes=[PS(gb)])
                    for k in range(8):
                        A("pe", lambda e, k=k, u=u, ub=ub: e.matmul(ps[ub][:], lhsT=wu_b[:, k, u * 128:(u + 1) * 128], rhs=h2T[:, k, :],
                                                                    start=(k == 0), stop=(k == 7)),
                          reads=hkeys3 + [("wu_b", k // 2)], writes=[PS(ub)])
                    sg_, sgk = sg3[u % 2], "sg3_%d" % (u % 2)
                    A("act", lambda e, sg_=sg_, gb=gb: e.activation(out=sg_[:], in_=ps[gb][:], func=AF.Silu), reads=[PS(gb)], writes=[sgk])
                    A("dve", lambda e, sg_=sg_, ub=ub, u=u: e.tensor_tensor(out=actT[:, u, :], in0=sg_[:], in1=ps[ub][:], op=ALU.mult),
                      reads=[PS(ub), sgk], writes=[("actT", u)])
                wds = []
                for u in range(NU):
                    i = wdc[0]
                    wdc[0] += 1
                    wd_, wdk = wdr[i % 6], "wdr%d" % (i % 6)
                    A("sp", lambda e, wd_=wd_, u=u: e.dma_start(out=wd_[:], in_=wsc_d[:, u, :]), writes=[wdk], dma_key=wdk)
                    for t in range(4):
                        for half in range(2):
                            A("pe", lambda e, t=t, half=half, u=u, wd_=wd_: e.matmul(
                                dwn_view(ps, t, half), lhsT=actT[:, u, t * 128:(t + 1) * 128], rhs=wd_[:, half * 512:(half + 1) * 512],
                                start=(u == 0), stop=(u == NU - 1)),
                              reads=[("actT", u), wdk], writes=[("ps", t * 2 + half)])
                for t in range(4):
                    xs, xk = xts[t]
                    tt = j * 4 + t
                    for half in range(2):
                        A("dve", lambda e, xs=xs, half=half, t=t: e.tensor_tensor(out=xs[:, half * 512:(half + 1) * 512],
                                                                                  in0=xs[:, half * 512:(half + 1) * 512],
                                                                                  in1=dwn_view(ps, t, half), op=ALU.add),
                          reads=[("ps", t * 2 + half), xk], writes=[xk])
                    A("pool", lambda e, xs=xs, tt=tt: e.dma_start(out=y[tt * 128:(tt + 1) * 128, :], in_=xs[:]), reads=[xk], dma_key=("yst", xk))
        S.emit_all(st)
    return nc


def dwn_view(ps, t, half):
    return ps[t * 2 + half][:]


def _consts():
    c = np.zeros((128, 5 * 128 + 512), np.float32)
    c[:, 0:128] = np.eye(128, dtype=np.float32)
    s_ = np.arange(128)[:, None]
    t_ = np.arange(128)[None, :]
    c[:, 128:256] = (s_ <= t_).astype(np.float32)
    c[:, 256:384] = (s_ >= t_).astype(np.float32)
    c[:, 384:512] = 1.0 / 128.0
    blk = np.zeros((128, 128), np.float32)
    blk[:64, :64] = 1.0 / 64.0
    blk[64:, 64:] = 1.0 / 64.0
    c[:, 512:640] = blk
    rm = np.ones(512, np.float32)
    rm[0::128] = 0.0
    c[:, 640:1152] = rm[None, :]
    return c


MASKV = -30000.0


def _na_table(rpb, row_base, rows):
    out = np.full((8, 128, NSLOT, 128), MASKV, np.float32)
    kr = np.arange(128) // 64
    kc = np.arange(128) % 64
    qr = np.arange(128) // 64
    qc = np.arange(128) % 64
    cs = np.clip(qc - 8, 0, 64 - 16)

    def fill(p, slot_base, keys):
        R = row_base + 2 * p + qr
        rs = np.clip(R - 4, 0, rows - 8)
        for c, ke in enumerate(keys):
            kt = ke - 2
            KR = row_base + 2 * kt + kr
            valid = ((KR[:, None] >= rs[None, :]) & (KR[:, None] < rs[None, :] + 8) &
                     (kc[:, None] >= cs[None, :]) & (kc[:, None] < cs[None, :] + 16))
            dr = np.clip(KR[:, None] - R[None, :] + 7, 0, 14)
            dc = np.clip(kc[:, None] - qc[None, :] + 15, 0, 30)
            vals = rpb[:, dr, dc]
            out[:, :, slot_base + c, :] = np.where(valid[None], vals, np.float32(MASKV))

    s0, keys = na_chunks(5)
    fill(5, s0, keys)
    for p in (0, 1, 30, 31):
        s0, keys = na_chunks(p)
        fill(p, s0, keys)
    return out.reshape(8, 128, NSLOT * 128)


def make_core_inputs(c, inp):
    f32 = np.float32
    w_in = np.asarray(inp["w_in"][0], f32)
    lbraw = np.asarray(inp["hgrn_lb"], f32)
    if c < 4:
        seq = np.asarray(inp["x_prompt"][c], f32)
        xloc = seq
        xhalo = np.zeros((512, D), f32)
        xoth = np.zeros((4096, D), f32)
        osel, alpha, beta = 0, 0.0, 0.0
        row_base, rows = 0, 64
    else:
        sidx, half = (c - 4) // 2, (c - 4) % 2
        seq = np.asarray(inp["x_sample"][sidx], f32)
        xhalo = np.zeros((512, D), f32)
        rows = 128
        if half == 0:
            xloc = seq[:4096]
            xhalo[256:512] = seq[4096:4352]
            xoth = seq[4096:][::-1]
            osel, alpha, beta = 1, 0.0, 1.0
            row_base = 0
        else:
            xloc = seq[4096:]
            xhalo[0:256] = seq[3840:4096]
            xoth = seq[:4096]
            osel, alpha, beta = 0, 1.0, 0.0
            row_base = 64
    fcol = 512 + osel * 512
    w_oth = np.concatenate([w_in[:, fcol:fcol + 512], w_in[:, 1536:2048]], axis=1)
    vecs = np.zeros((128, NV), f32)
    vecs[:, V_NMW:V_NMW + 8] = np.asarray(inp["norm_mix_w"][0], f32).reshape(8, 128).T
    vecs[:, V_NFW:V_NFW + 8] = np.asarray(inp["norm_ffn_w"][0], f32).reshape(8, 128).T
    lb4 = lbraw.reshape(2, 2, 4, 128)
    for sl in range(2):
        for d in range(2):
            for h in range(4):
                vecs[:, V_LB + sl * 8 + d * 4 + h] = lb4[sl, d, h]
        for h in range(4):
            vecs[:, V_LBO + sl * 4 + h] = lb4[sl, osel, h]
    vecs[:, V_GW] = np.asarray(inp["hgrn_gnorm_w"][0], f32)
    vecs[:, V_WQ] = np.tile(np.asarray(inp["na_q_norm_w"][0], f32), 2)
    vecs[:, V_WK] = np.tile(np.asarray(inp["na_k_norm_w"][0], f32), 2)
    vecs[:, V_AL] = alpha
    vecs[:, V_BE] = beta
    return {
        "xloc": np.ascontiguousarray(xloc), "xhalo": xhalo, "xoth": np.ascontiguousarray(xoth),
        "w_in": w_in, "w_oth": np.ascontiguousarray(w_oth),
        "w_out": np.asarray(inp["w_out"][0], f32), "w_gate": np.asarray(inp["w_gate"][0], f32),
        "w_up": np.asarray(inp["w_up"][0], f32), "w_down": np.asarray(inp["w_down"][0], f32),
        "vecs": vecs, "nab": _na_table(np.asarray(inp["na_rpb"][0], f32), row_base, rows),
        "consts": _consts(),
    }


_NC_CACHE = {}


def kernel(**inputs):
    key = "full"
    if key not in _NC_CACHE:
        _NC_CACHE[key] = build()
    nc = _NC_CACHE[key]
    in_maps = [make_core_inputs(c, inputs) for c in range(8)]
    res = run_bass_kernel_spmd(nc, in_maps, core_ids=list(range(8)))
    ys = [np.asarray(r["y"], np.float32) for r in res.results]
    y_prompt = np.stack(ys[0:4], axis=0)
    y_sample = np.stack([np.concatenate([ys[4], ys[5]], axis=0), np.concatenate([ys[6], ys[7]], axis=0)], axis=0)
    return (y_prompt, y_sample)
```
